# Optimizing a Trainium2 kernel written in Bass

```python
import functools
import jax, jax.numpy as jnp
from jax import lax
import numpy as np

D_MODEL = 1024
BATCH = 2
SEQ = 8192
DEPTH = 1
DEC_BATCH = 32
DEC_SEQ = 4
PAST_LEN = 8192
PAGE_SIZE = 128

N_HEADS = 8
HEAD_DIM = 64
ATT_WIDTH = N_HEADS * HEAD_DIM
CONV_WIDTH = 512
CONV_K = 31
FFN_HIDDEN = -(-8 * D_MODEL // (3 * 256)) * 256
Q_BLOCK = 128
FORGET_BIAS = 7.0
EPS = 1e-6
NEG_INF = -1e30

Q_OFF = 0
K_OFF = Q_OFF + ATT_WIDTH
V_OFF = K_OFF + ATT_WIDTH
F_OFF = V_OFF + ATT_WIDTH
GLU_OFF = F_OFF + N_HEADS
GA_OFF = GLU_OFF + 2 * CONV_WIDTH
GB_OFF = GA_OFF + D_MODEL
N_IN = GB_OFF + D_MODEL

kernel_name = 'fox_conformer_gated_hybrid_step'


def _rmsnorm(x, g):
    xf = x.astype(jnp.float32)
    y = xf * lax.rsqrt(jnp.mean(xf * xf, axis=-1, keepdims=True) + EPS)
    return (y * g.astype(jnp.float32)).astype(x.dtype)


def _layernorm(x, g, b):
    xf = x.astype(jnp.float32)
    mu = jnp.mean(xf, axis=-1, keepdims=True)
    var = jnp.mean(jnp.square(xf - mu), axis=-1, keepdims=True)
    y = (xf - mu) * lax.rsqrt(var + EPS) * g.astype(jnp.float32) + b.astype(jnp.float32)
    return y.astype(x.dtype)


def _fox_prompt(q, k, v, logf):
    B, S = q.shape[0], q.shape[1]
    nb = S // Q_BLOCK
    scale = HEAD_DIM ** -0.5
    kf = k.astype(jnp.float32)
    vf = v.astype(jnp.float32)
    C = lax.cumsum(logf.astype(jnp.float32), axis=1)
    Ck = jnp.transpose(C, (0, 2, 1))
    qb = jnp.swapaxes(q.reshape(B, nb, Q_BLOCK, N_HEADS, HEAD_DIM), 0, 1)
    Cb = jnp.swapaxes(C.reshape(B, nb, Q_BLOCK, N_HEADS), 0, 1)
    kpos = jnp.arange(S)

    def one_block(args):
        qi, Ci, i = args
        s = jnp.einsum('bqhd,bkhd->bhqk', qi.astype(jnp.float32), kf) * scale
        s = s + jnp.transpose(Ci, (0, 2, 1))[..., None] - Ck[:, :, None, :]
        qpos = i * Q_BLOCK + jnp.arange(Q_BLOCK)
        s = jnp.where(kpos[None, :] <= qpos[:, None], s, NEG_INF)
        p = jax.nn.softmax(s, axis=-1)
        return jnp.einsum('bhqk,bkhd->bqhd', p, vf)

    o = lax.map(one_block, (qb, Cb, jnp.arange(nb)))
    return jnp.swapaxes(o, 0, 1).reshape(B, S, N_HEADS, HEAD_DIM).astype(q.dtype)


def _fox_sample(q, k, v, logf, k_past, v_past, logf_past):
    P = k_past.shape[1]
    T = q.shape[1]
    scale = HEAD_DIM ** -0.5
    qf = q.astype(jnp.float32)
    Cn = jnp.transpose(lax.cumsum(logf.astype(jnp.float32), axis=1), (0, 2, 1))
    R = lax.cumsum(logf_past.astype(jnp.float32), axis=1, reverse=True)
    R = jnp.concatenate([R[:, 1:], jnp.zeros_like(R[:, :1])], axis=1)
    R = jnp.transpose(R, (0, 2, 1))
    s_past = jnp.einsum('bthd,bshd->bhts', qf, k_past.astype(jnp.float32)) * scale
    s_past = s_past + Cn[..., None] + R[:, :, None, :]
    s_new = jnp.einsum('bthd,bshd->bhts', qf, k.astype(jnp.float32)) * scale
    s_new = s_new + Cn[..., None] - Cn[:, :, None, :]
    s_new = jnp.where(jnp.tril(jnp.ones((T, T), dtype=bool)), s_new, NEG_INF)
    p = jax.nn.softmax(jnp.concatenate([s_past, s_new], axis=-1), axis=-1)
    o = (jnp.einsum('bhts,bshd->bthd', p[..., :P], v_past.astype(jnp.float32))
         + jnp.einsum('bhts,bshd->bthd', p[..., P:], v.astype(jnp.float32)))
    return o.astype(q.dtype)


def _depthwise_causal_conv(u_pad, w, b):
    y = lax.conv_general_dilated(u_pad, w[:, None, :].astype(u_pad.dtype), window_strides=(1,),
                                 padding='VALID', dimension_numbers=('NWC', 'WIO', 'NWC'),
                                 feature_group_count=CONV_WIDTH)
    return y + b.astype(u_pad.dtype)


def _layer(x, c, lw, attend, conv_left):
    (rms1_g, rms2_g, w_ada, b_ada, w_in, b_in, dw_w, dw_b, ln_g, ln_b,
     w_pa, w_pb, b_pb, w_o, w_ffn_in, w_ffn_out) = lw
    B, L = x.shape[0], x.shape[1]
    mod = jax.nn.silu(c) @ w_ada + b_ada
    sh1, sc1, g1, sh2, sc2, g2 = jnp.split(mod, 6, axis=-1)
    h = _rmsnorm(x, rms1_g) * (1 + sc1[:, None, :]) + sh1[:, None, :]
    z = h @ w_in + b_in
    q = z[..., Q_OFF:K_OFF].reshape(B, L, N_HEADS, HEAD_DIM)
    k = z[..., K_OFF:V_OFF].reshape(B, L, N_HEADS, HEAD_DIM)
    v = z[..., V_OFF:F_OFF].reshape(B, L, N_HEADS, HEAD_DIM)
    logf = jax.nn.log_sigmoid(z[..., F_OFF:GLU_OFF].astype(jnp.float32))
    glu_a, glu_b = jnp.split(z[..., GLU_OFF:GA_OFF], 2, axis=-1)
    gate_a = jax.nn.sigmoid(z[..., GA_OFF:GB_OFF])
    gate_b = jax.nn.sigmoid(z[..., GB_OFF:N_IN])
    o = attend(q, k, v, logf)
    y_a = o.reshape(B, L, ATT_WIDTH) @ w_pa
    u = glu_a * jax.nn.sigmoid(glu_b)
    u_pad = jnp.concatenate([conv_left.astype(u.dtype), u], axis=1)
    y_b = _depthwise_causal_conv(u_pad, dw_w, dw_b)
    y_b = jax.nn.silu(_layernorm(y_b, ln_g, ln_b)) @ w_pb + b_pb
    m = gate_a * y_a + gate_b * y_b
    x = x + g1[:, None, :] * (m @ w_o)
    h2 = _rmsnorm(x, rms2_g) * (1 + sc2[:, None, :]) + sh2[:, None, :]
    f_gate, f_up = jnp.split(h2 @ w_ffn_in, 2, axis=-1)
    x = x + g2[:, None, :] * ((jax.nn.silu(f_gate) * f_up) @ w_ffn_out)
    return x, k, v, logf, u_pad[:, -(CONV_K - 1):]


def setup_inputs(seed: int = 0) -> dict:
    key = jax.random.key(seed)
    ks = jax.random.split(key, 32)
    f32 = jnp.float32
    n_pages = PAST_LEN // PAGE_SIZE
    n_used = DEC_BATCH * n_pages
    n_phys = n_used + (n_used + 3) // 4

    def nrm(k, shape, fan_in, gain=1.0):
        return jax.random.normal(k, shape, f32) * (gain * fan_in ** -0.5)

    def small(k, shape):
        return 0.02 * jax.random.normal(k, shape, f32)

    def gain(k, shape):
        return 1.0 + 0.1 * jax.random.normal(k, shape, f32)

    page_table = jax.random.permutation(ks[0], n_phys)[:n_used].reshape(DEC_BATCH, n_pages).astype(jnp.int32)
    b_in = small(ks[1], (DEPTH, N_IN)).at[:, F_OFF:GLU_OFF].add(FORGET_BIAS)
    return {
        'x_prompt': jax.random.normal(ks[2], (BATCH, SEQ, D_MODEL), f32),
        'x_sample': jax.random.normal(ks[3], (DEC_BATCH, DEC_SEQ, D_MODEL), f32),
        'c_prompt': jax.random.normal(ks[4], (BATCH, D_MODEL), f32),
        'c_sample': jax.random.normal(ks[5], (DEC_BATCH, D_MODEL), f32),
        'cache_k': jax.random.normal(ks[6], (DEPTH, n_phys, PAGE_SIZE, N_HEADS, HEAD_DIM), f32),
        'cache_v': jax.random.normal(ks[7], (DEPTH, n_phys, PAGE_SIZE, N_HEADS, HEAD_DIM), f32),
        'cache_logf': jax.nn.log_sigmoid(FORGET_BIAS + 0.5 * jax.random.normal(ks[8], (DEPTH, n_phys, PAGE_SIZE, N_HEADS), f32)),
        'state_conv': 0.5 * jax.random.normal(ks[9], (DEPTH, DEC_BATCH, CONV_K - 1, CONV_WIDTH), f32),
        'page_table': page_table,
        'rms1_g': gain(ks[10], (DEPTH, D_MODEL)),
        'rms2_g': gain(ks[11], (DEPTH, D_MODEL)),
        'w_ada': nrm(ks[12], (DEPTH, D_MODEL, 6 * D_MODEL), D_MODEL, 0.5),
        'b_ada': small(ks[13], (DEPTH, 6 * D_MODEL)),
        'w_in': nrm(ks[14], (DEPTH, D_MODEL, N_IN), D_MODEL),
        'b_in': b_in,
        'dw_w': nrm(ks[15], (DEPTH, CONV_K, CONV_WIDTH), CONV_K),
        'dw_b': small(ks[16], (DEPTH, CONV_WIDTH)),
        'ln_g': gain(ks[17], (DEPTH, CONV_WIDTH)),
        'ln_b': small(ks[18], (DEPTH, CONV_WIDTH)),
        'w_pa': nrm(ks[19], (DEPTH, ATT_WIDTH, D_MODEL), ATT_WIDTH),
        'w_pb': nrm(ks[20], (DEPTH, CONV_WIDTH, D_MODEL), CONV_WIDTH),
        'b_pb': small(ks[21], (DEPTH, D_MODEL)),
        'w_o': nrm(ks[22], (DEPTH, D_MODEL, D_MODEL), D_MODEL),
        'w_ffn_in': nrm(ks[23], (DEPTH, D_MODEL, 2 * FFN_HIDDEN), D_MODEL),
        'w_ffn_out': nrm(ks[24], (DEPTH, FFN_HIDDEN, D_MODEL), FFN_HIDDEN),
        'final_g': gain(ks[25], (D_MODEL,)),
    }


def reference(x_prompt, x_sample, c_prompt, c_sample, cache_k, cache_v, cache_logf, state_conv,
              page_table, rms1_g, rms2_g, w_ada, b_ada, w_in, b_in, dw_w, dw_b, ln_g, ln_b,
              w_pa, w_pb, b_pb, w_o, w_ffn_in, w_ffn_out, final_g):
    B, S = x_prompt.shape[0], x_prompt.shape[1]
    DB = x_sample.shape[0]
    n_pages = page_table.shape[1]
    P = n_pages * PAGE_SIZE
    xp, xs = x_prompt, x_sample
    kp_l, vp_l, lp_l, cp_l, ks_l, vs_l, ls_l, cs_l = [], [], [], [], [], [], [], []
    for l in range(DEPTH):
        lw = (rms1_g[l], rms2_g[l], w_ada[l], b_ada[l], w_in[l], b_in[l], dw_w[l], dw_b[l],
              ln_g[l], ln_b[l], w_pa[l], w_pb[l], b_pb[l], w_o[l], w_ffn_in[l], w_ffn_out[l])
        zero_left = jnp.zeros((B, CONV_K - 1, CONV_WIDTH), xp.dtype)
        xp, k_p, v_p, lf_p, c_p = _layer(xp, c_prompt, lw, _fox_prompt, zero_left)
        kp_l.append(k_p.reshape(B, S // PAGE_SIZE, PAGE_SIZE, N_HEADS, HEAD_DIM))
        vp_l.append(v_p.reshape(B, S // PAGE_SIZE, PAGE_SIZE, N_HEADS, HEAD_DIM))
        lp_l.append(lf_p.reshape(B, S // PAGE_SIZE, PAGE_SIZE, N_HEADS))
        cp_l.append(c_p)
        k_past = cache_k[l][page_table].reshape(DB, P, N_HEADS, HEAD_DIM)
        v_past = cache_v[l][page_table].reshape(DB, P, N_HEADS, HEAD_DIM)
        lf_past = cache_logf[l][page_table].reshape(DB, P, N_HEADS)
        attend_s = functools.partial(_fox_sample, k_past=k_past, v_past=v_past, logf_past=lf_past)
        xs, k_s, v_s, lf_s, c_s = _layer(xs, c_sample, lw, attend_s, state_conv[l])
        ks_l.append(k_s)
        vs_l.append(v_s)
        ls_l.append(lf_s)
        cs_l.append(c_s)
    y_prompt = _rmsnorm(xp, final_g)
    y_sample = _rmsnorm(xs, final_g)
    return (y_prompt, y_sample,
            jnp.stack(kp_l), jnp.stack(vp_l), jnp.stack(lp_l), jnp.stack(cp_l),
            jnp.stack(ks_l), jnp.stack(vs_l), jnp.stack(ls_l), jnp.stack(cs_l))
```

```python
from contextlib import ExitStack

import numpy as np
import concourse.bass as bass
import concourse.mybir as mybir
from concourse.bass_utils import run_bass_kernel_spmd

F32 = mybir.dt.float32
BF16 = mybir.dt.bfloat16
I32 = mybir.dt.int32
AF = mybir.ActivationFunctionType
ALU = mybir.AluOpType

D = 1024
SEQ = 8192
NB = SEQ // 128
H = 8
DH = 64
NIN = 4616
Q_OFF, K_OFF, V_OFF, F_OFF, GLU_OFF, GA_OFF, GB_OFF = 0, 512, 1024, 1536, 1544, 2568, 3592
CW = 512
CK = 31
FH = 2816
EPS = 1e-6
SCALE = DH ** -0.5
BIG = 30000.0
NPHYS = 2560
NOWN = 16
ENG = ("pe", "act", "dve", "pool", "sp")

STAGE = 4
DBG_OT = False
SKIP_P5 = False
SKIP_ATT = False
SDBG = 0


class _Stop(Exception):
    pass


class Buf:
    __slots__ = ("w", "r", "sem", "semv", "name")

    def __init__(self, name=""):
        self.w = None
        self.r = []
        self.sem = None
        self.semv = 0
        self.name = name


class _Rec:
    def __init__(self):
        self.call = None

    def __getattr__(self, name):
        def f(*a, **k):
            self.call = (name, a, k)
            return self
        return f


def _record(fn):
    r = _Rec()
    fn(r)
    assert r.call is not None
    return r.call


class Prog:
    def __init__(self, nc, es):
        self.nc = nc
        self.es = es
        self.items = {e: [] for e in ENG}
        self.cnt = {e: 0 for e in ENG}
        self.sems = {e: es.enter_context(nc.semaphore("c_" + e)) for e in ENG}
        self.known = {e: {} for e in ENG}
        self.pending = {e: [] for e in ENG}
        self.dma_bufs = []
        self.nsem = 0

    def _deps(self, R, W):
        deps = []
        for b in R:
            if b.w is not None:
                deps.append(b.w)
        for b in W:
            if b.w is not None:
                deps.append(b.w)
            deps.extend(b.r)
        return deps

    def _waits(self, eng, deps):
        deps = list(deps) + self.pending[eng]
        self.pending[eng] = []
        best = {}
        for (sem, val, src) in deps:
            if eng == "pe" and src == "pe":
                continue
            k = id(sem)
            if self.known[eng].get(k, 0) >= val:
                continue
            if k not in best or best[k][1] < val:
                best[k] = (sem, val)
        for k, (sem, val) in best.items():
            self.known[eng][k] = val
        return list(best.values())

    def op(self, eng, fn, R=(), W=()):
        waits = self._waits(eng, self._deps(R, W))
        self.cnt[eng] += 1
        ev = (self.sems[eng], self.cnt[eng], eng)
        for b in R:
            b.r.append(ev)
        for b in W:
            b.w = ev
            b.r = []
        self.items[eng].append((waits, _record(fn), self.sems[eng], 1))

    def dma(self, q, fn, R=(), W=(), sb=None):
        waits = self._waits(q, self._deps(R, W))
        if sb is None:
            sb = W[0] if W else R[0]
        if sb.sem is None:
            sb.sem = self.es.enter_context(self.nc.semaphore("d%d" % self.nsem))
            self.nsem += 1
            self.dma_bufs.append(sb)
        sb.semv += 16
        ev = (sb.sem, sb.semv, "dma")
        for b in R:
            b.r.append(ev)
        for b in W:
            b.w = ev
            b.r = []
        self.items[q].append((waits, _record(fn), sb.sem, 16))

    def barrier(self):
        evs = [(self.sems[e], self.cnt[e], e) for e in ENG if self.cnt[e] > 0]
        evs += [(b.sem, b.semv, "dma") for b in self.dma_bufs]
        for e in ENG:
            self.pending[e] = list(evs)

    def finish(self):
        self.barrier()
        nc = self.nc
        with nc.Block() as block:
            def run(eng, items, final):
                for waits, (name, a, k), sem, inc in items:
                    for (s, v) in waits:
                        eng.wait_ge(s, v)
                    getattr(eng, name)(*a, **k).then_inc(sem, inc)
                for (s, v, _) in final:
                    eng.wait_ge(s, v)

            fin = self.pending["sp"]
            block.tensor(lambda e: run(e, self.items["pe"], []))
            block.scalar(lambda e: run(e, self.items["act"], []))
            block.vector(lambda e: run(e, self.items["dve"], []))
            block.gpsimd(lambda e: run(e, self.items["pool"], []))
            block.sync(lambda e: run(e, self.items["sp"], fin))


def own_blocks(j):
    return [16 * m + 4 * i + j for m in range(4) for i in range(4)]


FFN_GROUPS = []
_h = 0
for _n in (3, 3, 3, 3, 3, 3, 2, 2):
    FFN_GROUPS.append((_h, _n))
    _h += _n
QBASE = [0, 16, 48, 96]


def build_program():
    nc = bass.Bass("TRN2", target_bir_lowering=False)

    def din(name, shape, dt=F32):
        return nc.dram_tensor(name, list(shape), dt, kind="ExternalInput").ap()

    def dout(name, shape, dt=F32):
        return nc.dram_tensor(name, list(shape), dt, kind="ExternalOutput").ap()

    xseq = din("xseq", [SEQ, D])
    xown = din("xown", [NOWN, 160, D])
    xs_d = din("xs", [128, D])
    cT_d = din("cT", [128, 8, 5])
    w_ada = din("w_ada", [D, 6 * D])
    b_adaT = din("b_adaT", [128, 48])
    w_in = din("w_in", [D, NIN])
    b_fm = din("b_fm", [128, 32])
    b_rows = din("b_rows", [1, 1032])
    gains = din("gains", [128, 24])
    convp_d = din("convpar", [128, 4 * CK + 20])
    w_pa = din("w_pa", [512, D])
    w_pb = din("w_pb", [512, D])
    w_o = din("w_o", [D, D])
    w_f1 = din("w_f1", [D, 2 * FH])
    w_f2 = din("w_f2", [FH, D])
    NCR = NPHYS * 16 if STAGE >= 4 else 16
    cache_k = din("cache_k", [NCR, 8 * 512])
    cache_v = din("cache_v", [NCR, 8 * 512])
    cache_l = din("cache_l", [NCR, 64])
    pt_rep = din("pt_rep", [128, 32], I32)
    shi_d = din("shi", [128, 1])
    stT_d = din("stT", [128, 4, 4, 30])
    st_rows = din("st_rows", [4, 26, CW])
    kboff_d = din("kboff", [128, 4])
    hval_d = din("hval", [128, NOWN])
    bmask_d = din("bmask", [32, 512])
    cst_d = din("cst", [128, 5, 128])

    y_own = dout("y_own", [NOWN, 128, D])
    y_s = dout("y_s", [128, D])
    k_own = dout("k_own", [NOWN, 128, 512])
    v_own = dout("v_own", [NOWN, 128, 512])
    lf_own = dout("lf_own", [NOWN, 128, 8])
    convp_o = dout("convp", [32, CW])
    ks_o = dout("ks", [128, 512])
    vs_o = dout("vs", [128, 512])
    lfs_o = dout("lfs", [128, 8])
    convs_o = dout("convs", [4, 30, CW])
    dbg_o = dout("dbg", [64, 8, 2048 + 128], BF16) if (DBG_OT or STAGE < 3) else None
    dbg2 = dout("dbg2", [128, 2048], F32) if DBG_OT else None

    es = ExitStack()
    with es:
        P = Prog(nc, es)

        _uid = [0]

        def sb(name, shape, dt, stack=es):
            _uid[0] += 1
            return stack.enter_context(nc.sbuf_tensor("%s_s%d" % (name, _uid[0]), list(shape), dt))

        psum = es.enter_context(nc.psum_tensor("psum", [128, 8 * 512], F32))
        psum_bf = psum[:, :].bitcast(BF16)

        def bank(i, n=512, off=0):
            return psum[:, i * 512 + off: i * 512 + off + n]

        cst = sb("cst", [128, 5, 128], F32)
        identb = sb("identb", [128, 128], BF16)
        modT = sb("modT", [128, 48, 5], F32)
        a1T = sb("a1T", [128, 8, 5], F32)
        a2T = sb("a2T", [128, 8, 5], F32)
        bfm = sb("bfm", [128, 32], F32)
        gn = sb("gn", [128, 24], F32)
        cvp = sb("cvp", [128, 4 * CK + 20], F32)
        kboff = sb("kboff_t", [128, 4], F32)
        hval = sb("hval_t", [128, NOWN], F32)
        maskT = sb("maskT", [128, 4, 128], BF16)
        oT = sb("oT", [64, 8, 2048 + 128], BF16)
        KTs = sb("KTs", [128, 4, 128], BF16)
        QTs = sb("QTs", [128, 4, 128], BF16)
        Vs_bf = sb("Vs_bf", [128, 512], BF16)
        lfs_t = sb("lfs_t", [128, 8], F32)
        B_cst = Buf("cst")
        B_mod = Buf("mod")
        B_small = Buf("small")
        B_mask = Buf("mask")
        B_oT = [Buf("oT%d" % i) for i in range(5)]
        B_KTs = Buf("KTs")
        B_QTs = Buf("QTs")
        B_Vs = Buf("Vs")
        B_lfs = Buf("lfs")

        ident = cst[:, 0, :]
        ones = cst[:, 1, :]
        U_incl = cst[:, 2, :]
        U_strict = cst[:, 3, :]
        tmins = cst[:, 4, :]

        P.dma("sp", lambda e: e.dma_start(out=cst[:], in_=cst_d), W=[B_cst])
        for (t, d_) in ((bfm, b_fm), (gn, gains), (cvp, convp_d), (kboff, kboff_d), (hval, hval_d)):
            P.dma("sp", lambda e, t=t, d_=d_: e.dma_start(out=t[:], in_=d_), W=[B_small])
        P.op("dve", lambda e: e.tensor_copy(out=identb[:], in_=ident), R=[B_cst], W=[B_small])
        for r in range(4):
            P.op("dve", lambda e, r=r: e.tensor_scalar(out=maskT[:, r, :], in0=tmins, scalar1=kboff[:, r:r + 1],
                                                       scalar2=0.0, op0=ALU.subtract, op1=ALU.is_ge),
                 R=[B_cst, B_small], W=[B_mask])
        P.op("dve", lambda e: e.tensor_scalar(out=maskT[:], in0=maskT[:], scalar1=-1.0, scalar2=BIG,
                                              op0=ALU.add, op1=ALU.mult), R=[B_mask], W=[B_mask])

        with ExitStack() as s0:
            cTf = sb("cTf", [128, 8, 5], F32, s0)
            cTb = sb("cTb", [128, 8, 5], BF16, s0)
            badaT = sb("badaT", [128, 48], F32, s0)
            wada = [sb("wada%d" % i, [128, 8, 1024], BF16, s0) for i in range(2)]
            B_c = Buf()
            B_wada = [Buf(), Buf()]
            B_ps0 = Buf()
            P.dma("sp", lambda e: e.dma_start(out=cTf[:], in_=cT_d), W=[B_c])
            P.dma("sp", lambda e: e.dma_start(out=badaT[:], in_=b_adaT), W=[B_c])
            P.op("act", lambda e: e.activation(out=cTb[:], in_=cTf[:], func=AF.Silu), R=[B_c], W=[B_c])
            wada_v = w_ada.rearrange("(kc p) n -> p kc n", p=128)
            for pc in range(6):
                sl = pc % 2
                P.dma("pool", lambda e, pc=pc, sl=sl: e.dma_start(
                    out=wada[sl][:], in_=wada_v[:, :, pc * 1024:(pc + 1) * 1024]), W=[B_wada[sl]])
                for nch in range(8):
                    ch = pc * 8 + nch
                    for kc in range(8):
                        P.op("pe", lambda e, sl=sl, nch=nch, kc=kc, ch=ch: e.matmul(
                            bank(0, 5, ch * 5), lhsT=wada[sl][:, kc, nch * 128:(nch + 1) * 128], rhs=cTb[:, kc, :],
                            start=(kc == 0), stop=(kc == 7)), R=[B_wada[sl], B_c], W=[B_ps0])
            P.op("dve", lambda e: e.tensor_tensor(
                out=modT[:], in0=bank(0, 240).rearrange("p (c v) -> p c v", v=5),
                in1=badaT[:, :].unsqueeze(2).to_broadcast([128, 48, 5]), op=ALU.add), R=[B_ps0, B_c], W=[B_mod])
            P.op("dve", lambda e: e.scalar_tensor_tensor(
                out=a1T[:], in0=modT[:, 8:16, :], scalar=1.0, in1=gn[:, 0:8].unsqueeze(2).to_broadcast([128, 8, 5]),
                op0=ALU.add, op1=ALU.mult), R=[B_mod, B_small], W=[B_mod])
            P.op("dve", lambda e: e.scalar_tensor_tensor(
                out=a2T[:], in0=modT[:, 32:40, :], scalar=1.0, in1=gn[:, 8:16].unsqueeze(2).to_broadcast([128, 8, 5]),
                op0=ALU.add, op1=ALU.mult), R=[B_mod, B_small], W=[B_mod])
            P.barrier()

        def sh1(kc, v=0):
            return modT[:, 0 + kc, v:v + 1]

        def g1(kc, v=0):
            return modT[:, 16 + kc, v:v + 1]

        def sh2(kc, v=0):
            return modT[:, 24 + kc, v:v + 1]

        def g2(kc, v=0):
            return modT[:, 40 + kc, v:v + 1]

        w_in_v = w_in.rearrange("(kc p) n -> p kc n", p=128)

        class NormPipe:
            def __init__(self, stack, nx=3):
                self.nx = nx
                self.xsl = [sb("xsl%d" % i, [128, D], F32, stack) for i in range(nx)]
                self.xn = [sb("xn%d" % i, [128, D], BF16, stack) for i in range(2)]
                self.junk = sb("junk", [128, D], BF16, stack)
                self.ssq = sb("ssq", [128, 4], F32, stack)
                self.rs = sb("rs_t", [128, 4], F32, stack)
                self.B_x = [Buf() for _ in range(nx)]
                self.B_xn = [Buf() for _ in range(2)]
                self.B_junk = Buf()
                self.B_ss = [Buf() for _ in range(4)]
                self.B_rs = [Buf() for _ in range(4)]
                self.TPs = [psum_bf[:, 2048 * i:2048 * (i + 1)].rearrange("p (k t) -> p k t", t=256) for i in range(2)]
                self.B_TP = [Buf() for _ in range(2)]
                self.c = dict(x=0, xn=0, ss=0, tp=0)

            def load(self, src_ap, nrows):
                xi = self.c["x"] % self.nx
                self.c["x"] += 1
                P.dma("sp", lambda e: e.dma_start(out=self.xsl[xi][0:nrows, :], in_=src_ap), W=[self.B_x[xi]])
                return xi

            def rstd(self, xi, nrows, rstd_ap=None, B_rs=None):
                si = self.c["ss"] % 4
                self.c["ss"] += 1
                if rstd_ap is None:
                    rstd_ap, B_rs = self.rs[0:nrows, si:si + 1], self.B_rs[si]
                P.op("act", lambda e: e.activation(out=self.junk[0:nrows, :], in_=self.xsl[xi][0:nrows, :],
                                                   func=AF.Square, accum_out=self.ssq[0:nrows, si:si + 1]),
                     R=[self.B_x[xi]], W=[self.B_junk, self.B_ss[si]])
                P.op("act", lambda e: e.activation(out=self.ssq[0:nrows, si:si + 1], in_=self.ssq[0:nrows, si:si + 1],
                                                   func=AF.Sqrt, scale=1.0 / D, bias=EPS),
                     R=[self.B_ss[si]], W=[self.B_ss[si]])
                P.op("dve", lambda e: e.reciprocal(out=rstd_ap, in_=self.ssq[0:nrows, si:si + 1]),
                     R=[self.B_ss[si]], W=[B_rs])
                return rstd_ap, B_rs

            def normalize(self, xi, nrows, rstd_ap, B_rs):
                ni = self.c["xn"] % 2
                self.c["xn"] += 1
                P.op("act", lambda e: e.activation(out=self.xn[ni][0:nrows, :], in_=self.xsl[xi][0:nrows, :],
                                                   func=AF.Copy, scale=rstd_ap),
                     R=[self.B_x[xi], B_rs], W=[self.B_xn[ni]])
                return ni

            def new_tp(self):
                ti = self.c["tp"] % 2
                self.c["tp"] += 1
                return ti

            def transpose(self, ni, nrows, ti, col0):
                for kc in range(8):
                    P.op("pe", lambda e, kc=kc: e.transpose(self.TPs[ti][:, kc, col0:col0 + nrows],
                                                            self.xn[ni][0:nrows, kc * 128:(kc + 1) * 128],
                                                            identb[0:nrows, 0:nrows]),
                         R=[self.B_xn[ni], B_small], W=[self.B_TP[ti]])

            def evac_mod(self, ti, ncols, dst, B_dst, aT, shf, sample=False):
                if not sample:
                    for kc in range(8):
                        if kc % 2 == 0:
                            P.op("dve", lambda e, kc=kc: e.tensor_scalar(
                                out=dst(kc), in0=self.TPs[ti][:, kc, 0:ncols], scalar1=aT[:, kc, 0:1],
                                scalar2=shf(kc), op0=ALU.mult, op1=ALU.add), R=[self.B_TP[ti], B_mod], W=[B_dst])
                        else:
                            P.op("act", lambda e, kc=kc: e.activation(
                                out=dst(kc), in_=self.TPs[ti][:, kc, 0:ncols], func=AF.Identity,
                                scale=aT[:, kc, 0:1], bias=shf(kc)), R=[self.B_TP[ti], B_mod], W=[B_dst])
                else:
                    for kc in range(8):
                        for s in range(4):
                            c0 = 32 * s
                            P.op("dve", lambda e, kc=kc, s=s, c0=c0: e.tensor_scalar(
                                out=dst(kc)[:, c0:c0 + 32], in0=self.TPs[ti][:, kc, c0:c0 + 32],
                                scalar1=aT[:, kc, 1 + s:2 + s], scalar2=shf(kc, 1 + s), op0=ALU.mult, op1=ALU.add),
                                R=[self.B_TP[ti], B_mod], W=[B_dst])

        with ExitStack() as sA:
            KT = sb("KT", [128, 2, SEQ], BF16, sA)
            Vaug = sb("Vaug", [128, NB, 4, 65], BF16, sA)
            QT = sb("QT", [128, 2, 2048], BF16, sA)
            brow_bc = sb("brow_bc", [128, 1032], F32, sA)
            zf_all = sb("zf_all", [128, NB, 8], F32, sA)
            rstd_all = sb("rstd_all", [128, NB], F32, sA)
            rstd_o = sb("rstd_o", [128, NOWN + 1], F32, sA)
            bias_all = sb("bias_all", [128, 160, 8], F32, sA)
            zf_own = sb("zf_own", [128, NOWN + 1, 8], F32, sA)
            Wc = sb("Wc", [128, NB, 8], F32, sA)
            Tex = sb("Tex", [128, NB + 1, 8], F32, sA)
            onesrow = sb("onesrow", [128, NB], F32, sA)
            B_KT = [Buf() for _ in range(NB // 2)]
            B_V = [Buf() for _ in range(NB)]
            B_QT = [Buf() for _ in range(NOWN)]
            B_brow = Buf()
            B_zf = Buf()
            B_rstd = [Buf() for _ in range(NB)]
            B_rso = [Buf() for _ in range(NOWN + 1)]
            B_bias = Buf()
            B_zfo = Buf()
            B_lf = Buf()

            P.op("pool", lambda e: e.memset(Vaug[:, :, :, 64:65], 1.0), W=B_V)
            P.op("pool", lambda e: e.memset(onesrow[:], 1.0), W=[B_lf])
            P.op("pool", lambda e: e.memset(Tex[:, 0, :], 0.0), W=[B_lf])
            with ExitStack() as sb0:
                brow = sb("brow", [1, 1032], F32, sb0)
                B_br = Buf()
                B_pb = Buf()
                P.dma("sp", lambda e: e.dma_start(out=brow[:], in_=b_rows), W=[B_br])
                for (o, n) in ((0, 512), (512, 512), (1024, 8)):
                    P.op("pe", lambda e, o=o, n=n: e.matmul(bank(1, n), lhsT=ones[0:1, :], rhs=brow[0:1, o:o + n],
                                                            start=True, stop=True), R=[B_cst, B_br], W=[B_pb])
                    P.op("act", lambda e, o=o, n=n: e.activation(out=brow_bc[:, o:o + n], in_=bank(1, n), func=AF.Copy),
                         R=[B_pb], W=[B_brow])
                P.barrier()

            for g in range(2):
                with ExitStack() as s1:
                    NP = NormPipe(s1)
                    wqkv = sb("wqkv", [128, 8, 776], BF16, s1)
                    hT = [sb("hT%d" % i, [128, 8, 256], BF16, s1) for i in range(2)]
                    kst = [sb("kst%d" % i, [128, 256], F32, s1) for i in range(2)]
                    vst = [sb("vst%d" % i, [128, 256], F32, s1) for i in range(2)]
                    B_w = Buf()
                    B_hT = [Buf() for _ in range(2)]
                    B_pk = [Buf() for _ in range(1)]
                    B_pv = [Buf() for _ in range(3)]
                    B_kst = [Buf() for _ in range(2)]
                    B_vst = [Buf() for _ in range(2)]
                    c1 = dict(ht=0, pk=0, pv=0, kst=0, vst=0)
                    for (o, src, n) in ((0, Q_OFF + 256 * g, 256), (256, K_OFF + 256 * g, 256),
                                        (512, V_OFF + 256 * g, 256), (768, F_OFF, 8)):
                        P.dma("pool", lambda e, o=o, src=src, n=n: e.dma_start(out=wqkv[:, :, o:o + n],
                                                                               in_=w_in_v[:, :, src:src + n]), W=[B_w])
                    nv = 264 if g == 0 else 256
                    for bt in range(NB // 2):
                        ti = NP.new_tp()
                        for bb in range(2):
                            blk = 2 * bt + bb
                            xi = NP.load(xseq[blk * 128:(blk + 1) * 128, :], 128)
                            if g == 0:
                                NP.rstd(xi, 128, rstd_all[:, blk:blk + 1], B_rstd[blk])
                            ni = NP.normalize(xi, 128, rstd_all[:, blk:blk + 1], B_rstd[blk])
                            NP.transpose(ni, 128, ti, bb * 128)
                        hi = c1["ht"] % 2
                        c1["ht"] += 1
                        NP.evac_mod(ti, 256, lambda kc, hi=hi: hT[hi][:, kc, 0:256], B_hT[hi], a1T, sh1)
                        pk = 0
                        c1["pk"] += 1
                        for pl in range(2):
                            for kc in range(8):
                                P.op("pe", lambda e, pl=pl, kc=kc, pk=pk, hi=hi: e.matmul(
                                    bank(4 + pk, 256, pl * 256), lhsT=wqkv[:, kc, 256 + pl * 128:256 + (pl + 1) * 128],
                                    rhs=hT[hi][:, kc, 0:256], start=(kc == 0), stop=(kc == 7)),
                                    R=[B_w, B_hT[hi]], W=[B_pk[pk]])
                        for pl in range(2):
                            P.op("dve", lambda e, pl=pl, pk=pk, bt=bt, g=g: e.tensor_scalar(
                                out=KT[:, pl, bt * 256:(bt + 1) * 256], in0=bank(4 + pk, 256, pl * 256),
                                scalar1=bfm[:, 4 + 2 * g + pl:5 + 2 * g + pl], scalar2=None, op0=ALU.add),
                                R=[B_pk[pk], B_small], W=[B_KT[bt]])
                        for bb in range(2):
                            blk = 2 * bt + bb
                            pv = c1["pv"] % 3
                            c1["pv"] += 1
                            for kc in range(8):
                                P.op("pe", lambda e, bb=bb, kc=kc, pv=pv, hi=hi, nv=nv: e.matmul(
                                    bank(5 + pv, nv, 0), lhsT=hT[hi][:, kc, bb * 128:(bb + 1) * 128],
                                    rhs=wqkv[:, kc, 512:512 + nv], start=(kc == 0), stop=(kc == 7)),
                                    R=[B_w, B_hT[hi]], W=[B_pv[pv]])
                            P.op("dve", lambda e, pv=pv, blk=blk, g=g: e.tensor_tensor(
                                out=Vaug[:, blk, :, 0:64],
                                in0=bank(5 + pv, 256, 0).rearrange("p (h d) -> p h d", d=64),
                                in1=brow_bc[:, 512 + 256 * g:512 + 256 * (g + 1)].rearrange("p (h d) -> p h d", d=64),
                                op=ALU.add), R=[B_pv[pv], B_brow], W=[B_V[blk]])
                            if g == 0:
                                P.op("dve", lambda e, pv=pv, blk=blk: e.tensor_tensor(
                                    out=zf_all[:, blk, :], in0=bank(5 + pv, 8, 256),
                                    in1=brow_bc[:, 1024:1032], op=ALU.add), R=[B_pv[pv], B_brow], W=[B_zf])

                    for q in range(NOWN + 1):
                        sample = (q == NOWN)
                        ti = NP.new_tp()
                        xi = NP.load(xs_d if sample else xown[q, 32:160, :], 128)
                        if g == 0:
                            NP.rstd(xi, 128, rstd_o[:, q:q + 1], B_rso[q])
                        ni = NP.normalize(xi, 128, rstd_o[:, q:q + 1], B_rso[q])
                        NP.transpose(ni, 128, ti, 0)
                        hi = c1["ht"] % 2
                        c1["ht"] += 1
                        NP.evac_mod(ti, 128, lambda kc, hi=hi: hT[hi][:, kc, 0:128], B_hT[hi], a1T, sh1, sample=sample)
                        pk = 0
                        c1["pk"] += 1
                        for pl in range(2):
                            for kc in range(8):
                                P.op("pe", lambda e, pl=pl, kc=kc, pk=pk, hi=hi: e.matmul(
                                    bank(4 + pk, 128, pl * 128), lhsT=wqkv[:, kc, pl * 128:(pl + 1) * 128],
                                    rhs=hT[hi][:, kc, 0:128], start=(kc == 0), stop=(kc == 7)),
                                    R=[B_w, B_hT[hi]], W=[B_pk[pk]])
                        if sample:
                            for pl in range(2):
                                for kc in range(8):
                                    P.op("pe", lambda e, pl=pl, kc=kc, pk=pk, hi=hi: e.matmul(
                                        bank(4 + pk, 128, 256 + pl * 128),
                                        lhsT=wqkv[:, kc, 256 + pl * 128:256 + (pl + 1) * 128],
                                        rhs=hT[hi][:, kc, 0:128], start=(kc == 0), stop=(kc == 7)),
                                        R=[B_w, B_hT[hi]], W=[B_pk[pk]])
                        for pl in range(2):
                            qdst = QTs[:, 2 * g + pl, :] if sample else QT[:, pl, q * 128:(q + 1) * 128]
                            P.op("dve", lambda e, pl=pl, pk=pk, qdst=qdst, g=g: e.tensor_scalar(
                                out=qdst, in0=bank(4 + pk, 128, pl * 128),
                                scalar1=bfm[:, 2 * g + pl:2 * g + pl + 1], scalar2=None, op0=ALU.add),
                                R=[B_pk[pk], B_small], W=[B_QTs if sample else B_QT[q]])
                            if sample:
                                P.op("dve", lambda e, pl=pl, pk=pk, g=g: e.tensor_scalar(
                                    out=KTs[:, 2 * g + pl, :], in0=bank(4 + pk, 128, 256 + pl * 128),
                                    scalar1=bfm[:, 4 + 2 * g + pl:5 + 2 * g + pl], scalar2=None, op0=ALU.add),
                                    R=[B_pk[pk], B_small], W=[B_KTs])
                        pv = c1["pv"] % 3
                        c1["pv"] += 1
                        for kc in range(8):
                            P.op("pe", lambda e, kc=kc, pv=pv, hi=hi: e.matmul(
                                bank(5 + pv, 256, 0), lhsT=hT[hi][:, kc, 0:128], rhs=wqkv[:, kc, 256:512],
                                start=(kc == 0), stop=(kc == 7)), R=[B_w, B_hT[hi]], W=[B_pv[pv]])
                        ks_i = c1["kst"] % 2
                        c1["kst"] += 1
                        P.op("dve", lambda e, pv=pv, ks_i=ks_i, g=g: e.tensor_tensor(
                            out=kst[ks_i][:], in0=bank(5 + pv, 256, 0), in1=brow_bc[:, 256 * g:256 * (g + 1)],
                            op=ALU.add), R=[B_pv[pv], B_brow], W=[B_kst[ks_i]])
                        kdst = ks_o[:, 256 * g:256 * (g + 1)] if sample else k_own[q, :, 256 * g:256 * (g + 1)]
                        P.dma("act", lambda e, ks_i=ks_i, kdst=kdst: e.dma_start(out=kdst, in_=kst[ks_i][:]),
                              R=[B_kst[ks_i]])
                        pv = c1["pv"] % 3
                        c1["pv"] += 1
                        for kc in range(8):
                            P.op("pe", lambda e, kc=kc, pv=pv, hi=hi, nv=nv: e.matmul(
                                bank(5 + pv, nv, 0), lhsT=hT[hi][:, kc, 0:128], rhs=wqkv[:, kc, 512:512 + nv],
                                start=(kc == 0), stop=(kc == 7)), R=[B_w, B_hT[hi]], W=[B_pv[pv]])
                        vs_i = c1["vst"] % 2
                        c1["vst"] += 1
                        P.op("dve", lambda e, pv=pv, vs_i=vs_i, g=g: e.tensor_tensor(
                            out=vst[vs_i][:], in0=bank(5 + pv, 256, 0),
                            in1=brow_bc[:, 512 + 256 * g:512 + 256 * (g + 1)], op=ALU.add),
                            R=[B_pv[pv], B_brow], W=[B_vst[vs_i]])
                        if sample:
                            P.op("pool", lambda e, vs_i=vs_i, g=g: e.tensor_copy(
                                out=Vs_bf[:, 256 * g:256 * (g + 1)], in_=vst[vs_i][:]), R=[B_vst[vs_i]], W=[B_Vs])
                        if g == 0:
                            P.op("dve", lambda e, pv=pv, q=q: e.tensor_tensor(
                                out=zf_own[:, q, :], in0=bank(5 + pv, 8, 256), in1=brow_bc[:, 1024:1032], op=ALU.add),
                                R=[B_pv[pv], B_brow], W=[B_zfo])
                        vdst = vs_o[:, 256 * g:256 * (g + 1)] if sample else v_own[q, :, 256 * g:256 * (g + 1)]
                        P.dma("act", lambda e, vs_i=vs_i, vdst=vdst: e.dma_start(out=vdst, in_=vst[vs_i][:]),
                              R=[B_vst[vs_i]])
                    P.barrier()

                if g == 0:
                    for (zt, Bz) in ((zf_all, B_zf), (zf_own, B_zfo)):
                        P.op("act", lambda e, zt=zt: e.activation(out=zt[:], in_=zt[:], func=AF.Exp, scale=-1.0),
                             R=[Bz], W=[Bz])
                        P.op("act", lambda e, zt=zt: e.activation(out=zt[:], in_=zt[:], func=AF.Ln, bias=1.0),
                             R=[Bz], W=[Bz])
                        P.op("dve", lambda e, zt=zt: e.tensor_scalar(out=zt[:], in0=zt[:], scalar1=-1.0, scalar2=None,
                                                                     op0=ALU.mult), R=[Bz], W=[Bz])
                    P.dma("act", lambda e: e.dma_start(out=lf_own.rearrange("q p h -> p q h"),
                                                       in_=zf_own[:, 0:NOWN, :]), R=[B_zfo])
                    P.dma("act", lambda e: e.dma_start(out=lfs_o, in_=zf_own[:, NOWN, :]), R=[B_zfo])
                    P.op("dve", lambda e: e.tensor_copy(out=lfs_t[:], in_=zf_own[:, NOWN, :]), R=[B_zfo], W=[B_lfs])
                    B_pc = Buf()
                    lf_flat = zf_all[:].rearrange("p b h -> p (b h)")
                    P.op("pe", lambda e: e.matmul(bank(0), lhsT=U_incl, rhs=lf_flat, start=True, stop=True),
                         R=[B_cst, B_zf], W=[B_pc])
                    P.op("pe", lambda e: e.matmul(bank(1), lhsT=ones, rhs=lf_flat, start=True, stop=True),
                         R=[B_cst, B_zf], W=[B_pc])
                    P.op("act", lambda e: e.activation(out=Wc[:].rearrange("p b h -> p (b h)"), in_=bank(0),
                                                       func=AF.Copy), R=[B_pc], W=[B_lf])
                    for h in range(H):
                        P.op("dve", lambda e, h=h: e.tensor_tensor_scan(
                            out=Tex[:, 1:NB + 1, h], data0=onesrow[:, :],
                            data1=bank(1).rearrange("p (b h) -> p b h", h=8)[:, :, h], initial=0.0,
                            op0=ALU.mult, op1=ALU.add), R=[B_pc, B_lf], W=[B_lf])
                    for m in range(4):
                        nk = 16 * (m + 1)
                        dstb = bias_all[:, QBASE[m]:QBASE[m] + nk, :]
                        P.op("dve", lambda e, nk=nk, dstb=dstb: e.tensor_tensor(
                            out=dstb, in0=Tex[:, 0:nk, :], in1=Wc[:, 0:nk, :], op=ALU.add), R=[B_lf], W=[B_bias])
                        P.op("dve", lambda e, nk=nk, dstb=dstb, m=m: e.scalar_tensor_tensor(
                            out=dstb, in0=dstb, scalar=-1.0, in1=Tex[:, 16 * m:16 * m + 1, :].to_broadcast([128, nk, 8]),
                            op0=ALU.mult, op1=ALU.add), R=[B_lf, B_bias], W=[B_bias])
                    P.barrier()

                if STAGE < 2 or SKIP_ATT:
                    continue
                with ExitStack() as s2:
                    PT = [sb("PT%d" % i, [128, 512], BF16, s2) for i in range(4)]
                    Osb = [sb("Osb%d" % i, [128, 512], F32, s2) for i in range(2)]
                    rden = sb("rden", [128, 512], F32, s2)
                    B_PT = [Buf() for _ in range(4)]
                    B_S = [Buf() for _ in range(4)]
                    B_O = [Buf() for _ in range(2)]
                    B_Osb = [Buf() for _ in range(2)]
                    B_rden = Buf()
                    B_BC = Buf()
                    c2 = dict(s=0, o=0)
                    for m in range(4):
                        for hl in range(4):
                            pl, e2 = hl // 2, hl % 2
                            h = 4 * g + hl
                            prt = slice(64 * e2, 64 * e2 + 64)
                            kbs = [(kb, 0, None) for kb in range(16 * m)] + \
                                  [(16 * m + 4 * ip + r, 128 * ip, r) for ip in range(4) for r in range(4)]
                            oi = c2["o"] % 2
                            c2["o"] += 1
                            RQ = [B_QT[4 * m + i] for i in range(4)]

                            def emit_pv(pi, kb, c0, first, last, oi=oi, hl=hl):
                                P.op("pe", lambda e: e.matmul(
                                    bank(4 + oi)[0:65, c0:512], lhsT=Vaug[:, kb, hl, 0:65], rhs=PT[pi][:, c0:512],
                                    start=first, stop=last), R=[B_V[kb], B_PT[pi]], W=[B_O[oi]])

                            prev = None
                            for idx, (kb, c0, r) in enumerate(kbs):
                                si = c2["s"] % 4
                                c2["s"] += 1
                                P.op("pe", lambda e, si=si, kb=kb, c0=c0, r=r, pl=pl, prt=prt, m=m: e.matmul(
                                    bank(si)[:, c0:512], lhsT=KT[prt, pl, kb * 128:(kb + 1) * 128],
                                    rhs=QT[prt, pl, m * 512 + c0:(m + 1) * 512], start=True, stop=(r is None)),
                                    R=[B_KT[kb // 2]] + RQ, W=[B_S[si]])
                                if r is not None:
                                    P.op("pe", lambda e, si=si, c0=c0, r=r: e.matmul(
                                        bank(si)[:, c0:c0 + 128], lhsT=identb[:], rhs=maskT[:, r, :],
                                        start=False, stop=True), R=[B_mask, B_small], W=[B_S[si]])
                                if prev is not None:
                                    emit_pv(*prev)
                                P.op("act", lambda e, si=si, c0=c0, kb=kb, m=m, h=h: e.activation(
                                    out=PT[si][:, c0:512], in_=bank(si)[:, c0:512], func=AF.Exp, scale=SCALE,
                                    bias=bias_all[:, QBASE[m] + kb, h:h + 1]), R=[B_S[si], B_bias], W=[B_PT[si]])
                                prev = (si, kb, c0, idx == 0, idx == len(kbs) - 1)
                            emit_pv(*prev)
                            P.op("act", lambda e, oi=oi: e.activation(out=Osb[oi][0:65, :], in_=bank(4 + oi)[0:65, :],
                                                                      func=AF.Copy), R=[B_O[oi]], W=[B_Osb[oi]])
                            P.op("dve", lambda e, oi=oi: e.reciprocal(out=rden[64:65, :], in_=Osb[oi][64:65, :]),
                                 R=[B_Osb[oi]], W=[B_rden])
                            P.op("pe", lambda e: e.matmul(bank(6)[0:64, :], lhsT=ones[64:65, 0:64], rhs=rden[64:65, :],
                                                          start=True, stop=True), R=[B_cst, B_rden], W=[B_BC])
                            P.op("dve", lambda e, oi=oi, h=h, m=m: e.tensor_tensor(
                                out=oT[0:64, h, m * 512:(m + 1) * 512], in0=Osb[oi][0:64, :], in1=bank(6)[0:64, :],
                                op=ALU.mult), R=[B_Osb[oi], B_BC], W=[B_oT[m]])
                    P.barrier()


        def sample_phase():
          if True:
            with ExitStack() as sS:
                  ptr_sb = sb("ptr_sb", [128, 32], I32, sS)
                  shi_sb = sb("shi_sb", [128, 1], F32, sS)
                  idx_all = sb("idx_all", [128, 32], I32, sS)
                  bmask = sb("bmask", [32, 512], F32, sS)
                  K8 = [sb("K8_%d" % i, [128, 4096], F32, sS) for i in range(2)]
                  V8 = [sb("V8_%d" % i, [128, 4096], F32, sS) for i in range(2)]
                  V8b = [sb("V8b_%d" % i, [128, 4096], BF16, sS) for i in range(2)]
                  K8T = [sb("K8T_%d" % i, [128, 8, 4, 128], BF16, sS) for i in range(2)]
                  Lall = sb("Lall", [128, 8, 8, 8], F32, sS)
                  Ls = [sb("Ls%d" % i, [128, 8, 8, 8], F32, sS) for i in range(3)]
                  Rall = sb("Rall", [128, 8, 8, 8], F32, sS)
                  rowtot = sb("rowtot", [128, 8, 8], F32, sS)
                  gsuf = sb("gsuf", [128, 8, 8], F32, sS)
                  sctmp = [sb("sctmp%d" % i, [128, 8, 8, 4], F32, sS) for i in range(2)]
                  Pt = [sb("Pt%d" % i, [128, 8, 32], BF16, sS) for i in range(2)]
                  onesb = sb("onesb", [128, 2], BF16, sS)
                  m4 = sb("m4", [128, 4], F32, sS)
                  cnn = sb("cnn", [128, 8], F32, sS)
                  tmpN = sb("tmpN", [128, 8, 4], F32, sS)
                  PtN = sb("PtN", [128, 32], BF16, sS)
                  Oss = sb("Oss", [32, 512], F32, sS)
                  Osel = sb("Osel", [32, 64], F32, sS)
                  rdn = sb("rdn", [32, 2], F32, sS)
                  B_idx, B_bm, B_L, B_R, B_rt, B_gs, B_m4, B_cn, B_tN, B_PtN, B_Oss, B_Osel, B_rdn = (Buf() for _ in range(13))
                  B_Ls = [Buf() for _ in range(3)]
                  B_K8 = [Buf(), Buf()]
                  B_V8 = [Buf(), Buf()]
                  B_V8b = [Buf(), Buf()]
                  B_K8T = [Buf(), Buf()]
                  B_sct = [Buf(), Buf()]
                  B_Pt = [Buf(), Buf()]
                  BKs = [Buf() for _ in range(8)]
                  cS = dict(k=0, t=0, s=0, e=0)
                  P.dma("sp", lambda e: e.dma_start(out=ptr_sb[:], in_=pt_rep), W=[B_idx])
                  P.dma("sp", lambda e: e.dma_start(out=shi_sb[:], in_=shi_d), W=[B_idx])
                  P.dma("sp", lambda e: e.dma_start(out=bmask[:], in_=bmask_d), W=[B_bm])
                  P.op("dve", lambda e: e.tensor_scalar(out=idx_all[:], in0=ptr_sb[:], scalar1=16.0, scalar2=shi_sb[:, 0:1],
                                                        op0=ALU.mult, op1=ALU.add), R=[B_idx], W=[B_idx])
                  if SDBG == 1:
                      return
                  P.op("pool", lambda e: e.memset(onesb[:], 1.0), W=[B_m4])
                  P.op("pool", lambda e: e.memset(oT[:, :, 2048:2176], 0.0), W=[B_oT[4]])
                  P.op("dve", lambda e: e.tensor_scalar(out=m4[0:4, :], in0=tmins[0:4, 0:4], scalar1=0.0, scalar2=None,
                                                        op0=ALU.is_ge), R=[B_cst], W=[B_m4])
                  P.op("dve", lambda e: e.tensor_scalar(out=m4[0:4, :], in0=m4[0:4, :], scalar1=-1.0, scalar2=BIG,
                                                        op0=ALU.add, op1=ALU.mult), R=[B_m4], W=[B_m4])
                  Vs4 = sb("Vs4", [4, 4, 512], BF16, sS)
                  lfs4 = sb("lfs4", [4, 4, 8], F32, sS)
                  B_n4 = Buf()
                  for s_ in range(4):
                      P.dma("sp", lambda e, s_=s_: e.dma_start(out=Vs4[0:4, s_, :], in_=Vs_bf[32 * s_:32 * s_ + 4, :]),
                            R=[B_Vs], W=[B_n4])
                      P.dma("sp", lambda e, s_=s_: e.dma_start(out=lfs4[0:4, s_, :], in_=lfs_t[32 * s_:32 * s_ + 4, :]),
                            R=[B_lfs], W=[B_n4])
                  for s_ in range(4):
                      sl = slice(0, 4)
                      for g8 in range(8):
                          P.dma("pool", lambda e, g8=g8, s_=s_: e.indirect_dma_start(
                              out=Lall[:, g8, :, :].rearrange("p a b -> p (a b)"), out_offset=None, in_=cache_l,
                              in_offset=bass.IndirectOffsetOnAxis(ap=idx_all[:, 8 * s_ + g8:8 * s_ + g8 + 1], axis=0)),
                              R=[B_idx], W=[B_L])
                      if SDBG == 2:
                          return
                      P.op("dve", lambda e: e.tensor_reduce(out=rowtot[:], in_=Lall[:].rearrange("p g s h -> p g h s"),
                                                            axis=mybir.AxisListType.X, op=ALU.add), R=[B_L], W=[B_rt])
                      src, Bsrc = Lall, B_L
                      for step, (dst, Bd) in zip((1, 2, 4), zip(Ls, B_Ls)):
                          P.op("dve", lambda e, src=src, dst=dst: e.tensor_copy(out=dst[:], in_=src[:]), R=[Bsrc], W=[Bd])
                          P.op("dve", lambda e, src=src, dst=dst, step=step: e.tensor_tensor(
                              out=dst[:, :, 0:8 - step, :], in0=src[:, :, 0:8 - step, :], in1=src[:, :, step:8, :], op=ALU.add),
                              R=[Bsrc], W=[Bd])
                          src, Bsrc = dst, Bd
                      P.op("dve", lambda e, src=src: e.tensor_tensor(out=Rall[:], in0=src[:], in1=Lall[:], op=ALU.subtract),
                           R=[Bsrc, B_L], W=[B_R])
                      rt_flat = rowtot[:].rearrange("p g h -> p (g h)")
                      P.op("pe", lambda e: e.matmul(bank(6)[:, 0:64], lhsT=U_strict, rhs=rt_flat, start=True, stop=True),
                           R=[B_cst, B_rt], W=[BKs[6]])
                      P.op("pe", lambda e: e.matmul(bank(7)[:, 0:64], lhsT=ones, rhs=rt_flat, start=True, stop=True),
                           R=[B_cst, B_rt], W=[BKs[7]])
                      P.op("pool", lambda e: e.memset(gsuf[:], 0.0), W=[B_gs])
                      for g8 in range(6, -1, -1):
                          P.op("dve", lambda e, g8=g8: e.tensor_tensor(
                              out=gsuf[:, g8, :], in0=gsuf[:, g8 + 1, :], in1=bank(7)[:, 8 * (g8 + 1):8 * (g8 + 2)], op=ALU.add),
                              R=[BKs[7], B_gs], W=[B_gs])
                      P.op("dve", lambda e: e.tensor_tensor(out=gsuf[:], in0=gsuf[:],
                                                            in1=bank(6)[:, 0:64].rearrange("p (g h) -> p g h", h=8), op=ALU.add),
                           R=[BKs[6], B_gs], W=[B_gs])
                      P.op("dve", lambda e: e.tensor_tensor(out=Rall[:], in0=Rall[:],
                                                            in1=gsuf[:].unsqueeze(2).to_broadcast([128, 8, 8, 8]), op=ALU.add),
                           R=[B_gs, B_R], W=[B_R])
                      if SDBG == 3:
                          return
                      for g8 in range(8):
                          ki = cS["k"] % 2
                          cS["k"] += 1
                          ix = idx_all[:, 8 * s_ + g8:8 * s_ + g8 + 1]
                          P.dma("pool", lambda e, ki=ki, ix=ix: e.indirect_dma_start(
                              out=K8[ki][:, :], out_offset=None, in_=cache_k,
                              in_offset=bass.IndirectOffsetOnAxis(ap=ix, axis=0)), R=[B_idx], W=[B_K8[ki]])
                          P.dma("pool", lambda e, ki=ki, ix=ix: e.indirect_dma_start(
                              out=V8[ki][:, :], out_offset=None, in_=cache_v,
                              in_offset=bass.IndirectOffsetOnAxis(ap=ix, axis=0)), R=[B_idx], W=[B_V8[ki]])
                          for hh in range(2):
                              P.op("pool", lambda e, ki=ki, hh=hh: e.tensor_copy(out=V8b[ki][:, 2048 * hh:2048 * (hh + 1)],
                                                                                 in_=V8[ki][:, 2048 * hh:2048 * (hh + 1)]),
                                   R=[B_V8[ki]], W=[B_V8b[ki]])
                          for sl8 in range(8):
                              tb = cS["t"] % 2
                              cS["t"] += 1
                              for pr in range(4):
                                  P.op("pe", lambda e, ki=ki, sl8=sl8, pr=pr, tb=tb: e.transpose(
                                      bank(tb)[:, 128 * pr:128 * (pr + 1)],
                                      K8[ki][:, 512 * sl8 + 128 * pr:512 * sl8 + 128 * (pr + 1)], ident),
                                      R=[B_K8[ki], B_cst], W=[BKs[tb]])
                              ev = cS["e"] % 2
                              cS["e"] += 1
                              if ev == 0:
                                  P.op("act", lambda e, ki=ki, sl8=sl8, tb=tb: e.activation(
                                      out=K8T[ki][:, sl8, :, :].rearrange("p a b -> p (a b)"), in_=bank(tb), func=AF.Copy),
                                      R=[BKs[tb]], W=[B_K8T[ki]])
                              else:
                                  P.op("dve", lambda e, ki=ki, sl8=sl8, tb=tb: e.tensor_copy(
                                      out=K8T[ki][:, sl8, :, :].rearrange("p a b -> p (a b)"), in_=bank(tb)),
                                      R=[BKs[tb]], W=[B_K8T[ki]])
                          si = cS["s"] % 2
                          cS["s"] += 1
                          for sl8 in range(8):
                              for h_ in range(8):
                                  pr, e2 = h_ // 2, h_ % 2
                                  prt = slice(64 * e2, 64 * e2 + 64)
                                  P.op("pe", lambda e, ki=ki, sl8=sl8, h_=h_, pr=pr, prt=prt, si=si, s_=s_: e.matmul(
                                      bank(2 + si)[:, 32 * sl8 + 4 * h_:32 * sl8 + 4 * h_ + 4], lhsT=K8T[ki][prt, sl8, pr, :],
                                      rhs=QTs[prt, pr, 32 * s_:32 * s_ + 4], start=True, stop=True),
                                      R=[B_K8T[ki], B_QTs], W=[BKs[2 + si]])
                          P.op("dve", lambda e, si=si, g8=g8: e.scalar_tensor_tensor(
                              out=sctmp[si][:], in0=bank(2 + si)[:, 0:256].rearrange("p (a h t) -> p a h t", h=8, t=4),
                              scalar=SCALE, in1=Rall[:, g8, :, :].unsqueeze(3).to_broadcast([128, 8, 8, 4]),
                              op0=ALU.mult, op1=ALU.add), R=[BKs[2 + si], B_R], W=[B_sct[si]])
                          P.op("act", lambda e, si=si: e.activation(
                              out=Pt[si][:].rearrange("p a b -> p (a b)"), in_=sctmp[si][:].rearrange("p a h t -> p (a h t)"),
                              func=AF.Exp), R=[B_sct[si]], W=[B_Pt[si]])
                          for sl8 in range(8):
                              first = (g8 == 0 and sl8 == 0)
                              P.op("pe", lambda e, si=si, ki=ki, sl8=sl8, first=first: e.matmul(
                                  bank(4)[0:32, :], lhsT=Pt[si][:, sl8, :], rhs=V8b[ki][:, 512 * sl8:512 * (sl8 + 1)],
                                  start=first, stop=False), R=[B_Pt[si], B_V8b[ki]], W=[BKs[4]])
                              P.op("pe", lambda e, si=si, sl8=sl8, first=first: e.matmul(
                                  bank(5)[0:32, 0:2], lhsT=Pt[si][:, sl8, :], rhs=onesb[:, 0:2],
                                  start=first, stop=False), R=[B_Pt[si], B_m4], W=[BKs[5]])
                      if SDBG == 4:
                          return
                      P.op("pe", lambda e, sl=sl, s_=s_: e.matmul(bank(6)[sl, 0:8], lhsT=U_incl[sl, sl], rhs=lfs4[sl, s_, :],
                                                           start=True, stop=True), R=[B_cst, B_n4, B_gs], W=[BKs[6]])
                      P.op("dve", lambda e, sl=sl: e.tensor_scalar(out=cnn[sl, :], in0=bank(6)[sl, 0:8], scalar1=-1.0,
                                                                   scalar2=None, op0=ALU.mult), R=[BKs[6]], W=[B_cn])
                      if SDBG == 41:
                          return
                      for h_ in range(8):
                          pr, e2 = h_ // 2, h_ % 2
                          prt = slice(64 * e2, 64 * e2 + 64)
                          P.op("pe", lambda e, h_=h_, pr=pr, prt=prt, s_=s_, e2=e2: e.matmul(
                              bank(6 + e2)[0:32, 64 + 4 * h_:64 + 4 * h_ + 4], lhsT=KTs[prt, pr, 32 * s_:32 * s_ + 32],
                              rhs=QTs[prt, pr, 32 * s_:32 * s_ + 4], start=True, stop=True),
                              R=[B_KTs, B_QTs, B_gs, B_cn], W=[BKs[6 + e2]])
                      if SDBG == 42:
                          return
                      for e2 in range(2):
                          P.op("dve", lambda e, sl=sl, e2=e2: e.scalar_tensor_tensor(
                              out=tmpN[sl, :, :].rearrange("p (a b) t -> p a b t", b=2)[:, :, e2, :],
                              in0=bank(6 + e2)[sl, 64:96].rearrange("p (a b t) -> p a b t", b=2, t=4)[:, :, e2, :], scalar=SCALE,
                              in1=cnn[sl, :].rearrange("p (a b) -> p a b", b=2)[:, :, e2].unsqueeze(2).to_broadcast([4, 4, 4]),
                              op0=ALU.mult, op1=ALU.add), R=[BKs[6 + e2], B_cn], W=[B_tN])
                      P.op("dve", lambda e, sl=sl: e.tensor_tensor(
                          out=tmpN[sl, :, :], in0=tmpN[sl, :, :], in1=m4[sl, :].unsqueeze(1).to_broadcast([4, 8, 4]), op=ALU.add),
                          R=[B_tN, B_m4], W=[B_tN])
                      P.op("act", lambda e, sl=sl: e.activation(out=PtN[sl, :], in_=tmpN[sl, :, :].rearrange("p h t -> p (h t)"),
                                                                func=AF.Exp), R=[B_tN], W=[B_PtN])
                      if SDBG == 43:
                          return
                      P.op("pe", lambda e, sl=sl, s_=s_: e.matmul(bank(4)[0:32, :], lhsT=PtN[sl, :], rhs=Vs4[sl, s_, :],
                                                           start=False, stop=True), R=[B_PtN, B_n4], W=[BKs[4]])
                      P.op("pe", lambda e, sl=sl: e.matmul(bank(5)[0:32, 0:2], lhsT=PtN[sl, :], rhs=onesb[sl, 0:2],
                                                           start=False, stop=True), R=[B_PtN, B_m4], W=[BKs[5]])
                      if SDBG == 5:
                          return
                      P.op("act", lambda e: e.activation(out=Oss[:], in_=bank(4)[0:32, :], func=AF.Copy), R=[BKs[4]], W=[B_Oss])
                      P.op("dve", lambda e: e.reciprocal(out=rdn[:], in_=bank(5)[0:32, 0:2]), R=[BKs[5]], W=[B_rdn])
                      P.op("dve", lambda e: e.tensor_tensor(out=Oss[:], in0=Oss[:], in1=bmask[:], op=ALU.mult),
                           R=[B_bm, B_Oss], W=[B_Oss])
                      P.op("dve", lambda e: e.tensor_reduce(out=Osel[:], in_=Oss[:].rearrange("p (h d) -> p d h", d=64),
                                                            axis=mybir.AxisListType.X, op=ALU.add), R=[B_Oss], W=[B_Osel])
                      P.op("dve", lambda e: e.tensor_scalar(out=Osel[:], in0=Osel[:], scalar1=rdn[:, 0:1], scalar2=None,
                                                            op0=ALU.mult), R=[B_rdn, B_Osel], W=[B_Osel])
                      if DBG_OT and s_ == 0:
                          P.dma("sp", lambda e: e.dma_start(out=dbg2[:, 0:512], in_=Rall[:].rearrange("p a b c -> p (a b c)")), R=[B_R])
                          P.dma("sp", lambda e: e.dma_start(out=dbg2[:, 512:768], in_=sctmp[1][:].rearrange("p a b c -> p (a b c)")), R=[B_sct[1]])
                          P.dma("sp", lambda e: e.dma_start(out=dbg2[0:32, 768:1280], in_=Oss[:]), R=[B_Oss])
                          P.dma("sp", lambda e: e.dma_start(out=dbg2[0:32, 1280:1282], in_=rdn[:]), R=[B_rdn])
                          P.dma("sp", lambda e: e.dma_start(out=dbg2[0:32, 1290:1354], in_=Osel[:]), R=[B_Osel])
                          P.dma("sp", lambda e: e.dma_start(out=dbg2[:, 1400:1432], in_=idx_all[:].bitcast(F32)), R=[B_idx])
                          P.dma("sp", lambda e: e.dma_start(out=dbg2[0:4, 1440:1472], in_=tmpN[0:4, :, :].rearrange("p a b -> p (a b)")), R=[B_tN])
                      P.op("pe", lambda e: e.transpose(bank(7)[0:64, 64:96], Osel[:], ident[0:32, 0:32]),
                           R=[B_Osel, B_cst, B_tN], W=[BKs[7]])
                      P.op("act", lambda e, s_=s_: e.activation(
                          out=oT[0:64, :, 2048 + 32 * s_:2048 + 32 * s_ + 4],
                          in_=bank(7)[0:64, 64:96].rearrange("p (h t) -> p h t", t=4), func=AF.Copy),
                          R=[BKs[7]], W=[B_oT[4]])
                  P.barrier()
        if STAGE >= 4:
            sample_phase()
            P.barrier()

        if STAGE >= 3 and not SKIP_P5:
            def wv(w):
                return w.rearrange("(kc p) n -> p kc n", p=128)
            w_pa_v = w_pa.rearrange("(h p) n -> p h n", p=64)
            w_pb_v, w_o_v, w_f1_v, w_f2_v = wv(w_pb), wv(w_o), wv(w_f1), wv(w_f2)
            for hf in range(2):
                with ExitStack() as sH:
                    NT = 1152 if hf == 0 else 1024
                    nblk = NT // 128
                    xT = sb("xT", [128, 8, NT], F32, sH)
                    hTh = sb("hTh", [128, 8, NT], BF16, sH)
                    sT = sb("sT", [128, 4, NT], BF16, sH)
                    rsT = sb("rsT", [128, 512], F32, sH)
                    tA = [sb("tA%d" % i, [128, 512], F32, sH) for i in range(2)]
                    tB = [sb("tB%d" % i, [128, 512], F32, sH) for i in range(2)]
                    B_rsT = Buf()
                    B_tA = [Buf(), Buf()]
                    B_tB = [Buf(), Buf()]
                    BK = [Buf() for _ in range(8)]
                    tiles = [(0, 512, 2 * hf * 512), (512, 512, (2 * hf + 1) * 512)]
                    if hf == 0:
                        tiles.append((1024, 128, 2048))
                    B_xT = [Buf() for _ in tiles]
                    B_hTh = [Buf() for _ in tiles]
                    B_sT = [Buf() for _ in tiles]
                    rot = dict(a=0, b=0, c=0, d=0, t=0, u=0)

                    def nxt(k, n=2):
                        v = rot[k] % n
                        rot[k] += 1
                        return v

                    def per_sample(N, fn_full, fn_s):
                        if N == 512:
                            fn_full()
                        else:
                            for s_ in range(4):
                                fn_s(s_, 32 * s_)

                    def col_stats(ti, src_fn, nparts, scale, Rsrc=None):
                        c0, N, _ = tiles[ti]
                        for kc in range(nparts):
                            a = nxt("t")
                            P.op("act", lambda e, kc=kc, a=a: e.activation(out=tA[a][:, 0:N], in_=src_fn(kc), func=AF.Square),
                                 R=(Rsrc or [B_xT[ti]]), W=[B_tA[a]])
                            P.op("pe", lambda e, kc=kc, a=a: e.matmul(bank(6)[:, 0:N], lhsT=ones, rhs=tA[a][:, 0:N],
                                                                      start=(kc == 0), stop=(kc == nparts - 1)),
                                 R=[B_tA[a], B_cst], W=[BK[6]])
                        P.op("act", lambda e: e.activation(out=rsT[:, 0:N], in_=bank(6)[:, 0:N], func=AF.Sqrt, scale=scale,
                                                           bias=EPS), R=[BK[6]], W=[B_rsT])
                        P.op("dve", lambda e: e.reciprocal(out=rsT[:, 0:N], in_=rsT[:, 0:N]), R=[B_rsT], W=[B_rsT])

                    with ExitStack() as sa:
                        hTo = sb("hTo", [128, 8, 256], BF16, sa)
                        B_hTo = Buf()
                        with ExitStack() as s1:
                            NP = NormPipe(s1)
                            TPf = [psum[:, 2048 + 1024 * i:2048 + 1024 * (i + 1)].rearrange("p (k t) -> p k t", t=128)
                                   for i in range(2)]
                            B_TPf = [Buf(), Buf()]
                            for qq in range(nblk):
                                sample = (qq == 8)
                                q = 8 * hf + qq
                                col = qq * 128
                                ti_ = min(qq // 4, 2)
                                xi = NP.load(xs_d if sample else xown[q, 32:160, :], 128)
                                rs_ap, B_rs = NP.rstd(xi, 128)
                                ni = NP.normalize(xi, 128, rs_ap, B_rs)
                                tpi = NP.new_tp()
                                NP.transpose(ni, 128, tpi, 0)
                                NP.evac_mod(tpi, 128, lambda kc, col=col: hTh[:, kc, col:col + 128], B_hTh[ti_], a1T, sh1,
                                            sample=sample)
                                fi = qq % 2
                                for kc in range(8):
                                    P.op("pe", lambda e, kc=kc, fi=fi, xi=xi: e.transpose(
                                        TPf[fi][:, kc, :], NP.xsl[xi][:, kc * 128:(kc + 1) * 128], ident),
                                        R=[NP.B_x[xi], B_cst], W=[B_TPf[fi]])
                                P.op("act", lambda e, fi=fi, col=col: e.activation(out=xT[:, :, col:col + 128], in_=TPf[fi],
                                                                                   func=AF.Copy), R=[B_TPf[fi]], W=[B_xT[ti_]])
                                if not sample:
                                    xi = NP.load(xown[q, 0:32, :], 32)
                                    rs_ap, B_rs = NP.rstd(xi, 32)
                                    ni = NP.normalize(xi, 32, rs_ap, B_rs)
                                    tpi = NP.new_tp()
                                    NP.transpose(ni, 32, tpi, 0)
                                    NP.evac_mod(tpi, 32, lambda kc, qq=qq: hTo[:, kc, qq * 32:(qq + 1) * 32], B_hTo, a1T, sh1)
                            P.barrier()

                        with ExitStack() as s2:
                            wglu = sb("wglu", [128, 8, 1024], BF16, s2)
                            uT = sb("uT", [128, 4, 8, 160], BF16, s2)
                            uS = sb("uS", [128, 4, 4, 34], BF16, s2)
                            stTf = sb("stTf", [128, 4, 4, 30], F32, s2)
                            u32 = sb("u32", [128, 4, 128], F32, s2)
                            ucp = sb("ucp", [128, 512], F32, s2)
                            stro = sb("stro", [104, 512], F32, s2)
                            diag = [sb("diag%d" % i, [128, CK, 128], BF16, s2) for i in range(2)]
                            ycv = sb("ycv", [128, 4, 512], F32, s2)
                            mean_sb = sb("mean_sb", [128, 512], F32, s2)
                            B_wglu, B_uT, B_uS, B_u32, B_ucp, B_ycv, B_mean, B_stro = (Buf() for _ in range(8))
                            B_diag = [Buf(), Buf()]
                            for hh in range(2):
                                P.dma("pool", lambda e, hh=hh: e.dma_start(
                                    out=wglu[:, :, 512 * hh:512 * (hh + 1)],
                                    in_=w_in_v[:, :, GLU_OFF + 512 * hh:GLU_OFF + 512 * (hh + 1)]), W=[B_wglu])
                            P.op("pool", lambda e: e.memset(ycv[:], 0.0), W=[B_ycv])
                            P.op("pool", lambda e: e.memset(u32[:], 0.0), W=[B_u32])
                            if hf == 0:
                                P.dma("sp", lambda e: e.dma_start(out=stTf[:], in_=stT_d), W=[B_uS])
                                for cc in range(4):
                                    P.op("pool", lambda e, cc=cc: e.tensor_copy(out=uS[:, cc, :, 0:30], in_=stTf[:, cc, :, :]),
                                         R=[B_uS], W=[B_uS])
                                P.dma("sp", lambda e: e.dma_start(out=stro[:], in_=st_rows.rearrange("s r c -> (s r) c")),
                                      W=[B_stro])
                                for s_ in range(4):
                                    P.dma("act", lambda e, s_=s_: e.dma_start(out=convs_o[s_, 0:26, :],
                                                                              in_=stro[26 * s_:26 * (s_ + 1), :]), R=[B_stro])

                            def glu(src_fn, N, Rsrc, outs):
                                for cc in range(4):
                                    ia, ib = nxt("a"), nxt("b")
                                    for kc in range(8):
                                        P.op("pe", lambda e, kc=kc, cc=cc, ia=ia: e.matmul(
                                            bank(ia)[:, 0:N], lhsT=wglu[:, kc, cc * 128:(cc + 1) * 128], rhs=src_fn(kc),
                                            start=(kc == 0), stop=(kc == 7)), R=[B_wglu] + Rsrc, W=[BK[ia]])
                                    for kc in range(8):
                                        P.op("pe", lambda e, kc=kc, cc=cc, ib=ib: e.matmul(
                                            bank(2 + ib)[:, 0:N], lhsT=wglu[:, kc, 512 + cc * 128:512 + (cc + 1) * 128],
                                            rhs=src_fn(kc), start=(kc == 0), stop=(kc == 7)), R=[B_wglu] + Rsrc, W=[BK[2 + ib]])
                                    P.op("act", lambda e, cc=cc, ib=ib: e.activation(
                                        out=tB[ib][:, 0:N], in_=bank(2 + ib)[:, 0:N], func=AF.Sigmoid,
                                        bias=bfm[:, 12 + cc:13 + cc]), R=[BK[2 + ib], B_small], W=[B_tB[ib]])
                                    for (dst_fn, view, Wb) in outs:
                                        P.op("dve", lambda e, cc=cc, ia=ia, ib=ib, dst_fn=dst_fn, view=view: e.scalar_tensor_tensor(
                                            out=dst_fn(cc), in0=view(bank(ia)[:, 0:N]), scalar=bfm[:, 8 + cc:9 + cc],
                                            in1=view(tB[ib][:, 0:N]), op0=ALU.add, op1=ALU.mult),
                                            R=[BK[ia], B_tB[ib], B_small], W=Wb)

                            glu(lambda kc: hTo[:, kc, :], 256, [B_hTo],
                                [(lambda cc: uT[:, cc, :, 0:32], lambda a: a.rearrange("p (b t) -> p b t", t=32), [B_uT])])
                            for cc in range(4):
                                P.op("dve", lambda e, cc=cc: e.tensor_tensor(
                                    out=uT[:, cc, :, 0:32], in0=uT[:, cc, :, 0:32],
                                    in1=hval[:, 8 * hf:8 * hf + 8].unsqueeze(2).to_broadcast([128, 8, 32]), op=ALU.mult),
                                    R=[B_small, B_uT], W=[B_uT])
                            for ti, (c0, N, oc0) in enumerate(tiles):
                                if N == 512:
                                    outs = [(lambda cc, ti=ti: uT[:, cc, 4 * ti:4 * ti + 4, 32:160],
                                             lambda a: a.rearrange("p (b t) -> p b t", t=128), [B_uT])]
                                    if hf == 1 and ti == 1:
                                        outs.append((lambda cc: u32[:, cc, 0:32], lambda a: a[:, 480:512], [B_u32]))
                                else:
                                    outs = [(lambda cc: uS[:, cc, :, 30:34],
                                             lambda a: a.rearrange("p (s t) -> p s t", t=32)[:, :, 0:4], [B_uS]),
                                            (lambda cc: u32[:, cc, :], lambda a: a, [B_u32])]
                                glu(lambda kc, c0=c0, N=N: hTh[:, kc, c0:c0 + N], N, [B_hTh[ti]], outs)
                            for cc in range(4):
                                P.op("pe", lambda e, cc=cc: e.transpose(
                                    bank(5)[0:(128 if hf == 0 else 32), cc * 128:(cc + 1) * 128],
                                    u32[:, cc, 0:(128 if hf == 0 else 32)], ident), R=[B_u32, B_cst], W=[BK[5]])
                            nr = 128 if hf == 0 else 32
                            P.op("act", lambda e: e.activation(out=ucp[0:nr, :], in_=bank(5)[0:nr, :], func=AF.Copy),
                                 R=[BK[5]], W=[B_ucp])
                            if hf == 1:
                                P.dma("act", lambda e: e.dma_start(out=convp_o, in_=ucp[0:32, :]), R=[B_ucp])
                            else:
                                for s_ in range(4):
                                    P.dma("act", lambda e, s_=s_: e.dma_start(out=convs_o[s_, 26:30, :],
                                                                              in_=ucp[32 * s_:32 * s_ + 4, :]), R=[B_ucp])
                            for ti, (c0, N, oc0) in enumerate(tiles):
                                NC = N if N == 512 else 16
                                for cc in range(4):
                                    di = nxt("d")
                                    for k in range(CK):
                                        P.op("pool", lambda e, cc=cc, k=k, di=di: e.tensor_scalar(
                                            out=diag[di][:, k, :], in0=identb[:], scalar1=cvp[:, cc * CK + k:cc * CK + k + 1],
                                            scalar2=None, op0=ALU.mult), R=[B_small], W=[B_diag[di]])
                                    ci = nxt("c")
                                    for k in range(CK):
                                        if N == 512:
                                            rhs = uT[:, cc, 4 * ti:4 * ti + 4, 2 + k:2 + k + 128]
                                        else:
                                            rhs = uS[:, cc, :, k:k + 4]
                                        P.op("pe", lambda e, k=k, di=di, ci=ci, rhs=rhs: e.matmul(
                                            bank(4 + ci)[:, 0:NC], lhsT=diag[di][:, k, :], rhs=rhs, start=(k == 0),
                                            stop=(k == CK - 1)), R=[B_diag[di], B_uT, B_uS], W=[BK[4 + ci]])
                                    if N == 512:
                                        ydst, ysrc = ycv[:, cc, :], bank(4 + ci)[:, 0:512]
                                    else:
                                        ydst = ycv[:, cc, 0:128].rearrange("p (s t) -> p s t", t=32)[:, :, 0:4]
                                        ysrc = bank(4 + ci)[:, 0:16].rearrange("p (s t) -> p s t", t=4)
                                    P.op("act", lambda e, cc=cc, ydst=ydst, ysrc=ysrc: e.activation(
                                        out=ydst, in_=ysrc, func=AF.Identity, bias=cvp[:, 124 + cc:125 + cc]),
                                        R=[BK[4 + ci], B_small], W=[B_ycv])
                                for cc in range(4):
                                    P.op("pe", lambda e, cc=cc: e.matmul(bank(7)[:, 0:N], lhsT=ones, rhs=ycv[:, cc, 0:N],
                                                                         start=(cc == 0), stop=(cc == 3)),
                                         R=[B_ycv, B_cst], W=[BK[7]])
                                P.op("act", lambda e: e.activation(out=mean_sb[:, 0:N], in_=bank(7)[:, 0:N], func=AF.Copy,
                                                                   scale=1.0 / CW), R=[BK[7]], W=[B_mean])
                                for cc in range(4):
                                    P.op("dve", lambda e, cc=cc: e.tensor_tensor(out=ycv[:, cc, 0:N], in0=ycv[:, cc, 0:N],
                                                                                 in1=mean_sb[:, 0:N], op=ALU.subtract),
                                         R=[B_mean, B_ycv], W=[B_ycv])
                                B_save = B_xT[ti]
                                col_stats(ti, lambda kc: ycv[:, kc, 0:N], 4, 1.0 / CW, Rsrc=[B_ycv])
                                for cc in range(4):
                                    a = nxt("t")
                                    P.op("dve", lambda e, cc=cc, a=a: e.tensor_tensor(
                                        out=tA[a][:, 0:N], in0=ycv[:, cc, 0:N], in1=rsT[:, 0:N], op=ALU.mult),
                                        R=[B_ycv, B_rsT], W=[B_tA[a]])
                                    P.op("act", lambda e, cc=cc, a=a, c0=c0: e.activation(
                                        out=sT[:, cc, c0:c0 + N], in_=tA[a][:, 0:N], func=AF.Silu,
                                        scale=cvp[:, 128 + cc:129 + cc], bias=cvp[:, 132 + cc:133 + cc]),
                                        R=[B_tA[a], B_small], W=[B_sT[ti]])
                                P.barrier()
                            P.barrier()
                        P.barrier()

                    with ExitStack() as sb_:
                        mT = sb("mT", [128, 8, NT], BF16, sb_)
                        B_mT = [Buf() for _ in tiles]
                        with ExitStack() as s3:
                            wga = [sb("wga%d" % i, [128, 8, 256], BF16, s3) for i in range(2)]
                            wgb = [sb("wgb%d" % i, [128, 8, 256], BF16, s3) for i in range(2)]
                            wpa = [sb("wpa%d" % i, [64, 8, 256], BF16, s3) for i in range(2)]
                            wpb = [sb("wpb%d" % i, [128, 4, 256], BF16, s3) for i in range(2)]
                            t1 = sb("t1", [128, 512], F32, s3)
                            t2 = sb("t2", [128, 512], F32, s3)
                            B_ws = [Buf(), Buf()]
                            B_t1, B_t2 = Buf(), Buf()
                            for rr in range(4):
                                wi = rr % 2
                                cs = slice(256 * rr, 256 * (rr + 1))
                                P.dma("pool", lambda e, wi=wi, rr=rr: e.dma_start(
                                    out=wga[wi][:], in_=w_in_v[:, :, GA_OFF + 256 * rr:GA_OFF + 256 * (rr + 1)]), W=[B_ws[wi]])
                                P.dma("pool", lambda e, wi=wi, rr=rr: e.dma_start(
                                    out=wgb[wi][:], in_=w_in_v[:, :, GB_OFF + 256 * rr:GB_OFF + 256 * (rr + 1)]), W=[B_ws[wi]])
                                P.dma("pool", lambda e, wi=wi, cs=cs: e.dma_start(out=wpa[wi][:], in_=w_pa_v[:, :, cs]),
                                      W=[B_ws[wi]])
                                P.dma("pool", lambda e, wi=wi, cs=cs: e.dma_start(out=wpb[wi][:], in_=w_pb_v[:, :, cs]),
                                      W=[B_ws[wi]])
                                for ti, (c0, N, oc0) in enumerate(tiles):
                                    B_o = B_oT[4] if N == 128 else B_oT[oc0 // 512]
                                    for nn in range(2):
                                        n = 2 * rr + nn
                                        ns = slice(128 * nn, 128 * (nn + 1))
                                        r_ = nxt("u")
                                        for kc in range(8):
                                            P.op("pe", lambda e, kc=kc, wi=wi, ns=ns, r_=r_, c0=c0, N=N: e.matmul(
                                                bank(r_)[:, 0:N], lhsT=wga[wi][:, kc, ns], rhs=hTh[:, kc, c0:c0 + N],
                                                start=(kc == 0), stop=(kc == 7)), R=[B_ws[wi], B_hTh[ti]], W=[BK[r_]])
                                        for kc in range(8):
                                            P.op("pe", lambda e, kc=kc, wi=wi, ns=ns, r_=r_, c0=c0, N=N: e.matmul(
                                                bank(2 + r_)[:, 0:N], lhsT=wgb[wi][:, kc, ns], rhs=hTh[:, kc, c0:c0 + N],
                                                start=(kc == 0), stop=(kc == 7)), R=[B_ws[wi], B_hTh[ti]], W=[BK[2 + r_]])
                                        for h_ in range(8):
                                            P.op("pe", lambda e, h_=h_, wi=wi, ns=ns, r_=r_, oc0=oc0, N=N: e.matmul(
                                                bank(4 + r_)[:, 0:N], lhsT=wpa[wi][:, h_, ns], rhs=oT[:, h_, oc0:oc0 + N],
                                                start=(h_ == 0), stop=(h_ == 7)), R=[B_ws[wi], B_o], W=[BK[4 + r_]])
                                        for cc in range(4):
                                            P.op("pe", lambda e, cc=cc, wi=wi, ns=ns, r_=r_, c0=c0, N=N: e.matmul(
                                                bank(6 + r_)[:, 0:N], lhsT=wpb[wi][:, cc, ns], rhs=sT[:, cc, c0:c0 + N],
                                                start=(cc == 0), stop=(cc == 3)), R=[B_ws[wi], B_sT[ti]], W=[BK[6 + r_]])
                                        P.op("act", lambda e, r_=r_, n=n, N=N: e.activation(
                                            out=tA[r_][:, 0:N], in_=bank(r_)[:, 0:N], func=AF.Sigmoid,
                                            bias=bfm[:, 16 + n:17 + n]), R=[BK[r_], B_small], W=[B_tA[r_]])
                                        P.op("act", lambda e, r_=r_, n=n, N=N: e.activation(
                                            out=tB[r_][:, 0:N], in_=bank(2 + r_)[:, 0:N], func=AF.Sigmoid,
                                            bias=bfm[:, 24 + n:25 + n]), R=[BK[2 + r_], B_small], W=[B_tB[r_]])
                                        P.op("dve", lambda e, r_=r_, N=N: e.tensor_tensor(
                                            out=t1[:, 0:N], in0=tA[r_][:, 0:N], in1=bank(4 + r_)[:, 0:N], op=ALU.mult),
                                            R=[B_tA[r_], BK[4 + r_]], W=[B_t1])
                                        P.op("dve", lambda e, r_=r_, n=n, N=N: e.scalar_tensor_tensor(
                                            out=t2[:, 0:N], in0=bank(6 + r_)[:, 0:N], scalar=cvp[:, 136 + n:137 + n],
                                            in1=tB[r_][:, 0:N], op0=ALU.add, op1=ALU.mult),
                                            R=[B_tB[r_], BK[6 + r_], B_small], W=[B_t2])
                                        P.op("dve", lambda e, n=n, c0=c0, N=N: e.tensor_tensor(
                                            out=mT[:, n, c0:c0 + N], in0=t1[:, 0:N], in1=t2[:, 0:N], op=ALU.add),
                                            R=[B_t1, B_t2], W=[B_mT[ti]])
                            P.barrier()
                        with ExitStack() as s4:
                            wo = sb("wo", [128, 8, 1024], BF16, s4)
                            B_wo = Buf()
                            for hh in range(2):
                                P.dma("pool", lambda e, hh=hh: e.dma_start(out=wo[:, :, 512 * hh:512 * (hh + 1)],
                                                                           in_=w_o_v[:, :, 512 * hh:512 * (hh + 1)]), W=[B_wo])
                            for ti, (c0, N, oc0) in enumerate(tiles):
                                for n in range(8):
                                    r_ = nxt("u", 4)
                                    for kc in range(8):
                                        P.op("pe", lambda e, kc=kc, n=n, r_=r_, c0=c0, N=N: e.matmul(
                                            bank(r_)[:, 0:N], lhsT=wo[:, kc, n * 128:(n + 1) * 128], rhs=mT[:, kc, c0:c0 + N],
                                            start=(kc == 0), stop=(kc == 7)), R=[B_wo, B_mT[ti]], W=[BK[r_]])
                                    per_sample(
                                        N,
                                        lambda n=n, r_=r_, c0=c0, ti=ti: P.op("dve", lambda e: e.scalar_tensor_tensor(
                                            out=xT[:, n, c0:c0 + 512], in0=bank(r_)[:, 0:512], scalar=g1(n),
                                            in1=xT[:, n, c0:c0 + 512], op0=ALU.mult, op1=ALU.add),
                                            R=[BK[r_], B_mod, B_xT[ti]], W=[B_xT[ti]]),
                                        lambda s_, o_, n=n, r_=r_, c0=c0, ti=ti: P.op("dve", lambda e: e.scalar_tensor_tensor(
                                            out=xT[:, n, c0 + o_:c0 + o_ + 32], in0=bank(r_)[:, o_:o_ + 32], scalar=g1(n, 1 + s_),
                                            in1=xT[:, n, c0 + o_:c0 + o_ + 32], op0=ALU.mult, op1=ALU.add),
                                            R=[BK[r_], B_mod, B_xT[ti]], W=[B_xT[ti]]))
                            P.barrier()

                    for ti, (c0, N, oc0) in enumerate(tiles):
                        col_stats(ti, lambda kc, c0=c0, N=N: xT[:, kc, c0:c0 + N], 8, 1.0 / D)
                        for kc in range(8):
                            a = nxt("t")
                            P.op("dve", lambda e, kc=kc, a=a, c0=c0, N=N: e.tensor_tensor(
                                out=tA[a][:, 0:N], in0=xT[:, kc, c0:c0 + N], in1=rsT[:, 0:N], op=ALU.mult),
                                R=[B_xT[ti], B_rsT], W=[B_tA[a]])
                            per_sample(
                                N,
                                lambda kc=kc, a=a, c0=c0, ti=ti: P.op("act", lambda e: e.activation(
                                    out=hTh[:, kc, c0:c0 + 512], in_=tA[a][:, 0:512], func=AF.Identity, scale=a2T[:, kc, 0:1],
                                    bias=sh2(kc)), R=[B_tA[a], B_mod], W=[B_hTh[ti]]),
                                lambda s_, o_, kc=kc, a=a, c0=c0, ti=ti: P.op("dve", lambda e: e.tensor_scalar(
                                    out=hTh[:, kc, c0 + o_:c0 + o_ + 32], in0=tA[a][:, o_:o_ + 32],
                                    scalar1=a2T[:, kc, 1 + s_:2 + s_], scalar2=sh2(kc, 1 + s_), op0=ALU.mult, op1=ALU.add),
                                    R=[B_tA[a], B_mod], W=[B_hTh[ti]]))
                    P.barrier()

                    with ExitStack() as s5:
                        wG = [sb("wG%d" % i, [128, 8, 384], BF16, s5) for i in range(2)]
                        wU = [sb("wU%d" % i, [128, 8, 384], BF16, s5) for i in range(2)]
                        wD = [sb("wD%d" % i, [128, 3, 1024], BF16, s5) for i in range(2)]
                        aT = [sb("aT%d" % i, [128, 3, 512], BF16, s5) for i in range(2)]
                        B_wf = [Buf(), Buf()]
                        B_aT = [Buf(), Buf()]
                        for gi, (h0, ng) in enumerate(FFN_GROUPS):
                            wi = gi % 2
                            P.dma("pool", lambda e, wi=wi, h0=h0, ng=ng: e.dma_start(
                                out=wG[wi][:, :, 0:128 * ng], in_=w_f1_v[:, :, 128 * h0:128 * (h0 + ng)]), W=[B_wf[wi]])
                            P.dma("pool", lambda e, wi=wi, h0=h0, ng=ng: e.dma_start(
                                out=wU[wi][:, :, 0:128 * ng], in_=w_f1_v[:, :, FH + 128 * h0:FH + 128 * (h0 + ng)]),
                                W=[B_wf[wi]])
                            P.dma("pool", lambda e, wi=wi, h0=h0, ng=ng: e.dma_start(
                                out=wD[wi][:, 0:ng, :], in_=w_f2_v[:, h0:h0 + ng, :]), W=[B_wf[wi]])
                            for ti, (c0, N, oc0) in enumerate(tiles):
                                ai = nxt("a")
                                for i in range(ng):
                                    r_ = nxt("b")
                                    for kc in range(8):
                                        P.op("pe", lambda e, kc=kc, i=i, wi=wi, r_=r_, c0=c0, N=N: e.matmul(
                                            bank(r_)[:, 0:N], lhsT=wG[wi][:, kc, 128 * i:128 * (i + 1)],
                                            rhs=hTh[:, kc, c0:c0 + N], start=(kc == 0), stop=(kc == 7)),
                                            R=[B_wf[wi], B_hTh[ti]], W=[BK[r_]])
                                    for kc in range(8):
                                        P.op("pe", lambda e, kc=kc, i=i, wi=wi, r_=r_, c0=c0, N=N: e.matmul(
                                            bank(2 + r_)[:, 0:N], lhsT=wU[wi][:, kc, 128 * i:128 * (i + 1)],
                                            rhs=hTh[:, kc, c0:c0 + N], start=(kc == 0), stop=(kc == 7)),
                                            R=[B_wf[wi], B_hTh[ti]], W=[BK[2 + r_]])
                                    P.op("act", lambda e, r_=r_, N=N: e.activation(out=tA[r_][:, 0:N], in_=bank(r_)[:, 0:N],
                                                                                   func=AF.Silu), R=[BK[r_]], W=[B_tA[r_]])
                                    P.op("dve", lambda e, r_=r_, i=i, ai=ai, N=N: e.tensor_tensor(
                                        out=aT[ai][:, i, 0:N], in0=tA[r_][:, 0:N], in1=bank(2 + r_)[:, 0:N], op=ALU.mult),
                                        R=[B_tA[r_], BK[2 + r_]], W=[B_aT[ai]])
                                for n in range(8):
                                    r4 = 4 + nxt("c", 4)
                                    for i in range(ng):
                                        P.op("pe", lambda e, i=i, n=n, wi=wi, ai=ai, r4=r4, N=N, ng=ng: e.matmul(
                                            bank(r4)[:, 0:N], lhsT=wD[wi][:, i, n * 128:(n + 1) * 128], rhs=aT[ai][:, i, 0:N],
                                            start=(i == 0), stop=(i == ng - 1)), R=[B_wf[wi], B_aT[ai]], W=[BK[r4]])
                                    per_sample(
                                        N,
                                        lambda n=n, r4=r4, c0=c0, ti=ti: P.op("dve", lambda e: e.scalar_tensor_tensor(
                                            out=xT[:, n, c0:c0 + 512], in0=bank(r4)[:, 0:512], scalar=g2(n),
                                            in1=xT[:, n, c0:c0 + 512], op0=ALU.mult, op1=ALU.add),
                                            R=[BK[r4], B_mod, B_xT[ti]], W=[B_xT[ti]]),
                                        lambda s_, o_, n=n, r4=r4, c0=c0, ti=ti: P.op("dve", lambda e: e.scalar_tensor_tensor(
                                            out=xT[:, n, c0 + o_:c0 + o_ + 32], in0=bank(r4)[:, o_:o_ + 32], scalar=g2(n, 1 + s_),
                                            in1=xT[:, n, c0 + o_:c0 + o_ + 32], op0=ALU.mult, op1=ALU.add),
                                            R=[BK[r4], B_mod, B_xT[ti]], W=[B_xT[ti]]))
                        P.barrier()

                    with ExitStack() as s6:
                        yst = [sb("yst%d" % i, [128, D], F32, s6) for i in range(2)]
                        B_yst = [Buf(), Buf()]
                        for ti, (c0, N, oc0) in enumerate(tiles):
                            col_stats(ti, lambda kc, c0=c0, N=N: xT[:, kc, c0:c0 + N], 8, 1.0 / D)
                            for kc in range(8):
                                P.op("dve", lambda e, kc=kc, c0=c0, N=N: e.scalar_tensor_tensor(
                                    out=xT[:, kc, c0:c0 + N], in0=xT[:, kc, c0:c0 + N], scalar=gn[:, 16 + kc:17 + kc],
                                    in1=rsT[:, 0:N], op0=ALU.mult, op1=ALU.mult), R=[B_xT[ti], B_rsT, B_small], W=[B_xT[ti]])
                            for bq in range(N // 128):
                                col = c0 + 128 * bq
                                yi = nxt("d")
                                for kc in range(8):
                                    P.op("pe", lambda e, kc=kc, col=col, yi=yi: e.transpose(
                                        psum[:, 1024 * yi + 128 * kc + 2048:1024 * yi + 128 * (kc + 1) + 2048],
                                        xT[:, kc, col:col + 128], ident), R=[B_xT[ti], B_cst], W=[BK[4 + 2 * yi]])
                                P.op("act", lambda e, yi=yi: e.activation(out=yst[yi][:], in_=psum[:, 2048 + 1024 * yi:3072 + 1024 * yi],
                                                                          func=AF.Copy), R=[BK[4 + 2 * yi]], W=[B_yst[yi]])
                                ydst = y_s if N == 128 else y_own[8 * hf + col // 128]
                                P.dma("act", lambda e, yi=yi, ydst=ydst: e.dma_start(out=ydst, in_=yst[yi][:]), R=[B_yst[yi]])
                        P.barrier()

        if STAGE < 3 or DBG_OT:
            P.dma("sp", lambda e: e.dma_start(out=dbg_o, in_=oT[:]), R=B_oT)
        P.finish()
    return nc


_CACHE = {}


def _consts():
    c = np.zeros((128, 5, 128), np.float32)
    c[:, 0, :] = np.eye(128, dtype=np.float32)
    c[:, 1, :] = 1.0
    s = np.arange(128)
    c[:, 2, :] = (s[:, None] <= s[None, :]).astype(np.float32)
    c[:, 3, :] = (s[:, None] > s[None, :]).astype(np.float32)
    c[:, 4, :] = (s[None, :] - s[:, None]).astype(np.float32)
    return c


def kernel(x_prompt, x_sample, c_prompt, c_sample, cache_k, cache_v, cache_logf, state_conv,
           page_table, rms1_g, rms2_g, w_ada, b_ada, w_in, b_in, dw_w, dw_b, ln_g, ln_b,
           w_pa, w_pb, b_pb, w_o, w_ffn_in, w_ffn_out, final_g):
    f32 = np.float32
    A = lambda a: np.ascontiguousarray(np.asarray(a))
    x_prompt, x_sample = A(x_prompt), A(x_sample)
    if "nc" not in _CACHE:
        _CACHE["nc"] = build_program()
    nc = _CACHE["nc"]

    def fm(v, nchunk):
        return A(np.asarray(v, f32).reshape(nchunk, 128).T)

    b_in0 = np.asarray(b_in, f32)[0]
    b_fm = np.concatenate([fm(b_in0[Q_OFF:Q_OFF + 512], 4), fm(b_in0[K_OFF:K_OFF + 512], 4),
                           fm(b_in0[GLU_OFF:GLU_OFF + 1024], 8), fm(b_in0[GA_OFF:GA_OFF + 1024], 8),
                           fm(b_in0[GB_OFF:GB_OFF + 1024], 8)], axis=1)
    b_rows = np.concatenate([b_in0[K_OFF:K_OFF + 512], b_in0[V_OFF:V_OFF + 512], b_in0[F_OFF:F_OFF + 8]])[None, :]
    gains = np.concatenate([fm(np.asarray(rms1_g)[0], 8), fm(np.asarray(rms2_g)[0], 8), fm(np.asarray(final_g), 8)], axis=1)
    dwT = np.asarray(dw_w, f32)[0].T.reshape(4, 128, CK).transpose(1, 0, 2).reshape(128, 4 * CK)
    convpar = np.concatenate([dwT, fm(np.asarray(dw_b)[0], 4), fm(np.asarray(ln_g)[0], 4), fm(np.asarray(ln_b)[0], 4),
                              fm(np.asarray(b_pb)[0], 8)], axis=1)
    b_adaT = fm(np.asarray(b_ada)[0], 48)
    ck = A(cache_k).reshape(NPHYS * 16, 8 * 512)
    cv = A(cache_v).reshape(NPHYS * 16, 8 * 512)
    cl = A(cache_logf).reshape(NPHYS * 16, 64)
    if STAGE < 4:
        ck, cv, cl = A(ck[:16]), A(cv[:16]), A(cl[:16])
    shared = dict(w_ada=A(np.asarray(w_ada)[0]), b_adaT=A(b_adaT), w_in=A(np.asarray(w_in)[0]), b_fm=A(b_fm),
                  b_rows=A(b_rows), gains=A(gains), convpar=A(convpar), w_pa=A(np.asarray(w_pa)[0]),
                  w_pb=A(np.asarray(w_pb)[0]), w_o=A(np.asarray(w_o)[0]), w_f1=A(np.asarray(w_ffn_in)[0]),
                  w_f2=A(np.asarray(w_ffn_out)[0]), cache_k=ck, cache_v=cv, cache_l=cl, cst=_consts(),
                  shi=A((np.arange(128) % 16).astype(f32)[:, None]),
                  bmask=A((np.arange(32)[:, None] // 4 == np.arange(512)[None, :] // 64).astype(f32)))
    pt = np.asarray(page_table)
    sc = np.asarray(state_conv, f32)[0]
    in_maps = []
    for c in range(8):
        b, j = c // 4, c % 4
        ob = own_blocks(j)
        xo = np.zeros((NOWN, 160, D), f32)
        hv = np.zeros((128, NOWN), f32)
        for q, blk in enumerate(ob):
            lo = blk * 128 - 32
            if lo >= 0:
                xo[q] = x_prompt[b, lo:lo + 160]
                hv[:, q] = 1.0
            else:
                xo[q, 32:] = x_prompt[b, 0:128]
        xs = np.zeros((128, D), f32)
        cvec = np.zeros((5, D), f32)
        cvec[0] = np.asarray(c_prompt)[b]
        ptr = np.zeros((128, 32), np.int32)
        stT = np.zeros((128, 4, 4, 30), f32)
        strows = np.zeros((4, 26, CW), f32)
        for s in range(4):
            gs = 4 * c + s
            xs[32 * s:32 * s + 4] = x_sample[gs]
            cvec[1 + s] = np.asarray(c_sample)[gs]
            for g in range(8):
                ptr[:, s * 8 + g] = pt[gs, 8 * g + np.arange(128) // 16]
            stT[:, :, s, :] = sc[gs].T.reshape(4, 128, 30).transpose(1, 0, 2)
            strows[s] = sc[gs, 4:30]
        cT = A(cvec.reshape(5, 8, 128).transpose(2, 1, 0))
        kbo = np.tile((128.0 * (np.arange(4) - j)).astype(f32)[None, :], (128, 1))
        m = dict(shared)
        m.update(xseq=A(x_prompt[b]), xown=xo, xs=xs, cT=cT, pt_rep=ptr, stT=stT, st_rows=strows,
                 kboff=A(kbo), hval=hv)
        in_maps.append(m)

    ncores = _CACHE.get("ncores", 8)
    res = run_bass_kernel_spmd(nc, in_maps[:ncores], core_ids=list(range(ncores)))
    R = list(res.results) + [res.results[0]] * (8 - ncores)
    _CACHE["last"] = R
    y_prompt = np.zeros((2, SEQ, D), f32)
    k_p = np.zeros((2, SEQ, 512), f32)
    v_p = np.zeros((2, SEQ, 512), f32)
    l_p = np.zeros((2, SEQ, 8), f32)
    conv_p = np.zeros((1, 2, 30, CW), f32)
    y_sample = np.zeros((32, 4, D), f32)
    k_s = np.zeros((32, 4, 512), f32)
    v_s = np.zeros((32, 4, 512), f32)
    l_s = np.zeros((32, 4, 8), f32)
    conv_s = np.zeros((1, 32, 30, CW), f32)
    for c in range(8):
        b, j = c // 4, c % 4
        r = R[c]
        for q, blk in enumerate(own_blocks(j)):
            sl = slice(blk * 128, blk * 128 + 128)
            y_prompt[b, sl] = r["y_own"][q]
            k_p[b, sl] = r["k_own"][q]
            v_p[b, sl] = r["v_own"][q]
            l_p[b, sl] = r["lf_own"][q]
        if j == 3:
            conv_p[0, b] = r["convp"][2:32]
        for s in range(4):
            gs = 4 * c + s
            y_sample[gs] = r["y_s"][32 * s:32 * s + 4]
            k_s[gs] = r["ks"][32 * s:32 * s + 4]
            v_s[gs] = r["vs"][32 * s:32 * s + 4]
            l_s[gs] = r["lfs"][32 * s:32 * s + 4]
            conv_s[0, gs] = r["convs"][s]
    return (y_prompt, y_sample,
            k_p.reshape(1, 2, NB, 128, H, DH), v_p.reshape(1, 2, NB, 128, H, DH), l_p.reshape(1, 2, NB, 128, H),
            conv_p, k_s.reshape(1, 32, 4, H, DH), v_s.reshape(1, 32, 4, H, DH), l_s.reshape(1, 32, 4, H), conv_s)
```

```python
from contextlib import ExitStack

import numpy as np
import concourse.bass as bass
import concourse.mybir as mybir
from concourse.bass_utils import run_bass_kernel_spmd

F32 = mybir.dt.float32
BF16 = mybir.dt.bfloat16
I32 = mybir.dt.int32
AF = mybir.ActivationFunctionType
ALU = mybir.AluOpType

D = 1024
SEQ = 8192
NB = SEQ // 128
H = 8
DH = 64
NIN = 4616
Q_OFF, K_OFF, V_OFF, F_OFF, GLU_OFF, GA_OFF, GB_OFF = 0, 512, 1024, 1536, 1544, 2568, 3592
CW = 512
CK = 31
FH = 2816
EPS = 1e-6
SCALE = DH ** -0.5
BIG = 30000.0
NPHYS = 2560
NOWN = 16
ENG = ("pe", "act", "dve", "pool", "sp")

STAGE = 4
DBG_OT = False
SKIP_P5 = False
SKIP_ATT = False
SDBG = 0


class _Stop(Exception):
    pass


class Buf:
    __slots__ = ("w", "r", "sem", "semv", "name")

    def __init__(self, name=""):
        self.w = None
        self.r = []
        self.sem = None
        self.semv = 0
        self.name = name


class _Rec:
    def __init__(self):
        self.call = None

    def __getattr__(self, name):
        def f(*a, **k):
            self.call = (name, a, k)
            return self
        return f


def _record(fn):
    r = _Rec()
    fn(r)
    assert r.call is not None
    return r.call


class Prog:
    def __init__(self, nc, es):
        self.nc = nc
        self.es = es
        self.items = {e: [] for e in ENG}
        self.cnt = {e: 0 for e in ENG}
        self.sems = {e: es.enter_context(nc.semaphore("c_" + e)) for e in ENG}
        self.known = {e: {} for e in ENG}
        self.pending = {e: [] for e in ENG}
        self.dma_bufs = []
        self.nsem = 0

    def _deps(self, R, W):
        deps = []
        for b in R:
            if b.w is not None:
                deps.append(b.w)
        for b in W:
            if b.w is not None:
                deps.append(b.w)
            deps.extend(b.r)
        return deps

    def _waits(self, eng, deps):
        deps = list(deps) + self.pending[eng]
        self.pending[eng] = []
        best = {}
        for (sem, val, src) in deps:
            if eng == "pe" and src == "pe":
                continue
            k = id(sem)
            if self.known[eng].get(k, 0) >= val:
                continue
            if k not in best or best[k][1] < val:
                best[k] = (sem, val)
        for k, (sem, val) in best.items():
            self.known[eng][k] = val
        return list(best.values())

    def op(self, eng, fn, R=(), W=()):
        waits = self._waits(eng, self._deps(R, W))
        self.cnt[eng] += 1
        ev = (self.sems[eng], self.cnt[eng], eng)
        for b in R:
            b.r.append(ev)
        for b in W:
            b.w = ev
            b.r = []
        self.items[eng].append((waits, _record(fn), self.sems[eng], 1))

    def dma(self, q, fn, R=(), W=(), sb=None):
        waits = self._waits(q, self._deps(R, W))
        if sb is None:
            sb = W[0] if W else R[0]
        if sb.sem is None:
            sb.sem = self.es.enter_context(self.nc.semaphore("d%d" % self.nsem))
            self.nsem += 1
            self.dma_bufs.append(sb)
        sb.semv += 16
        ev = (sb.sem, sb.semv, "dma")
        for b in R:
            b.r.append(ev)
        for b in W:
            b.w = ev
            b.r = []
        self.items[q].append((waits, _record(fn), sb.sem, 16))

    def barrier(self):
        evs = [(self.sems[e], self.cnt[e], e) for e in ENG if self.cnt[e] > 0]
        evs += [(b.sem, b.semv, "dma") for b in self.dma_bufs]
        for e in ENG:
            self.pending[e] = list(evs)

    def finish(self):
        self.barrier()
        nc = self.nc
        with nc.Block() as block:
            def run(eng, items, final):
                for waits, (name, a, k), sem, inc in items:
                    for (s, v) in waits:
                        eng.wait_ge(s, v)
                    getattr(eng, name)(*a, **k).then_inc(sem, inc)
                for (s, v, _) in final:
                    eng.wait_ge(s, v)

            fin = self.pending["sp"]
            block.tensor(lambda e: run(e, self.items["pe"], []))
            block.scalar(lambda e: run(e, self.items["act"], []))
            block.vector(lambda e: run(e, self.items["dve"], []))
            block.gpsimd(lambda e: run(e, self.items["pool"], []))
            block.sync(lambda e: run(e, self.items["sp"], fin))


def own_blocks(j):
    return [16 * m + 4 * i + j for m in range(4) for i in range(4)]


FFN_GROUPS = []
_h = 0
for _n in (3, 3, 3, 3, 3, 3, 2, 2):
    FFN_GROUPS.append((_h, _n))
    _h += _n
QBASE = [0, 16, 48, 96]


def build_program():
    nc = bass.Bass("TRN2", target_bir_lowering=False)

    def din(name, shape, dt=F32):
        return nc.dram_tensor(name, list(shape), dt, kind="ExternalInput").ap()

    def dout(name, shape, dt=F32):
        return nc.dram_tensor(name, list(shape), dt, kind="ExternalOutput").ap()

    xseq = din("xseq", [SEQ, D])
    xown = din("xown", [NOWN, 160, D])
    xs_d = din("xs", [128, D])
    cT_d = din("cT", [128, 8, 5])
    w_ada = din("w_ada", [D, 6 * D])
    b_adaT = din("b_adaT", [128, 48])
    w_in = din("w_in", [D, NIN])
    b_fm = din("b_fm", [128, 32])
    b_rows = din("b_rows", [1, 1032])
    gains = din("gains", [128, 24])
    convp_d = din("convpar", [128, 4 * CK + 20])
    w_pa = din("w_pa", [512, D])
    w_pb = din("w_pb", [512, D])
    w_o = din("w_o", [D, D])
    w_f1 = din("w_f1", [D, 2 * FH])
    w_f2 = din("w_f2", [FH, D])
    NCR = NPHYS * 16 if STAGE >= 4 else 16
    cache_k = din("cache_k", [NCR, 8 * 512])
    cache_v = din("cache_v", [NCR, 8 * 512])
    cache_l = din("cache_l", [NCR, 64])
    pt_rep = din("pt_rep", [128, 32], I32)
    shi_d = din("shi", [128, 1])
    stT_d = din("stT", [128, 4, 4, 30])
    st_rows = din("st_rows", [4, 26, CW])
    kboff_d = din("kboff", [128, 4])
    hval_d = din("hval", [128, NOWN])
    bmask_d = din("bmask", [32, 512])
    cst_d = din("cst", [128, 5, 128])

    y_own = dout("y_own", [NOWN, 128, D])
    y_s = dout("y_s", [128, D])
    k_own = dout("k_own", [NOWN, 128, 512])
    v_own = dout("v_own", [NOWN, 128, 512])
    lf_own = dout("lf_own", [NOWN, 128, 8])
    convp_o = dout("convp", [32, CW])
    ks_o = dout("ks", [128, 512])
    vs_o = dout("vs", [128, 512])
    lfs_o = dout("lfs", [128, 8])
    convs_o = dout("convs", [4, 30, CW])
    dbg_o = dout("dbg", [64, 8, 2048 + 128], BF16) if (DBG_OT or STAGE < 3) else None
    dbg2 = dout("dbg2", [128, 2048], F32) if DBG_OT else None

    es = ExitStack()
    with es:
        P = Prog(nc, es)

        _uid = [0]

        def sb(name, shape, dt, stack=es):
            _uid[0] += 1
            return stack.enter_context(nc.sbuf_tensor("%s_s%d" % (name, _uid[0]), list(shape), dt))

        psum = es.enter_context(nc.psum_tensor("psum", [128, 8 * 512], F32))
        psum_bf = psum[:, :].bitcast(BF16)

        def bank(i, n=512, off=0):
            return psum[:, i * 512 + off: i * 512 + off + n]

        cst = sb("cst", [128, 5, 128], F32)
        identb = sb("identb", [128, 128], BF16)
        modT = sb("modT", [128, 48, 5], F32)
        a1T = sb("a1T", [128, 8, 5], F32)
        a2T = sb("a2T", [128, 8, 5], F32)
        bfm = sb("bfm", [128, 32], F32)
        gn = sb("gn", [128, 24], F32)
        cvp = sb("cvp", [128, 4 * CK + 20], F32)
        kboff = sb("kboff_t", [128, 4], F32)
        hval = sb("hval_t", [128, NOWN], F32)
        maskT = sb("maskT", [128, 4, 128], BF16)
        oT = sb("oT", [64, 8, 2048 + 128], BF16)
        KTs = sb("KTs", [128, 4, 128], BF16)
        QTs = sb("QTs", [128, 4, 128], BF16)
        Vs_bf = sb("Vs_bf", [128, 512], BF16)
        lfs_t = sb("lfs_t", [128, 8], F32)
        B_cst = Buf("cst")
        B_mod = Buf("mod")
        B_small = Buf("small")
        B_mask = Buf("mask")
        B_oT = [Buf("oT%d" % i) for i in range(5)]
        B_KTs = Buf("KTs")
        B_QTs = Buf("QTs")
        B_Vs = Buf("Vs")
        B_lfs = Buf("lfs")

        ident = cst[:, 0, :]
        ones = cst[:, 1, :]
        U_incl = cst[:, 2, :]
        U_strict = cst[:, 3, :]
        tmins = cst[:, 4, :]

        P.dma("sp", lambda e: e.dma_start(out=cst[:], in_=cst_d), W=[B_cst])
        for (t, d_) in ((bfm, b_fm), (gn, gains), (cvp, convp_d), (kboff, kboff_d), (hval, hval_d)):
            P.dma("sp", lambda e, t=t, d_=d_: e.dma_start(out=t[:], in_=d_), W=[B_small])
        P.op("dve", lambda e: e.tensor_copy(out=identb[:], in_=ident), R=[B_cst], W=[B_small])
        for r in range(4):
            P.op("dve", lambda e, r=r: e.tensor_scalar(out=maskT[:, r, :], in0=tmins, scalar1=kboff[:, r:r + 1],
                                                       scalar2=0.0, op0=ALU.subtract, op1=ALU.is_ge),
                 R=[B_cst, B_small], W=[B_mask])
        P.op("dve", lambda e: e.tensor_scalar(out=maskT[:], in0=maskT[:], scalar1=-1.0, scalar2=BIG,
                                              op0=ALU.add, op1=ALU.mult), R=[B_mask], W=[B_mask])

        with ExitStack() as s0:
            cTf = sb("cTf", [128, 8, 5], F32, s0)
            cTb = sb("cTb", [128, 8, 5], BF16, s0)
            badaT = sb("badaT", [128, 48], F32, s0)
            wada = [sb("wada%d" % i, [128, 8, 1024], BF16, s0) for i in range(2)]
            B_c = Buf()
            B_wada = [Buf(), Buf()]
            B_ps0 = Buf()
            P.dma("sp", lambda e: e.dma_start(out=cTf[:], in_=cT_d), W=[B_c])
            P.dma("sp", lambda e: e.dma_start(out=badaT[:], in_=b_adaT), W=[B_c])
            P.op("act", lambda e: e.activation(out=cTb[:], in_=cTf[:], func=AF.Silu), R=[B_c], W=[B_c])
            wada_v = w_ada.rearrange("(kc p) n -> p kc n", p=128)
            for pc in range(6):
                sl = pc % 2
                P.dma("pool", lambda e, pc=pc, sl=sl: e.dma_start(
                    out=wada[sl][:], in_=wada_v[:, :, pc * 1024:(pc + 1) * 1024]), W=[B_wada[sl]])
                for nch in range(8):
                    ch = pc * 8 + nch
                    for kc in range(8):
                        P.op("pe", lambda e, sl=sl, nch=nch, kc=kc, ch=ch: e.matmul(
                            bank(0, 5, ch * 5), lhsT=wada[sl][:, kc, nch * 128:(nch + 1) * 128], rhs=cTb[:, kc, :],
                            start=(kc == 0), stop=(kc == 7)), R=[B_wada[sl], B_c], W=[B_ps0])
            P.op("dve", lambda e: e.tensor_tensor(
                out=modT[:], in0=bank(0, 240).rearrange("p (c v) -> p c v", v=5),
                in1=badaT[:, :].unsqueeze(2).to_broadcast([128, 48, 5]), op=ALU.add), R=[B_ps0, B_c], W=[B_mod])
            P.op("dve", lambda e: e.scalar_tensor_tensor(
                out=a1T[:], in0=modT[:, 8:16, :], scalar=1.0, in1=gn[:, 0:8].unsqueeze(2).to_broadcast([128, 8, 5]),
                op0=ALU.add, op1=ALU.mult), R=[B_mod, B_small], W=[B_mod])
            P.op("dve", lambda e: e.scalar_tensor_tensor(
                out=a2T[:], in0=modT[:, 32:40, :], scalar=1.0, in1=gn[:, 8:16].unsqueeze(2).to_broadcast([128, 8, 5]),
                op0=ALU.add, op1=ALU.mult), R=[B_mod, B_small], W=[B_mod])
            P.barrier()

        def sh1(kc, v=0):
            return modT[:, 0 + kc, v:v + 1]

        def g1(kc, v=0):
            return modT[:, 16 + kc, v:v + 1]

        def sh2(kc, v=0):
            return modT[:, 24 + kc, v:v + 1]

        def g2(kc, v=0):
            return modT[:, 40 + kc, v:v + 1]

        w_in_v = w_in.rearrange("(kc p) n -> p kc n", p=128)

        class NormPipe:
            def __init__(self, stack, nx=3):
                self.nx = nx
                self.xsl = [sb("xsl%d" % i, [128, D], F32, stack) for i in range(nx)]
                self.xn = [sb("xn%d" % i, [128, D], BF16, stack) for i in range(2)]
                self.junk = sb("junk", [128, D], BF16, stack)
                self.ssq = sb("ssq", [128, 4], F32, stack)
                self.rs = sb("rs_t", [128, 4], F32, stack)
                self.B_x = [Buf() for _ in range(nx)]
                self.B_xn = [Buf() for _ in range(2)]
                self.B_junk = Buf()
                self.B_ss = [Buf() for _ in range(4)]
                self.B_rs = [Buf() for _ in range(4)]
                self.TPs = [psum_bf[:, 2048 * i:2048 * (i + 1)].rearrange("p (k t) -> p k t", t=256) for i in range(2)]
                self.B_TP = [Buf() for _ in range(2)]
                self.c = dict(x=0, xn=0, ss=0, tp=0)

            def load(self, src_ap, nrows):
                xi = self.c["x"] % self.nx
                self.c["x"] += 1
                P.dma("sp", lambda e: e.dma_start(out=self.xsl[xi][0:nrows, :], in_=src_ap), W=[self.B_x[xi]])
                return xi

            def rstd(self, xi, nrows, rstd_ap=None, B_rs=None):
                si = self.c["ss"] % 4
                self.c["ss"] += 1
                if rstd_ap is None:
                    rstd_ap, B_rs = self.rs[0:nrows, si:si + 1], self.B_rs[si]
                P.op("act", lambda e: e.activation(out=self.junk[0:nrows, :], in_=self.xsl[xi][0:nrows, :],
                                                   func=AF.Square, accum_out=self.ssq[0:nrows, si:si + 1]),
                     R=[self.B_x[xi]], W=[self.B_junk, self.B_ss[si]])
                P.op("act", lambda e: e.activation(out=self.ssq[0:nrows, si:si + 1], in_=self.ssq[0:nrows, si:si + 1],
                                                   func=AF.Sqrt, scale=1.0 / D, bias=EPS),
                     R=[self.B_ss[si]], W=[self.B_ss[si]])
                P.op("dve", lambda e: e.reciprocal(out=rstd_ap, in_=self.ssq[0:nrows, si:si + 1]),
                     R=[self.B_ss[si]], W=[B_rs])
                return rstd_ap, B_rs

            def normalize(self, xi, nrows, rstd_ap, B_rs):
                ni = self.c["xn"] % 2
                self.c["xn"] += 1
                P.op("act", lambda e: e.activation(out=self.xn[ni][0:nrows, :], in_=self.xsl[xi][0:nrows, :],
                                                   func=AF.Copy, scale=rstd_ap),
                     R=[self.B_x[xi], B_rs], W=[self.B_xn[ni]])
                return ni

            def new_tp(self):
                ti = self.c["tp"] % 2
                self.c["tp"] += 1
                return ti

            def transpose(self, ni, nrows, ti, col0):
                for kc in range(8):
                    P.op("pe", lambda e, kc=kc: e.transpose(self.TPs[ti][:, kc, col0:col0 + nrows],
                                                            self.xn[ni][0:nrows, kc * 128:(kc + 1) * 128],
                                                            identb[0:nrows, 0:nrows]),
                         R=[self.B_xn[ni], B_small], W=[self.B_TP[ti]])

            def evac_mod(self, ti, ncols, dst, B_dst, aT, shf, sample=False):
                if not sample:
                    for kc in range(8):
                        if kc % 2 == 0:
                            P.op("dve", lambda e, kc=kc: e.tensor_scalar(
                                out=dst(kc), in0=self.TPs[ti][:, kc, 0:ncols], scalar1=aT[:, kc, 0:1],
                                scalar2=shf(kc), op0=ALU.mult, op1=ALU.add), R=[self.B_TP[ti], B_mod], W=[B_dst])
                        else:
                            P.op("act", lambda e, kc=kc: e.activation(
                                out=dst(kc), in_=self.TPs[ti][:, kc, 0:ncols], func=AF.Identity,
                                scale=aT[:, kc, 0:1], bias=shf(kc)), R=[self.B_TP[ti], B_mod], W=[B_dst])
                else:
                    for kc in range(8):
                        for s in range(4):
                            c0 = 32 * s
                            P.op("dve", lambda e, kc=kc, s=s, c0=c0: e.tensor_scalar(
                                out=dst(kc)[:, c0:c0 + 32], in0=self.TPs[ti][:, kc, c0:c0 + 32],
                                scalar1=aT[:, kc, 1 + s:2 + s], scalar2=shf(kc, 1 + s), op0=ALU.mult, op1=ALU.add),
                                R=[self.B_TP[ti], B_mod], W=[B_dst])

        with ExitStack() as sA:
            KT = sb("KT", [128, 2, SEQ], BF16, sA)
            Vaug = sb("Vaug", [128, NB, 4, 65], BF16, sA)
            QT = sb("QT", [128, 2, 2048], BF16, sA)
            brow_bc = sb("brow_bc", [128, 1032], F32, sA)
            zf_all = sb("zf_all", [128, NB, 8], F32, sA)
            rstd_all = sb("rstd_all", [128, NB], F32, sA)
            rstd_o = sb("rstd_o", [128, NOWN + 1], F32, sA)
            bias_all = sb("bias_all", [128, 160, 8], F32, sA)
            zf_own = sb("zf_own", [128, NOWN + 1, 8], F32, sA)
            Wc = sb("Wc", [128, NB, 8], F32, sA)
            Tex = sb("Tex", [128, NB + 1, 8], F32, sA)
            onesrow = sb("onesrow", [128, NB], F32, sA)
            B_KT = [Buf() for _ in range(NB // 2)]
            B_V = [Buf() for _ in range(NB)]
            B_QT = [Buf() for _ in range(NOWN)]
            B_brow = Buf()
            B_zf = Buf()
            B_rstd = [Buf() for _ in range(NB)]
            B_rso = [Buf() for _ in range(NOWN + 1)]
            B_bias = Buf()
            B_zfo = Buf()
            B_lf = Buf()

            P.op("pool", lambda e: e.memset(Vaug[:, :, :, 64:65], 1.0), W=B_V)
            P.op("pool", lambda e: e.memset(onesrow[:], 1.0), W=[B_lf])
            P.op("pool", lambda e: e.memset(Tex[:, 0, :], 0.0), W=[B_lf])
            with ExitStack() as sb0:
                brow = sb("brow", [1, 1032], F32, sb0)
                B_br = Buf()
                B_pb = Buf()
                P.dma("sp", lambda e: e.dma_start(out=brow[:], in_=b_rows), W=[B_br])
                for (o, n) in ((0, 512), (512, 512), (1024, 8)):
                    P.op("pe", lambda e, o=o, n=n: e.matmul(bank(1, n), lhsT=ones[0:1, :], rhs=brow[0:1, o:o + n],
                                                            start=True, stop=True), R=[B_cst, B_br], W=[B_pb])
                    P.op("act", lambda e, o=o, n=n: e.activation(out=brow_bc[:, o:o + n], in_=bank(1, n), func=AF.Copy),
                         R=[B_pb], W=[B_brow])
                P.barrier()

            for g in range(2):
                with ExitStack() as s1:
                    NP = NormPipe(s1)
                    wqkv = sb("wqkv", [128, 8, 776], BF16, s1)
                    hT = [sb("hT%d" % i, [128, 8, 256], BF16, s1) for i in range(2)]
                    kst = [sb("kst%d" % i, [128, 256], F32, s1) for i in range(2)]
                    vst = [sb("vst%d" % i, [128, 256], F32, s1) for i in range(2)]
                    B_w = Buf()
                    B_hT = [Buf() for _ in range(2)]
                    B_pk = [Buf() for _ in range(1)]
                    B_pv = [Buf() for _ in range(3)]
                    B_kst = [Buf() for _ in range(2)]
                    B_vst = [Buf() for _ in range(2)]
                    c1 = dict(ht=0, pk=0, pv=0, kst=0, vst=0)
                    for (o, src, n) in ((0, Q_OFF + 256 * g, 256), (256, K_OFF + 256 * g, 256),
                                        (512, V_OFF + 256 * g, 256), (768, F_OFF, 8)):
                        P.dma("pool", lambda e, o=o, src=src, n=n: e.dma_start(out=wqkv[:, :, o:o + n],
                                                                               in_=w_in_v[:, :, src:src + n]), W=[B_w])
                    nv = 264 if g == 0 else 256
                    for bt in range(NB // 2):
                        ti = NP.new_tp()
                        for bb in range(2):
                            blk = 2 * bt + bb
                            xi = NP.load(xseq[blk * 128:(blk + 1) * 128, :], 128)
                            if g == 0:
                                NP.rstd(xi, 128, rstd_all[:, blk:blk + 1], B_rstd[blk])
                            ni = NP.normalize(xi, 128, rstd_all[:, blk:blk + 1], B_rstd[blk])
                            NP.transpose(ni, 128, ti, bb * 128)
                        hi = c1["ht"] % 2
                        c1["ht"] += 1
                        NP.evac_mod(ti, 256, lambda kc, hi=hi: hT[hi][:, kc, 0:256], B_hT[hi], a1T, sh1)
                        pk = 0
                        c1["pk"] += 1
                        for pl in range(2):
                            for kc in range(8):
                                P.op("pe", lambda e, pl=pl, kc=kc, pk=pk, hi=hi: e.matmul(
                                    bank(4 + pk, 256, pl * 256), lhsT=wqkv[:, kc, 256 + pl * 128:256 + (pl + 1) * 128],
                                    rhs=hT[hi][:, kc, 0:256], start=(kc == 0), stop=(kc == 7)),
                                    R=[B_w, B_hT[hi]], W=[B_pk[pk]])
                        for pl in range(2):
                            P.op("dve", lambda e, pl=pl, pk=pk, bt=bt, g=g: e.tensor_scalar(
                                out=KT[:, pl, bt * 256:(bt + 1) * 256], in0=bank(4 + pk, 256, pl * 256),
                                scalar1=bfm[:, 4 + 2 * g + pl:5 + 2 * g + pl], scalar2=None, op0=ALU.add),
                                R=[B_pk[pk], B_small], W=[B_KT[bt]])
                        for bb in range(2):
                            blk = 2 * bt + bb
                            pv = c1["pv"] % 3
                            c1["pv"] += 1
                            for kc in range(8):
                                P.op("pe", lambda e, bb=bb, kc=kc, pv=pv, hi=hi, nv=nv: e.matmul(
                                    bank(5 + pv, nv, 0), lhsT=hT[hi][:, kc, bb * 128:(bb + 1) * 128],
                                    rhs=wqkv[:, kc, 512:512 + nv], start=(kc == 0), stop=(kc == 7)),
                                    R=[B_w, B_hT[hi]], W=[B_pv[pv]])
                            P.op("dve", lambda e, pv=pv, blk=blk, g=g: e.tensor_tensor(
                                out=Vaug[:, blk, :, 0:64],
                                in0=bank(5 + pv, 256, 0).rearrange("p (h d) -> p h d", d=64),
                                in1=brow_bc[:, 512 + 256 * g:512 + 256 * (g + 1)].rearrange("p (h d) -> p h d", d=64),
                                op=ALU.add), R=[B_pv[pv], B_brow], W=[B_V[blk]])
                            if g == 0:
                                P.op("dve", lambda e, pv=pv, blk=blk: e.tensor_tensor(
                                    out=zf_all[:, blk, :], in0=bank(5 + pv, 8, 256),
                                    in1=brow_bc[:, 1024:1032], op=ALU.add), R=[B_pv[pv], B_brow], W=[B_zf])

                    for q in range(NOWN + 1):
                        sample = (q == NOWN)
                        ti = NP.new_tp()
                        xi = NP.load(xs_d if sample else xown[q, 32:160, :], 128)
                        if g == 0:
                            NP.rstd(xi, 128, rstd_o[:, q:q + 1], B_rso[q])
                        ni = NP.normalize(xi, 128, rstd_o[:, q:q + 1], B_rso[q])
                        NP.transpose(ni, 128, ti, 0)
                        hi = c1["ht"] % 2
                        c1["ht"] += 1
                        NP.evac_mod(ti, 128, lambda kc, hi=hi: hT[hi][:, kc, 0:128], B_hT[hi], a1T, sh1, sample=sample)
                        pk = 0
                        c1["pk"] += 1
                        for pl in range(2):
                            for kc in range(8):
                                P.op("pe", lambda e, pl=pl, kc=kc, pk=pk, hi=hi: e.matmul(
                                    bank(4 + pk, 128, pl * 128), lhsT=wqkv[:, kc, pl * 128:(pl + 1) * 128],
                                    rhs=hT[hi][:, kc, 0:128], start=(kc == 0), stop=(kc == 7)),
                                    R=[B_w, B_hT[hi]], W=[B_pk[pk]])
                        if sample:
                            for pl in range(2):
                                for kc in range(8):
                                    P.op("pe", lambda e, pl=pl, kc=kc, pk=pk, hi=hi: e.matmul(
                                        bank(4 + pk, 128, 256 + pl * 128),
                                        lhsT=wqkv[:, kc, 256 + pl * 128:256 + (pl + 1) * 128],
                                        rhs=hT[hi][:, kc, 0:128], start=(kc == 0), stop=(kc == 7)),
                                        R=[B_w, B_hT[hi]], W=[B_pk[pk]])
                        for pl in range(2):
                            qdst = QTs[:, 2 * g + pl, :] if sample else QT[:, pl, q * 128:(q + 1) * 128]
                            P.op("dve", lambda e, pl=pl, pk=pk, qdst=qdst, g=g: e.tensor_scalar(
                                out=qdst, in0=bank(4 + pk, 128, pl * 128),
                                scalar1=bfm[:, 2 * g + pl:2 * g + pl + 1], scalar2=None, op0=ALU.add),
                                R=[B_pk[pk], B_small], W=[B_QTs if sample else B_QT[q]])
                            if sample:
                                P.op("dve", lambda e, pl=pl, pk=pk, g=g: e.tensor_scalar(
                                    out=KTs[:, 2 * g + pl, :], in0=bank(4 + pk, 128, 256 + pl * 128),
                                    scalar1=bfm[:, 4 + 2 * g + pl:5 + 2 * g + pl], scalar2=None, op0=ALU.add),
                                    R=[B_pk[pk], B_small], W=[B_KTs])
                        pv = c1["pv"] % 3
                        c1["pv"] += 1
                        for kc in range(8):
                            P.op("pe", lambda e, kc=kc, pv=pv, hi=hi: e.matmul(
                                bank(5 + pv, 256, 0), lhsT=hT[hi][:, kc, 0:128], rhs=wqkv[:, kc, 256:512],
                                start=(kc == 0), stop=(kc == 7)), R=[B_w, B_hT[hi]], W=[B_pv[pv]])
                        ks_i = c1["kst"] % 2
                        c1["kst"] += 1
                        P.op("dve", lambda e, pv=pv, ks_i=ks_i, g=g: e.tensor_tensor(
                            out=kst[ks_i][:], in0=bank(5 + pv, 256, 0), in1=brow_bc[:, 256 * g:256 * (g + 1)],
                            op=ALU.add), R=[B_pv[pv], B_brow], W=[B_kst[ks_i]])
                        kdst = ks_o[:, 256 * g:256 * (g + 1)] if sample else k_own[q, :, 256 * g:256 * (g + 1)]
                        P.dma("act", lambda e, ks_i=ks_i, kdst=kdst: e.dma_start(out=kdst, in_=kst[ks_i][:]),
                              R=[B_kst[ks_i]])
                        pv = c1["pv"] % 3
                        c1["pv"] += 1
                        for kc in range(8):
                            P.op("pe", lambda e, kc=kc, pv=pv, hi=hi, nv=nv: e.matmul(
                                bank(5 + pv, nv, 0), lhsT=hT[hi][:, kc, 0:128], rhs=wqkv[:, kc, 512:512 + nv],
                                start=(kc == 0), stop=(kc == 7)), R=[B_w, B_hT[hi]], W=[B_pv[pv]])
                        vs_i = c1["vst"] % 2
                        c1["vst"] += 1
                        P.op("dve", lambda e, pv=pv, vs_i=vs_i, g=g: e.tensor_tensor(
                            out=vst[vs_i][:], in0=bank(5 + pv, 256, 0),
                            in1=brow_bc[:, 512 + 256 * g:512 + 256 * (g + 1)], op=ALU.add),
                            R=[B_pv[pv], B_brow], W=[B_vst[vs_i]])
                        if sample:
                            P.op("pool", lambda e, vs_i=vs_i, g=g: e.tensor_copy(
                                out=Vs_bf[:, 256 * g:256 * (g + 1)], in_=vst[vs_i][:]), R=[B_vst[vs_i]], W=[B_Vs])
                        if g == 0:
                            P.op("dve", lambda e, pv=pv, q=q: e.tensor_tensor(
                                out=zf_own[:, q, :], in0=bank(5 + pv, 8, 256), in1=brow_bc[:, 1024:1032], op=ALU.add),
                                R=[B_pv[pv], B_brow], W=[B_zfo])
                        vdst = vs_o[:, 256 * g:256 * (g + 1)] if sample else v_own[q, :, 256 * g:256 * (g + 1)]
                        P.dma("act", lambda e, vs_i=vs_i, vdst=vdst: e.dma_start(out=vdst, in_=vst[vs_i][:]),
                              R=[B_vst[vs_i]])
                    P.barrier()

                if g == 0:
                    for (zt, Bz) in ((zf_all, B_zf), (zf_own, B_zfo)):
                        P.op("act", lambda e, zt=zt: e.activation(out=zt[:], in_=zt[:], func=AF.Exp, scale=-1.0),
                             R=[Bz], W=[Bz])
                        P.op("act", lambda e, zt=zt: e.activation(out=zt[:], in_=zt[:], func=AF.Ln, bias=1.0),
                             R=[Bz], W=[Bz])
                        P.op("dve", lambda e, zt=zt: e.tensor_scalar(out=zt[:], in0=zt[:], scalar1=-1.0, scalar2=None,
                                                                     op0=ALU.mult), R=[Bz], W=[Bz])
                    P.dma("act", lambda e: e.dma_start(out=lf_own.rearrange("q p h -> p q h"),
                                                       in_=zf_own[:, 0:NOWN, :]), R=[B_zfo])
                    P.dma("act", lambda e: e.dma_start(out=lfs_o, in_=zf_own[:, NOWN, :]), R=[B_zfo])
                    P.op("dve", lambda e: e.tensor_copy(out=lfs_t[:], in_=zf_own[:, NOWN, :]), R=[B_zfo], W=[B_lfs])
                    B_pc = Buf()
                    lf_flat = zf_all[:].rearrange("p b h -> p (b h)")
                    P.op("pe", lambda e: e.matmul(bank(0), lhsT=U_incl, rhs=lf_flat, start=True, stop=True),
                         R=[B_cst, B_zf], W=[B_pc])
                    P.op("pe", lambda e: e.matmul(bank(1), lhsT=ones, rhs=lf_flat, start=True, stop=True),
                         R=[B_cst, B_zf], W=[B_pc])
                    P.op("act", lambda e: e.activation(out=Wc[:].rearrange("p b h -> p (b h)"), in_=bank(0),
                                                       func=AF.Copy), R=[B_pc], W=[B_lf])
                    for h in range(H):
                        P.op("dve", lambda e, h=h: e.tensor_tensor_scan(
                            out=Tex[:, 1:NB + 1, h], data0=onesrow[:, :],
                            data1=bank(1).rearrange("p (b h) -> p b h", h=8)[:, :, h], initial=0.0,
                            op0=ALU.mult, op1=ALU.add), R=[B_pc, B_lf], W=[B_lf])
                    for m in range(4):
                        nk = 16 * (m + 1)
                        dstb = bias_all[:, QBASE[m]:QBASE[m] + nk, :]
                        P.op("dve", lambda e, nk=nk, dstb=dstb: e.tensor_tensor(
                            out=dstb, in0=Tex[:, 0:nk, :], in1=Wc[:, 0:nk, :], op=ALU.add), R=[B_lf], W=[B_bias])
                        P.op("dve", lambda e, nk=nk, dstb=dstb, m=m: e.scalar_tensor_tensor(
                            out=dstb, in0=dstb, scalar=-1.0, in1=Tex[:, 16 * m:16 * m + 1, :].to_broadcast([128, nk, 8]),
                            op0=ALU.mult, op1=ALU.add), R=[B_lf, B_bias], W=[B_bias])
                    P.barrier()

                if STAGE < 2 or SKIP_ATT:
                    continue
                with ExitStack() as s2:
                    PT = [sb("PT%d" % i, [128, 512], BF16, s2) for i in range(4)]
                    Osb = [sb("Osb%d" % i, [128, 512], F32, s2) for i in range(2)]
                    rden = sb("rden", [128, 512], F32, s2)
                    B_PT = [Buf() for _ in range(4)]
                    B_S = [Buf() for _ in range(4)]
                    B_O = [Buf() for _ in range(2)]
                    B_Osb = [Buf() for _ in range(2)]
                    B_rden = Buf()
                    B_BC = Buf()
                    c2 = dict(s=0, o=0)
                    for m in range(4):
                        for hl in range(4):
                            pl, e2 = hl // 2, hl % 2
                            h = 4 * g + hl
                            prt = slice(64 * e2, 64 * e2 + 64)
                            kbs = [(kb, 0, None) for kb in range(16 * m)] + \
                                  [(16 * m + 4 * ip + r, 128 * ip, r) for ip in range(4) for r in range(4)]
                            oi = c2["o"] % 2
                            c2["o"] += 1
                            RQ = [B_QT[4 * m + i] for i in range(4)]

                            def emit_pv(pi, kb, c0, first, last, oi=oi, hl=hl):
                                P.op("pe", lambda e: e.matmul(
                                    bank(4 + oi)[0:65, c0:512], lhsT=Vaug[:, kb, hl, 0:65], rhs=PT[pi][:, c0:512],
                                    start=first, stop=last), R=[B_V[kb], B_PT[pi]], W=[B_O[oi]])

                            pend = []
                            for idx, (kb, c0, r) in enumerate(kbs):
                                si = c2["s"] % 4
                                c2["s"] += 1
                                P.op("pe", lambda e, si=si, kb=kb, c0=c0, r=r, pl=pl, prt=prt, m=m: e.matmul(
                                    bank(si)[:, c0:512], lhsT=KT[prt, pl, kb * 128:(kb + 1) * 128],
                                    rhs=QT[prt, pl, m * 512 + c0:(m + 1) * 512], start=True, stop=(r is None)),
                                    R=[B_KT[kb // 2]] + RQ, W=[B_S[si]])
                                if r is not None:
                                    P.op("pe", lambda e, si=si, c0=c0, r=r: e.matmul(
                                        bank(si)[:, c0:c0 + 128], lhsT=identb[:], rhs=maskT[:, r, :],
                                        start=False, stop=True), R=[B_mask, B_small], W=[B_S[si]])
                                if len(pend) >= 2:
                                    emit_pv(*pend.pop(0))
                                P.op("act", lambda e, si=si, c0=c0, kb=kb, m=m, h=h: e.activation(
                                    out=PT[si][:, c0:512], in_=bank(si)[:, c0:512], func=AF.Exp, scale=SCALE,
                                    bias=bias_all[:, QBASE[m] + kb, h:h + 1]), R=[B_S[si], B_bias], W=[B_PT[si]])
                                pend.append((si, kb, c0, idx == 0, idx == len(kbs) - 1))
                            for pv_ in pend:
                                emit_pv(*pv_)
                            P.op("act", lambda e, oi=oi: e.activation(out=Osb[oi][0:65, :], in_=bank(4 + oi)[0:65, :],
                                                                      func=AF.Copy), R=[B_O[oi]], W=[B_Osb[oi]])
                            P.op("dve", lambda e, oi=oi: e.reciprocal(out=rden[64:65, :], in_=Osb[oi][64:65, :]),
                                 R=[B_Osb[oi]], W=[B_rden])
                            P.op("pe", lambda e: e.matmul(bank(6)[0:64, :], lhsT=ones[64:65, 0:64], rhs=rden[64:65, :],
                                                          start=True, stop=True), R=[B_cst, B_rden], W=[B_BC])
                            P.op("dve", lambda e, oi=oi, h=h, m=m: e.tensor_tensor(
                                out=oT[0:64, h, m * 512:(m + 1) * 512], in0=Osb[oi][0:64, :], in1=bank(6)[0:64, :],
                                op=ALU.mult), R=[B_Osb[oi], B_BC], W=[B_oT[m]])
                    P.barrier()


        def sample_phase():
          if True:
            with ExitStack() as sS:
                  ptr_sb = sb("ptr_sb", [128, 32], I32, sS)
                  shi_sb = sb("shi_sb", [128, 1], F32, sS)
                  idx_all = sb("idx_all", [128, 32], I32, sS)
                  bmask = sb("bmask", [32, 512], F32, sS)
                  K8 = [sb("K8_%d" % i, [128, 4096], F32, sS) for i in range(2)]
                  V8 = [sb("V8_%d" % i, [128, 4096], F32, sS) for i in range(2)]
                  V8b = [sb("V8b_%d" % i, [128, 4096], BF16, sS) for i in range(2)]
                  K8T = [sb("K8T_%d" % i, [128, 8, 4, 128], BF16, sS) for i in range(2)]
                  Lall = sb("Lall", [128, 8, 8, 8], F32, sS)
                  Ls = [sb("Ls%d" % i, [128, 8, 8, 8], F32, sS) for i in range(3)]
                  Rall = sb("Rall", [128, 8, 8, 8], F32, sS)
                  rowtot = sb("rowtot", [128, 8, 8], F32, sS)
                  gsuf = sb("gsuf", [128, 8, 8], F32, sS)
                  sctmp = [sb("sctmp%d" % i, [128, 8, 8, 4], F32, sS) for i in range(2)]
                  Pt = [sb("Pt%d" % i, [128, 8, 32], BF16, sS) for i in range(2)]
                  onesb = sb("onesb", [128, 2], BF16, sS)
                  m4 = sb("m4", [128, 4], F32, sS)
                  cnn = sb("cnn", [128, 8], F32, sS)
                  tmpN = sb("tmpN", [128, 8, 4], F32, sS)
                  PtN = sb("PtN", [128, 32], BF16, sS)
                  Oss = sb("Oss", [32, 512], F32, sS)
                  Osel = sb("Osel", [32, 64], F32, sS)
                  rdn = sb("rdn", [32, 2], F32, sS)
                  B_idx, B_bm, B_L, B_R, B_rt, B_gs, B_m4, B_cn, B_tN, B_PtN, B_Oss, B_Osel, B_rdn = (Buf() for _ in range(13))
                  B_Ls = [Buf() for _ in range(3)]
                  B_K8 = [Buf(), Buf()]
                  B_V8 = [Buf(), Buf()]
                  B_V8b = [Buf(), Buf()]
                  B_K8T = [Buf(), Buf()]
                  B_sct = [Buf(), Buf()]
                  B_Pt = [Buf(), Buf()]
                  BKs = [Buf() for _ in range(8)]
                  cS = dict(k=0, t=0, s=0, e=0)
                  P.dma("sp", lambda e: e.dma_start(out=ptr_sb[:], in_=pt_rep), W=[B_idx])
                  P.dma("sp", lambda e: e.dma_start(out=shi_sb[:], in_=shi_d), W=[B_idx])
                  P.dma("sp", lambda e: e.dma_start(out=bmask[:], in_=bmask_d), W=[B_bm])
                  P.op("dve", lambda e: e.tensor_scalar(out=idx_all[:], in0=ptr_sb[:], scalar1=16.0, scalar2=shi_sb[:, 0:1],
                                                        op0=ALU.mult, op1=ALU.add), R=[B_idx], W=[B_idx])
                  if SDBG == 1:
                      return
                  P.op("pool", lambda e: e.memset(onesb[:], 1.0), W=[B_m4])
                  P.op("pool", lambda e: e.memset(oT[:, :, 2048:2176], 0.0), W=[B_oT[4]])
                  P.op("dve", lambda e: e.tensor_scalar(out=m4[0:4, :], in0=tmins[0:4, 0:4], scalar1=0.0, scalar2=None,
                                                        op0=ALU.is_ge), R=[B_cst], W=[B_m4])
                  P.op("dve", lambda e: e.tensor_scalar(out=m4[0:4, :], in0=m4[0:4, :], scalar1=-1.0, scalar2=BIG,
                                                        op0=ALU.add, op1=ALU.mult), R=[B_m4], W=[B_m4])
                  Vs4 = sb("Vs4", [4, 4, 512], BF16, sS)
                  lfs4 = sb("lfs4", [4, 4, 8], F32, sS)
                  B_n4 = Buf()
                  for s_ in range(4):
                      P.dma("sp", lambda e, s_=s_: e.dma_start(out=Vs4[0:4, s_, :], in_=Vs_bf[32 * s_:32 * s_ + 4, :]),
                            R=[B_Vs], W=[B_n4])
                      P.dma("sp", lambda e, s_=s_: e.dma_start(out=lfs4[0:4, s_, :], in_=lfs_t[32 * s_:32 * s_ + 4, :]),
                            R=[B_lfs], W=[B_n4])
                  for s_ in range(4):
                      sl = slice(0, 4)
                      for g8 in range(8):
                          P.dma("pool", lambda e, g8=g8, s_=s_: e.indirect_dma_start(
                              out=Lall[:, g8, :, :].rearrange("p a b -> p (a b)"), out_offset=None, in_=cache_l,
                              in_offset=bass.IndirectOffsetOnAxis(ap=idx_all[:, 8 * s_ + g8:8 * s_ + g8 + 1], axis=0)),
                              R=[B_idx], W=[B_L])
                      if SDBG == 2:
                          return
                      P.op("dve", lambda e: e.tensor_reduce(out=rowtot[:], in_=Lall[:].rearrange("p g s h -> p g h s"),
                                                            axis=mybir.AxisListType.X, op=ALU.add), R=[B_L], W=[B_rt])
                      src, Bsrc = Lall, B_L
                      for step, (dst, Bd) in zip((1, 2, 4), zip(Ls, B_Ls)):
                          P.op("dve", lambda e, src=src, dst=dst: e.tensor_copy(out=dst[:], in_=src[:]), R=[Bsrc], W=[Bd])
                          P.op("dve", lambda e, src=src, dst=dst, step=step: e.tensor_tensor(
                              out=dst[:, :, 0:8 - step, :], in0=src[:, :, 0:8 - step, :], in1=src[:, :, step:8, :], op=ALU.add),
                              R=[Bsrc], W=[Bd])
                          src, Bsrc = dst, Bd
                      P.op("dve", lambda e, src=src: e.tensor_tensor(out=Rall[:], in0=src[:], in1=Lall[:], op=ALU.subtract),
                           R=[Bsrc, B_L], W=[B_R])
                      rt_flat = rowtot[:].rearrange("p g h -> p (g h)")
                      P.op("pe", lambda e: e.matmul(bank(6)[:, 0:64], lhsT=U_strict, rhs=rt_flat, start=True, stop=True),
                           R=[B_cst, B_rt], W=[BKs[6]])
                      P.op("pe", lambda e: e.matmul(bank(7)[:, 0:64], lhsT=ones, rhs=rt_flat, start=True, stop=True),
                           R=[B_cst, B_rt], W=[BKs[7]])
                      P.op("pool", lambda e: e.memset(gsuf[:], 0.0), W=[B_gs])
                      for g8 in range(6, -1, -1):
                          P.op("dve", lambda e, g8=g8: e.tensor_tensor(
                              out=gsuf[:, g8, :], in0=gsuf[:, g8 + 1, :], in1=bank(7)[:, 8 * (g8 + 1):8 * (g8 + 2)], op=ALU.add),
                              R=[BKs[7], B_gs], W=[B_gs])
                      P.op("dve", lambda e: e.tensor_tensor(out=gsuf[:], in0=gsuf[:],
                                                            in1=bank(6)[:, 0:64].rearrange("p (g h) -> p g h", h=8), op=ALU.add),
                           R=[BKs[6], B_gs], W=[B_gs])
                      P.op("dve", lambda e: e.tensor_tensor(out=Rall[:], in0=Rall[:],
                                                            in1=gsuf[:].unsqueeze(2).to_broadcast([128, 8, 8, 8]), op=ALU.add),
                           R=[B_gs, B_R], W=[B_R])
                      if SDBG == 3:
                          return
                      for g8 in range(8):
                          ki = cS["k"] % 2
                          cS["k"] += 1
                          ix = idx_all[:, 8 * s_ + g8:8 * s_ + g8 + 1]
                          P.dma("pool", lambda e, ki=ki, ix=ix: e.indirect_dma_start(
                              out=K8[ki][:, :], out_offset=None, in_=cache_k,
                              in_offset=bass.IndirectOffsetOnAxis(ap=ix, axis=0)), R=[B_idx], W=[B_K8[ki]])
                          P.dma("pool", lambda e, ki=ki, ix=ix: e.indirect_dma_start(
                              out=V8[ki][:, :], out_offset=None, in_=cache_v,
                              in_offset=bass.IndirectOffsetOnAxis(ap=ix, axis=0)), R=[B_idx], W=[B_V8[ki]])
                          for hh in range(2):
                              P.op("pool", lambda e, ki=ki, hh=hh: e.tensor_copy(out=V8b[ki][:, 2048 * hh:2048 * (hh + 1)],
                                                                                 in_=V8[ki][:, 2048 * hh:2048 * (hh + 1)]),
                                   R=[B_V8[ki]], W=[B_V8b[ki]])
                          for sl8 in range(8):
                              tb = cS["t"] % 2
                              cS["t"] += 1
                              for pr in range(4):
                                  P.op("pe", lambda e, ki=ki, sl8=sl8, pr=pr, tb=tb: e.transpose(
                                      bank(tb)[:, 128 * pr:128 * (pr + 1)],
                                      K8[ki][:, 512 * sl8 + 128 * pr:512 * sl8 + 128 * (pr + 1)], ident),
                                      R=[B_K8[ki], B_cst], W=[BKs[tb]])
                              ev = cS["e"] % 2
                              cS["e"] += 1
                              if ev == 0:
                                  P.op("act", lambda e, ki=ki, sl8=sl8, tb=tb: e.activation(
                                      out=K8T[ki][:, sl8, :, :].rearrange("p a b -> p (a b)"), in_=bank(tb), func=AF.Copy),
                                      R=[BKs[tb]], W=[B_K8T[ki]])
                              else:
                                  P.op("dve", lambda e, ki=ki, sl8=sl8, tb=tb: e.tensor_copy(
                                      out=K8T[ki][:, sl8, :, :].rearrange("p a b -> p (a b)"), in_=bank(tb)),
                                      R=[BKs[tb]], W=[B_K8T[ki]])
                          si = cS["s"] % 2
                          cS["s"] += 1
                          for sl8 in range(8):
                              for h_ in range(8):
                                  pr, e2 = h_ // 2, h_ % 2
                                  prt = slice(64 * e2, 64 * e2 + 64)
                                  P.op("pe", lambda e, ki=ki, sl8=sl8, h_=h_, pr=pr, prt=prt, si=si, s_=s_: e.matmul(
                                      bank(2 + si)[:, 32 * sl8 + 4 * h_:32 * sl8 + 4 * h_ + 4], lhsT=K8T[ki][prt, sl8, pr, :],
                                      rhs=QTs[prt, pr, 32 * s_:32 * s_ + 4], start=True, stop=True),
                                      R=[B_K8T[ki], B_QTs], W=[BKs[2 + si]])
                          P.op("dve", lambda e, si=si, g8=g8: e.scalar_tensor_tensor(
                              out=sctmp[si][:], in0=bank(2 + si)[:, 0:256].rearrange("p (a h t) -> p a h t", h=8, t=4),
                              scalar=SCALE, in1=Rall[:, g8, :, :].unsqueeze(3).to_broadcast([128, 8, 8, 4]),
                              op0=ALU.mult, op1=ALU.add), R=[BKs[2 + si], B_R], W=[B_sct[si]])
                          P.op("act", lambda e, si=si: e.activation(
                              out=Pt[si][:].rearrange("p a b -> p (a b)"), in_=sctmp[si][:].rearrange("p a h t -> p (a h t)"),
                              func=AF.Exp), R=[B_sct[si]], W=[B_Pt[si]])
                          for sl8 in range(8):
                              first = (g8 == 0 and sl8 == 0)
                              P.op("pe", lambda e, si=si, ki=ki, sl8=sl8, first=first: e.matmul(
                                  bank(4)[0:32, :], lhsT=Pt[si][:, sl8, :], rhs=V8b[ki][:, 512 * sl8:512 * (sl8 + 1)],
                                  start=first, stop=False), R=[B_Pt[si], B_V8b[ki]], W=[BKs[4]])
                              P.op("pe", lambda e, si=si, sl8=sl8, first=first: e.matmul(
                                  bank(5)[0:32, 0:2], lhsT=Pt[si][:, sl8, :], rhs=onesb[:, 0:2],
                                  start=first, stop=False), R=[B_Pt[si], B_m4], W=[BKs[5]])
                      if SDBG == 4:
                          return
                      P.op("pe", lambda e, sl=sl, s_=s_: e.matmul(bank(6)[sl, 0:8], lhsT=U_incl[sl, sl], rhs=lfs4[sl, s_, :],
                                                           start=True, stop=True), R=[B_cst, B_n4, B_gs], W=[BKs[6]])
                      P.op("dve", lambda e, sl=sl: e.tensor_scalar(out=cnn[sl, :], in0=bank(6)[sl, 0:8], scalar1=-1.0,
                                                                   scalar2=None, op0=ALU.mult), R=[BKs[6]], W=[B_cn])
                      if SDBG == 41:
                          return
                      for h_ in range(8):
                          pr, e2 = h_ // 2, h_ % 2
                          prt = slice(64 * e2, 64 * e2 + 64)
                          P.op("pe", lambda e, h_=h_, pr=pr, prt=prt, s_=s_, e2=e2: e.matmul(
                              bank(6 + e2)[0:32, 64 + 4 * h_:64 + 4 * h_ + 4], lhsT=KTs[prt, pr, 32 * s_:32 * s_ + 32],
                              rhs=QTs[prt, pr, 32 * s_:32 * s_ + 4], start=True, stop=True),
                              R=[B_KTs, B_QTs, B_gs, B_cn], W=[BKs[6 + e2]])
                      if SDBG == 42:
                          return
                      for e2 in range(2):
                          P.op("dve", lambda e, sl=sl, e2=e2: e.scalar_tensor_tensor(
                              out=tmpN[sl, :, :].rearrange("p (a b) t -> p a b t", b=2)[:, :, e2, :],
                              in0=bank(6 + e2)[sl, 64:96].rearrange("p (a b t) -> p a b t", b=2, t=4)[:, :, e2, :], scalar=SCALE,
                              in1=cnn[sl, :].rearrange("p (a b) -> p a b", b=2)[:, :, e2].unsqueeze(2).to_broadcast([4, 4, 4]),
                              op0=ALU.mult, op1=ALU.add), R=[BKs[6 + e2], B_cn], W=[B_tN])
                      P.op("dve", lambda e, sl=sl: e.tensor_tensor(
                          out=tmpN[sl, :, :], in0=tmpN[sl, :, :], in1=m4[sl, :].unsqueeze(1).to_broadcast([4, 8, 4]), op=ALU.add),
                          R=[B_tN, B_m4], W=[B_tN])
                      P.op("act", lambda e, sl=sl: e.activation(out=PtN[sl, :], in_=tmpN[sl, :, :].rearrange("p h t -> p (h t)"),
                                                                func=AF.Exp), R=[B_tN], W=[B_PtN])
                      if SDBG == 43:
                          return
                      P.op("pe", lambda e, sl=sl, s_=s_: e.matmul(bank(4)[0:32, :], lhsT=PtN[sl, :], rhs=Vs4[sl, s_, :],
                                                           start=False, stop=True), R=[B_PtN, B_n4], W=[BKs[4]])
                      P.op("pe", lambda e, sl=sl: e.matmul(bank(5)[0:32, 0:2], lhsT=PtN[sl, :], rhs=onesb[sl, 0:2],
                                                           start=False, stop=True), R=[B_PtN, B_m4], W=[BKs[5]])
                      if SDBG == 5:
                          return
                      P.op("act", lambda e: e.activation(out=Oss[:], in_=bank(4)[0:32, :], func=AF.Copy), R=[BKs[4]], W=[B_Oss])
                      P.op("dve", lambda e: e.reciprocal(out=rdn[:], in_=bank(5)[0:32, 0:2]), R=[BKs[5]], W=[B_rdn])
                      P.op("dve", lambda e: e.tensor_tensor(out=Oss[:], in0=Oss[:], in1=bmask[:], op=ALU.mult),
                           R=[B_bm, B_Oss], W=[B_Oss])
                      P.op("dve", lambda e: e.tensor_reduce(out=Osel[:], in_=Oss[:].rearrange("p (h d) -> p d h", d=64),
                                                            axis=mybir.AxisListType.X, op=ALU.add), R=[B_Oss], W=[B_Osel])
                      P.op("dve", lambda e: e.tensor_scalar(out=Osel[:], in0=Osel[:], scalar1=rdn[:, 0:1], scalar2=None,
                                                            op0=ALU.mult), R=[B_rdn, B_Osel], W=[B_Osel])
                      if DBG_OT and s_ == 0:
                          P.dma("sp", lambda e: e.dma_start(out=dbg2[:, 0:512], in_=Rall[:].rearrange("p a b c -> p (a b c)")), R=[B_R])
                          P.dma("sp", lambda e: e.dma_start(out=dbg2[:, 512:768], in_=sctmp[1][:].rearrange("p a b c -> p (a b c)")), R=[B_sct[1]])
                          P.dma("sp", lambda e: e.dma_start(out=dbg2[0:32, 768:1280], in_=Oss[:]), R=[B_Oss])
                          P.dma("sp", lambda e: e.dma_start(out=dbg2[0:32, 1280:1282], in_=rdn[:]), R=[B_rdn])
                          P.dma("sp", lambda e: e.dma_start(out=dbg2[0:32, 1290:1354], in_=Osel[:]), R=[B_Osel])
                          P.dma("sp", lambda e: e.dma_start(out=dbg2[:, 1400:1432], in_=idx_all[:].bitcast(F32)), R=[B_idx])
                          P.dma("sp", lambda e: e.dma_start(out=dbg2[0:4, 1440:1472], in_=tmpN[0:4, :, :].rearrange("p a b -> p (a b)")), R=[B_tN])
                      P.op("pe", lambda e: e.transpose(bank(7)[0:64, 64:96], Osel[:], ident[0:32, 0:32]),
                           R=[B_Osel, B_cst, B_tN], W=[BKs[7]])
                      P.op("act", lambda e, s_=s_: e.activation(
                          out=oT[0:64, :, 2048 + 32 * s_:2048 + 32 * s_ + 4],
                          in_=bank(7)[0:64, 64:96].rearrange("p (h t) -> p h t", t=4), func=AF.Copy),
                          R=[BKs[7]], W=[B_oT[4]])
                  P.barrier()
        if STAGE >= 4:
            sample_phase()
            P.barrier()

        if STAGE >= 3 and not SKIP_P5:
            def wv(w):
                return w.rearrange("(kc p) n -> p kc n", p=128)
            w_pa_v = w_pa.rearrange("(h p) n -> p h n", p=64)
            w_pb_v, w_o_v, w_f1_v, w_f2_v = wv(w_pb), wv(w_o), wv(w_f1), wv(w_f2)
            for hf in range(2):
                with ExitStack() as sH:
                    NT = 1152 if hf == 0 else 1024
                    nblk = NT // 128
                    xT = sb("xT", [128, 8, NT], F32, sH)
                    hTh = sb("hTh", [128, 8, NT], BF16, sH)
                    sT = sb("sT", [128, 4, NT], BF16, sH)
                    rsT = sb("rsT", [128, 512], F32, sH)
                    tA = [sb("tA%d" % i, [128, 512], F32, sH) for i in range(2)]
                    tB = [sb("tB%d" % i, [128, 512], F32, sH) for i in range(2)]
                    B_rsT = Buf()
                    B_tA = [Buf(), Buf()]
                    B_tB = [Buf(), Buf()]
                    BK = [Buf() for _ in range(8)]
                    tiles = [(0, 512, 2 * hf * 512), (512, 512, (2 * hf + 1) * 512)]
                    if hf == 0:
                        tiles.append((1024, 128, 2048))
                    B_xT = [Buf() for _ in tiles]
                    B_hTh = [Buf() for _ in tiles]
                    B_sT = [Buf() for _ in tiles]
                    rot = dict(a=0, b=0, c=0, d=0, t=0, u=0)

                    def nxt(k, n=2):
                        v = rot[k] % n
                        rot[k] += 1
                        return v

                    def per_sample(N, fn_full, fn_s):
                        if N == 512:
                            fn_full()
                        else:
                            for s_ in range(4):
                                fn_s(s_, 32 * s_)

                    def col_stats(ti, src_fn, nparts, scale, Rsrc=None):
                        c0, N, _ = tiles[ti]
                        for kc in range(nparts):
                            a = nxt("t")
                            P.op("act", lambda e, kc=kc, a=a: e.activation(out=tA[a][:, 0:N], in_=src_fn(kc), func=AF.Square),
                                 R=(Rsrc or [B_xT[ti]]), W=[B_tA[a]])
                            P.op("pe", lambda e, kc=kc, a=a: e.matmul(bank(6)[:, 0:N], lhsT=ones, rhs=tA[a][:, 0:N],
                                                                      start=(kc == 0), stop=(kc == nparts - 1)),
                                 R=[B_tA[a], B_cst], W=[BK[6]])
                        P.op("act", lambda e: e.activation(out=rsT[:, 0:N], in_=bank(6)[:, 0:N], func=AF.Sqrt, scale=scale,
                                                           bias=EPS), R=[BK[6]], W=[B_rsT])
                        P.op("dve", lambda e: e.reciprocal(out=rsT[:, 0:N], in_=rsT[:, 0:N]), R=[B_rsT], W=[B_rsT])

                    with ExitStack() as sa:
                        hTo = sb("hTo", [128, 8, 256], BF16, sa)
                        B_hTo = Buf()
                        with ExitStack() as s1:
                            NP = NormPipe(s1)
                            TPf = [psum[:, 2048 + 1024 * i:2048 + 1024 * (i + 1)].rearrange("p (k t) -> p k t", t=128)
                                   for i in range(2)]
                            B_TPf = [Buf(), Buf()]
                            for qq in range(nblk):
                                sample = (qq == 8)
                                q = 8 * hf + qq
                                col = qq * 128
                                ti_ = min(qq // 4, 2)
                                xi = NP.load(xs_d if sample else xown[q, 32:160, :], 128)
                                rs_ap, B_rs = NP.rstd(xi, 128)
                                ni = NP.normalize(xi, 128, rs_ap, B_rs)
                                tpi = NP.new_tp()
                                NP.transpose(ni, 128, tpi, 0)
                                NP.evac_mod(tpi, 128, lambda kc, col=col: hTh[:, kc, col:col + 128], B_hTh[ti_], a1T, sh1,
                                            sample=sample)
                                fi = qq % 2
                                for kc in range(8):
                                    P.op("pe", lambda e, kc=kc, fi=fi, xi=xi: e.transpose(
                                        TPf[fi][:, kc, :], NP.xsl[xi][:, kc * 128:(kc + 1) * 128], ident),
                                        R=[NP.B_x[xi], B_cst], W=[B_TPf[fi]])
                                P.op("act", lambda e, fi=fi, col=col: e.activation(out=xT[:, :, col:col + 128], in_=TPf[fi],
                                                                                   func=AF.Copy), R=[B_TPf[fi]], W=[B_xT[ti_]])
                                if not sample:
                                    xi = NP.load(xown[q, 0:32, :], 32)
                                    rs_ap, B_rs = NP.rstd(xi, 32)
                                    ni = NP.normalize(xi, 32, rs_ap, B_rs)
                                    tpi = NP.new_tp()
                                    NP.transpose(ni, 32, tpi, 0)
                                    NP.evac_mod(tpi, 32, lambda kc, qq=qq: hTo[:, kc, qq * 32:(qq + 1) * 32], B_hTo, a1T, sh1)
                            P.barrier()

                        with ExitStack() as s2:
                            wglu = sb("wglu", [128, 8, 1024], BF16, s2)
                            uT = sb("uT", [128, 4, 8, 160], BF16, s2)
                            uS = sb("uS", [128, 4, 4, 34], BF16, s2)
                            stTf = sb("stTf", [128, 4, 4, 30], F32, s2)
                            u32 = sb("u32", [128, 4, 128], F32, s2)
                            ucp = sb("ucp", [128, 512], F32, s2)
                            stro = sb("stro", [104, 512], F32, s2)
                            diag = [sb("diag%d" % i, [128, CK, 128], BF16, s2) for i in range(2)]
                            ycv = sb("ycv", [128, 4, 512], F32, s2)
                            mean_sb = sb("mean_sb", [128, 512], F32, s2)
                            B_wglu, B_uT, B_uS, B_u32, B_ucp, B_ycv, B_mean, B_stro = (Buf() for _ in range(8))
                            B_diag = [Buf(), Buf()]
                            for hh in range(2):
                                P.dma("pool", lambda e, hh=hh: e.dma_start(
                                    out=wglu[:, :, 512 * hh:512 * (hh + 1)],
                                    in_=w_in_v[:, :, GLU_OFF + 512 * hh:GLU_OFF + 512 * (hh + 1)]), W=[B_wglu])
                            P.op("pool", lambda e: e.memset(ycv[:], 0.0), W=[B_ycv])
                            P.op("pool", lambda e: e.memset(u32[:], 0.0), W=[B_u32])
                            if hf == 0:
                                P.dma("sp", lambda e: e.dma_start(out=stTf[:], in_=stT_d), W=[B_uS])
                                for cc in range(4):
                                    P.op("pool", lambda e, cc=cc: e.tensor_copy(out=uS[:, cc, :, 0:30], in_=stTf[:, cc, :, :]),
                                         R=[B_uS], W=[B_uS])
                                P.dma("sp", lambda e: e.dma_start(out=stro[:], in_=st_rows.rearrange("s r c -> (s r) c")),
                                      W=[B_stro])
                                for s_ in range(4):
                                    P.dma("act", lambda e, s_=s_: e.dma_start(out=convs_o[s_, 0:26, :],
                                                                              in_=stro[26 * s_:26 * (s_ + 1), :]), R=[B_stro])

                            def glu(src_fn, N, Rsrc, outs):
                                for cc in range(4):
                                    ia, ib = nxt("a"), nxt("b")
                                    for kc in range(8):
                                        P.op("pe", lambda e, kc=kc, cc=cc, ia=ia: e.matmul(
                                            bank(ia)[:, 0:N], lhsT=wglu[:, kc, cc * 128:(cc + 1) * 128], rhs=src_fn(kc),
                                            start=(kc == 0), stop=(kc == 7)), R=[B_wglu] + Rsrc, W=[BK[ia]])
                                    for kc in range(8):
                                        P.op("pe", lambda e, kc=kc, cc=cc, ib=ib: e.matmul(
                                            bank(2 + ib)[:, 0:N], lhsT=wglu[:, kc, 512 + cc * 128:512 + (cc + 1) * 128],
                                            rhs=src_fn(kc), start=(kc == 0), stop=(kc == 7)), R=[B_wglu] + Rsrc, W=[BK[2 + ib]])
                                    P.op("act", lambda e, cc=cc, ib=ib: e.activation(
                                        out=tB[ib][:, 0:N], in_=bank(2 + ib)[:, 0:N], func=AF.Sigmoid,
                                        bias=bfm[:, 12 + cc:13 + cc]), R=[BK[2 + ib], B_small], W=[B_tB[ib]])
                                    for (dst_fn, view, Wb) in outs:
                                        P.op("dve", lambda e, cc=cc, ia=ia, ib=ib, dst_fn=dst_fn, view=view: e.scalar_tensor_tensor(
                                            out=dst_fn(cc), in0=view(bank(ia)[:, 0:N]), scalar=bfm[:, 8 + cc:9 + cc],
                                            in1=view(tB[ib][:, 0:N]), op0=ALU.add, op1=ALU.mult),
                                            R=[BK[ia], B_tB[ib], B_small], W=Wb)

                            glu(lambda kc: hTo[:, kc, :], 256, [B_hTo],
                                [(lambda cc: uT[:, cc, :, 0:32], lambda a: a.rearrange("p (b t) -> p b t", t=32), [B_uT])])
                            for cc in range(4):
                                P.op("dve", lambda e, cc=cc: e.tensor_tensor(
                                    out=uT[:, cc, :, 0:32], in0=uT[:, cc, :, 0:32],
                                    in1=hval[:, 8 * hf:8 * hf + 8].unsqueeze(2).to_broadcast([128, 8, 32]), op=ALU.mult),
                                    R=[B_small, B_uT], W=[B_uT])
                            for ti, (c0, N, oc0) in enumerate(tiles):
                                if N == 512:
                                    outs = [(lambda cc, ti=ti: uT[:, cc, 4 * ti:4 * ti + 4, 32:160],
                                             lambda a: a.rearrange("p (b t) -> p b t", t=128), [B_uT])]
                                    if hf == 1 and ti == 1:
                                        outs.append((lambda cc: u32[:, cc, 0:32], lambda a: a[:, 480:512], [B_u32]))
                                else:
                                    outs = [(lambda cc: uS[:, cc, :, 30:34],
                                             lambda a: a.rearrange("p (s t) -> p s t", t=32)[:, :, 0:4], [B_uS]),
                                            (lambda cc: u32[:, cc, :], lambda a: a, [B_u32])]
                                glu(lambda kc, c0=c0, N=N: hTh[:, kc, c0:c0 + N], N, [B_hTh[ti]], outs)
                            for cc in range(4):
                                P.op("pe", lambda e, cc=cc: e.transpose(
                                    bank(5)[0:(128 if hf == 0 else 32), cc * 128:(cc + 1) * 128],
                                    u32[:, cc, 0:(128 if hf == 0 else 32)], ident), R=[B_u32, B_cst], W=[BK[5]])
                            nr = 128 if hf == 0 else 32
                            P.op("act", lambda e: e.activation(out=ucp[0:nr, :], in_=bank(5)[0:nr, :], func=AF.Copy),
                                 R=[BK[5]], W=[B_ucp])
                            if hf == 1:
                                P.dma("act", lambda e: e.dma_start(out=convp_o, in_=ucp[0:32, :]), R=[B_ucp])
                            else:
                                for s_ in range(4):
                                    P.dma("act", lambda e, s_=s_: e.dma_start(out=convs_o[s_, 26:30, :],
                                                                              in_=ucp[32 * s_:32 * s_ + 4, :]), R=[B_ucp])
                            for ti, (c0, N, oc0) in enumerate(tiles):
                                NC = N if N == 512 else 16
                                for cc in range(4):
                                    di = nxt("d")
                                    for k in range(CK):
                                        P.op("pool", lambda e, cc=cc, k=k, di=di: e.tensor_scalar(
                                            out=diag[di][:, k, :], in0=identb[:], scalar1=cvp[:, cc * CK + k:cc * CK + k + 1],
                                            scalar2=0.0, op0=ALU.mult, op1=ALU.add), R=[B_small], W=[B_diag[di]])
                                    ci = nxt("c")
                                    for k in range(CK):
                                        if N == 512:
                                            rhs = uT[:, cc, 4 * ti:4 * ti + 4, 2 + k:2 + k + 128]
                                        else:
                                            rhs = uS[:, cc, :, k:k + 4]
                                        P.op("pe", lambda e, k=k, di=di, ci=ci, rhs=rhs: e.matmul(
                                            bank(4 + ci)[:, 0:NC], lhsT=diag[di][:, k, :], rhs=rhs, start=(k == 0),
                                            stop=(k == CK - 1)), R=[B_diag[di], B_uT, B_uS], W=[BK[4 + ci]])
                                    if N == 512:
                                        ydst, ysrc = ycv[:, cc, :], bank(4 + ci)[:, 0:512]
                                    else:
                                        ydst = ycv[:, cc, 0:128].rearrange("p (s t) -> p s t", t=32)[:, :, 0:4]
                                        ysrc = bank(4 + ci)[:, 0:16].rearrange("p (s t) -> p s t", t=4)
                                    P.op("act", lambda e, cc=cc, ydst=ydst, ysrc=ysrc: e.activation(
                                        out=ydst, in_=ysrc, func=AF.Identity, bias=cvp[:, 124 + cc:125 + cc]),
                                        R=[BK[4 + ci], B_small], W=[B_ycv])
                                for cc in range(4):
                                    P.op("pe", lambda e, cc=cc: e.matmul(bank(7)[:, 0:N], lhsT=ones, rhs=ycv[:, cc, 0:N],
                                                                         start=(cc == 0), stop=(cc == 3)),
                                         R=[B_ycv, B_cst], W=[BK[7]])
                                P.op("act", lambda e: e.activation(out=mean_sb[:, 0:N], in_=bank(7)[:, 0:N], func=AF.Copy,
                                                                   scale=1.0 / CW), R=[BK[7]], W=[B_mean])
                                for cc in range(4):
                                    P.op("dve", lambda e, cc=cc: e.tensor_tensor(out=ycv[:, cc, 0:N], in0=ycv[:, cc, 0:N],
                                                                                 in1=mean_sb[:, 0:N], op=ALU.subtract),
                                         R=[B_mean, B_ycv], W=[B_ycv])
                                B_save = B_xT[ti]
                                col_stats(ti, lambda kc: ycv[:, kc, 0:N], 4, 1.0 / CW, Rsrc=[B_ycv])
                                for cc in range(4):
                                    a = nxt("t")
                                    P.op("dve", lambda e, cc=cc, a=a: e.tensor_tensor(
                                        out=tA[a][:, 0:N], in0=ycv[:, cc, 0:N], in1=rsT[:, 0:N], op=ALU.mult),
                                        R=[B_ycv, B_rsT], W=[B_tA[a]])
                                    P.op("act", lambda e, cc=cc, a=a, c0=c0: e.activation(
                                        out=sT[:, cc, c0:c0 + N], in_=tA[a][:, 0:N], func=AF.Silu,
                                        scale=cvp[:, 128 + cc:129 + cc], bias=cvp[:, 132 + cc:133 + cc]),
                                        R=[B_tA[a], B_small], W=[B_sT[ti]])
                                P.barrier()
                            P.barrier()
                        P.barrier()

                    with ExitStack() as sb_:
                        mT = sb("mT", [128, 8, NT], BF16, sb_)
                        B_mT = [Buf() for _ in tiles]
                        with ExitStack() as s3:
                            wga = [sb("wga%d" % i, [128, 8, 256], BF16, s3) for i in range(2)]
                            wgb = [sb("wgb%d" % i, [128, 8, 256], BF16, s3) for i in range(2)]
                            wpa = [sb("wpa%d" % i, [64, 8, 256], BF16, s3) for i in range(2)]
                            wpb = [sb("wpb%d" % i, [128, 4, 256], BF16, s3) for i in range(2)]
                            t1 = sb("t1", [128, 512], F32, s3)
                            t2 = sb("t2", [128, 512], F32, s3)
                            B_ws = [Buf(), Buf()]
                            B_t1, B_t2 = Buf(), Buf()
                            for rr in range(4):
                                wi = rr % 2
                                cs = slice(256 * rr, 256 * (rr + 1))
                                P.dma("pool", lambda e, wi=wi, rr=rr: e.dma_start(
                                    out=wga[wi][:], in_=w_in_v[:, :, GA_OFF + 256 * rr:GA_OFF + 256 * (rr + 1)]), W=[B_ws[wi]])
                                P.dma("pool", lambda e, wi=wi, rr=rr: e.dma_start(
                                    out=wgb[wi][:], in_=w_in_v[:, :, GB_OFF + 256 * rr:GB_OFF + 256 * (rr + 1)]), W=[B_ws[wi]])
                                P.dma("pool", lambda e, wi=wi, cs=cs: e.dma_start(out=wpa[wi][:], in_=w_pa_v[:, :, cs]),
                                      W=[B_ws[wi]])
                                P.dma("pool", lambda e, wi=wi, cs=cs: e.dma_start(out=wpb[wi][:], in_=w_pb_v[:, :, cs]),
                                      W=[B_ws[wi]])
                                for ti, (c0, N, oc0) in enumerate(tiles):
                                    B_o = B_oT[4] if N == 128 else B_oT[oc0 // 512]
                                    for nn in range(2):
                                        n = 2 * rr + nn
                                        ns = slice(128 * nn, 128 * (nn + 1))
                                        r_ = nxt("u")
                                        for kc in range(8):
                                            P.op("pe", lambda e, kc=kc, wi=wi, ns=ns, r_=r_, c0=c0, N=N: e.matmul(
                                                bank(r_)[:, 0:N], lhsT=wga[wi][:, kc, ns], rhs=hTh[:, kc, c0:c0 + N],
                                                start=(kc == 0), stop=(kc == 7)), R=[B_ws[wi], B_hTh[ti]], W=[BK[r_]])
                                        for kc in range(8):
                                            P.op("pe", lambda e, kc=kc, wi=wi, ns=ns, r_=r_, c0=c0, N=N: e.matmul(
                                                bank(2 + r_)[:, 0:N], lhsT=wgb[wi][:, kc, ns], rhs=hTh[:, kc, c0:c0 + N],
                                                start=(kc == 0), stop=(kc == 7)), R=[B_ws[wi], B_hTh[ti]], W=[BK[2 + r_]])
                                        for h_ in range(8):
                                            P.op("pe", lambda e, h_=h_, wi=wi, ns=ns, r_=r_, oc0=oc0, N=N: e.matmul(
                                                bank(4 + r_)[:, 0:N], lhsT=wpa[wi][:, h_, ns], rhs=oT[:, h_, oc0:oc0 + N],
                                                start=(h_ == 0), stop=(h_ == 7)), R=[B_ws[wi], B_o], W=[BK[4 + r_]])
                                        for cc in range(4):
                                            P.op("pe", lambda e, cc=cc, wi=wi, ns=ns, r_=r_, c0=c0, N=N: e.matmul(
                                                bank(6 + r_)[:, 0:N], lhsT=wpb[wi][:, cc, ns], rhs=sT[:, cc, c0:c0 + N],
                                                start=(cc == 0), stop=(cc == 3)), R=[B_ws[wi], B_sT[ti]], W=[BK[6 + r_]])
                                        P.op("act", lambda e, r_=r_, n=n, N=N: e.activation(
                                            out=tA[r_][:, 0:N], in_=bank(r_)[:, 0:N], func=AF.Sigmoid,
                                            bias=bfm[:, 16 + n:17 + n]), R=[BK[r_], B_small], W=[B_tA[r_]])
                                        P.op("act", lambda e, r_=r_, n=n, N=N: e.activation(
                                            out=tB[r_][:, 0:N], in_=bank(2 + r_)[:, 0:N], func=AF.Sigmoid,
                                            bias=bfm[:, 24 + n:25 + n]), R=[BK[2 + r_], B_small], W=[B_tB[r_]])
                                        P.op("dve", lambda e, r_=r_, N=N: e.tensor_tensor(
                                            out=t1[:, 0:N], in0=tA[r_][:, 0:N], in1=bank(4 + r_)[:, 0:N], op=ALU.mult),
                                            R=[B_tA[r_], BK[4 + r_]], W=[B_t1])
                                        P.op("dve", lambda e, r_=r_, n=n, N=N: e.scalar_tensor_tensor(
                                            out=t2[:, 0:N], in0=bank(6 + r_)[:, 0:N], scalar=cvp[:, 136 + n:137 + n],
                                            in1=tB[r_][:, 0:N], op0=ALU.add, op1=ALU.mult),
                                            R=[B_tB[r_], BK[6 + r_], B_small], W=[B_t2])
                                        P.op("dve", lambda e, n=n, c0=c0, N=N: e.tensor_tensor(
                                            out=mT[:, n, c0:c0 + N], in0=t1[:, 0:N], in1=t2[:, 0:N], op=ALU.add),
                                            R=[B_t1, B_t2], W=[B_mT[ti]])
                            P.barrier()
                        with ExitStack() as s4:
                            wo = sb("wo", [128, 8, 1024], BF16, s4)
                            B_wo = Buf()
                            for hh in range(2):
                                P.dma("pool", lambda e, hh=hh: e.dma_start(out=wo[:, :, 512 * hh:512 * (hh + 1)],
                                                                           in_=w_o_v[:, :, 512 * hh:512 * (hh + 1)]), W=[B_wo])
                            for ti, (c0, N, oc0) in enumerate(tiles):
                                for n in range(8):
                                    r_ = nxt("u", 4)
                                    for kc in range(8):
                                        P.op("pe", lambda e, kc=kc, n=n, r_=r_, c0=c0, N=N: e.matmul(
                                            bank(r_)[:, 0:N], lhsT=wo[:, kc, n * 128:(n + 1) * 128], rhs=mT[:, kc, c0:c0 + N],
                                            start=(kc == 0), stop=(kc == 7)), R=[B_wo, B_mT[ti]], W=[BK[r_]])
                                    per_sample(
                                        N,
                                        lambda n=n, r_=r_, c0=c0, ti=ti: P.op("dve", lambda e: e.scalar_tensor_tensor(
                                            out=xT[:, n, c0:c0 + 512], in0=bank(r_)[:, 0:512], scalar=g1(n),
                                            in1=xT[:, n, c0:c0 + 512], op0=ALU.mult, op1=ALU.add),
                                            R=[BK[r_], B_mod, B_xT[ti]], W=[B_xT[ti]]),
                                        lambda s_, o_, n=n, r_=r_, c0=c0, ti=ti: P.op("dve", lambda e: e.scalar_tensor_tensor(
                                            out=xT[:, n, c0 + o_:c0 + o_ + 32], in0=bank(r_)[:, o_:o_ + 32], scalar=g1(n, 1 + s_),
                                            in1=xT[:, n, c0 + o_:c0 + o_ + 32], op0=ALU.mult, op1=ALU.add),
                                            R=[BK[r_], B_mod, B_xT[ti]], W=[B_xT[ti]]))
                            P.barrier()

                    for ti, (c0, N, oc0) in enumerate(tiles):
                        col_stats(ti, lambda kc, c0=c0, N=N: xT[:, kc, c0:c0 + N], 8, 1.0 / D)
                        for kc in range(8):
                            a = nxt("t")
                            P.op("dve", lambda e, kc=kc, a=a, c0=c0, N=N: e.tensor_tensor(
                                out=tA[a][:, 0:N], in0=xT[:, kc, c0:c0 + N], in1=rsT[:, 0:N], op=ALU.mult),
                                R=[B_xT[ti], B_rsT], W=[B_tA[a]])
                            per_sample(
                                N,
                                lambda kc=kc, a=a, c0=c0, ti=ti: P.op("act", lambda e: e.activation(
                                    out=hTh[:, kc, c0:c0 + 512], in_=tA[a][:, 0:512], func=AF.Identity, scale=a2T[:, kc, 0:1],
                                    bias=sh2(kc)), R=[B_tA[a], B_mod], W=[B_hTh[ti]]),
                                lambda s_, o_, kc=kc, a=a, c0=c0, ti=ti: P.op("dve", lambda e: e.tensor_scalar(
                                    out=hTh[:, kc, c0 + o_:c0 + o_ + 32], in0=tA[a][:, o_:o_ + 32],
                                    scalar1=a2T[:, kc, 1 + s_:2 + s_], scalar2=sh2(kc, 1 + s_), op0=ALU.mult, op1=ALU.add),
                                    R=[B_tA[a], B_mod], W=[B_hTh[ti]]))
                    P.barrier()

                    with ExitStack() as s5:
                        wG = [sb("wG%d" % i, [128, 8, 384], BF16, s5) for i in range(2)]
                        wU = [sb("wU%d" % i, [128, 8, 384], BF16, s5) for i in range(2)]
                        wD = [sb("wD%d" % i, [128, 3, 1024], BF16, s5) for i in range(2)]
                        aT = [sb("aT%d" % i, [128, 3, 512], BF16, s5) for i in range(2)]
                        B_wf = [Buf(), Buf()]
                        B_aT = [Buf(), Buf()]
                        for gi, (h0, ng) in enumerate(FFN_GROUPS):
                            wi = gi % 2
                            P.dma("pool", lambda e, wi=wi, h0=h0, ng=ng: e.dma_start(
                                out=wG[wi][:, :, 0:128 * ng], in_=w_f1_v[:, :, 128 * h0:128 * (h0 + ng)]), W=[B_wf[wi]])
                            P.dma("pool", lambda e, wi=wi, h0=h0, ng=ng: e.dma_start(
                                out=wU[wi][:, :, 0:128 * ng], in_=w_f1_v[:, :, FH + 128 * h0:FH + 128 * (h0 + ng)]),
                                W=[B_wf[wi]])
                            P.dma("pool", lambda e, wi=wi, h0=h0, ng=ng: e.dma_start(
                                out=wD[wi][:, 0:ng, :], in_=w_f2_v[:, h0:h0 + ng, :]), W=[B_wf[wi]])
                            for ti, (c0, N, oc0) in enumerate(tiles):
                                ai = nxt("a")
                                for i in range(ng):
                                    r_ = nxt("b")
                                    for kc in range(8):
                                        P.op("pe", lambda e, kc=kc, i=i, wi=wi, r_=r_, c0=c0, N=N: e.matmul(
                                            bank(r_)[:, 0:N], lhsT=wG[wi][:, kc, 128 * i:128 * (i + 1)],
                                            rhs=hTh[:, kc, c0:c0 + N], start=(kc == 0), stop=(kc == 7)),
                                            R=[B_wf[wi], B_hTh[ti]], W=[BK[r_]])
                                    for kc in range(8):
                                        P.op("pe", lambda e, kc=kc, i=i, wi=wi, r_=r_, c0=c0, N=N: e.matmul(
                                            bank(2 + r_)[:, 0:N], lhsT=wU[wi][:, kc, 128 * i:128 * (i + 1)],
                                            rhs=hTh[:, kc, c0:c0 + N], start=(kc == 0), stop=(kc == 7)),
                                            R=[B_wf[wi], B_hTh[ti]], W=[BK[2 + r_]])
                                    P.op("act", lambda e, r_=r_, N=N: e.activation(out=tA[r_][:, 0:N], in_=bank(r_)[:, 0:N],
                                                                                   func=AF.Silu), R=[BK[r_]], W=[B_tA[r_]])
                                    P.op("dve", lambda e, r_=r_, i=i, ai=ai, N=N: e.tensor_tensor(
                                        out=aT[ai][:, i, 0:N], in0=tA[r_][:, 0:N], in1=bank(2 + r_)[:, 0:N], op=ALU.mult),
                                        R=[B_tA[r_], BK[2 + r_]], W=[B_aT[ai]])
                                for n in range(8):
                                    r4 = 4 + nxt("c", 4)
                                    for i in range(ng):
                                        P.op("pe", lambda e, i=i, n=n, wi=wi, ai=ai, r4=r4, N=N, ng=ng: e.matmul(
                                            bank(r4)[:, 0:N], lhsT=wD[wi][:, i, n * 128:(n + 1) * 128], rhs=aT[ai][:, i, 0:N],
                                            start=(i == 0), stop=(i == ng - 1)), R=[B_wf[wi], B_aT[ai]], W=[BK[r4]])
                                    per_sample(
                                        N,
                                        lambda n=n, r4=r4, c0=c0, ti=ti: P.op("dve", lambda e: e.scalar_tensor_tensor(
                                            out=xT[:, n, c0:c0 + 512], in0=bank(r4)[:, 0:512], scalar=g2(n),
                                            in1=xT[:, n, c0:c0 + 512], op0=ALU.mult, op1=ALU.add),
                                            R=[BK[r4], B_mod, B_xT[ti]], W=[B_xT[ti]]),
                                        lambda s_, o_, n=n, r4=r4, c0=c0, ti=ti: P.op("dve", lambda e: e.scalar_tensor_tensor(
                                            out=xT[:, n, c0 + o_:c0 + o_ + 32], in0=bank(r4)[:, o_:o_ + 32], scalar=g2(n, 1 + s_),
                                            in1=xT[:, n, c0 + o_:c0 + o_ + 32], op0=ALU.mult, op1=ALU.add),
                                            R=[BK[r4], B_mod, B_xT[ti]], W=[B_xT[ti]]))
                        P.barrier()

                    with ExitStack() as s6:
                        yst = [sb("yst%d" % i, [128, D], F32, s6) for i in range(2)]
                        B_yst = [Buf(), Buf()]
                        for ti, (c0, N, oc0) in enumerate(tiles):
                            col_stats(ti, lambda kc, c0=c0, N=N: xT[:, kc, c0:c0 + N], 8, 1.0 / D)
                            for kc in range(8):
                                P.op("dve", lambda e, kc=kc, c0=c0, N=N: e.scalar_tensor_tensor(
                                    out=xT[:, kc, c0:c0 + N], in0=xT[:, kc, c0:c0 + N], scalar=gn[:, 16 + kc:17 + kc],
                                    in1=rsT[:, 0:N], op0=ALU.mult, op1=ALU.mult), R=[B_xT[ti], B_rsT, B_small], W=[B_xT[ti]])
                            for bq in range(N // 128):
                                col = c0 + 128 * bq
                                yi = nxt("d")
                                for kc in range(8):
                                    P.op("pe", lambda e, kc=kc, col=col, yi=yi: e.transpose(
                                        psum[:, 1024 * yi + 128 * kc + 2048:1024 * yi + 128 * (kc + 1) + 2048],
                                        xT[:, kc, col:col + 128], ident), R=[B_xT[ti], B_cst], W=[BK[4 + 2 * yi]])
                                P.op("act", lambda e, yi=yi: e.activation(out=yst[yi][:], in_=psum[:, 2048 + 1024 * yi:3072 + 1024 * yi],
                                                                          func=AF.Copy), R=[BK[4 + 2 * yi]], W=[B_yst[yi]])
                                ydst = y_s if N == 128 else y_own[8 * hf + col // 128]
                                P.dma("act", lambda e, yi=yi, ydst=ydst: e.dma_start(out=ydst, in_=yst[yi][:]), R=[B_yst[yi]])
                        P.barrier()

        if STAGE < 3 or DBG_OT:
            P.dma("sp", lambda e: e.dma_start(out=dbg_o, in_=oT[:]), R=B_oT)
        P.finish()
    return nc


_CACHE = {}


def _consts():
    c = np.zeros((128, 5, 128), np.float32)
    c[:, 0, :] = np.eye(128, dtype=np.float32)
    c[:, 1, :] = 1.0
    s = np.arange(128)
    c[:, 2, :] = (s[:, None] <= s[None, :]).astype(np.float32)
    c[:, 3, :] = (s[:, None] > s[None, :]).astype(np.float32)
    c[:, 4, :] = (s[None, :] - s[:, None]).astype(np.float32)
    return c


def kernel(x_prompt, x_sample, c_prompt, c_sample, cache_k, cache_v, cache_logf, state_conv,
           page_table, rms1_g, rms2_g, w_ada, b_ada, w_in, b_in, dw_w, dw_b, ln_g, ln_b,
           w_pa, w_pb, b_pb, w_o, w_ffn_in, w_ffn_out, final_g):
    f32 = np.float32
    A = lambda a: np.ascontiguousarray(np.asarray(a))
    x_prompt, x_sample = A(x_prompt), A(x_sample)
    if "nc" not in _CACHE:
        _CACHE["nc"] = build_program()
    nc = _CACHE["nc"]

    def fm(v, nchunk):
        return A(np.asarray(v, f32).reshape(nchunk, 128).T)

    b_in0 = np.asarray(b_in, f32)[0]
    b_fm = np.concatenate([fm(b_in0[Q_OFF:Q_OFF + 512], 4), fm(b_in0[K_OFF:K_OFF + 512], 4),
                           fm(b_in0[GLU_OFF:GLU_OFF + 1024], 8), fm(b_in0[GA_OFF:GA_OFF + 1024], 8),
                           fm(b_in0[GB_OFF:GB_OFF + 1024], 8)], axis=1)
    b_rows = np.concatenate([b_in0[K_OFF:K_OFF + 512], b_in0[V_OFF:V_OFF + 512], b_in0[F_OFF:F_OFF + 8]])[None, :]
    gains = np.concatenate([fm(np.asarray(rms1_g)[0], 8), fm(np.asarray(rms2_g)[0], 8), fm(np.asarray(final_g), 8)], axis=1)
    dwT = np.asarray(dw_w, f32)[0].T.reshape(4, 128, CK).transpose(1, 0, 2).reshape(128, 4 * CK)
    convpar = np.concatenate([dwT, fm(np.asarray(dw_b)[0], 4), fm(np.asarray(ln_g)[0], 4), fm(np.asarray(ln_b)[0], 4),
                              fm(np.asarray(b_pb)[0], 8)], axis=1)
    b_adaT = fm(np.asarray(b_ada)[0], 48)
    ck = A(cache_k).reshape(NPHYS * 16, 8 * 512)
    cv = A(cache_v).reshape(NPHYS * 16, 8 * 512)
    cl = A(cache_logf).reshape(NPHYS * 16, 64)
    if STAGE < 4:
        ck, cv, cl = A(ck[:16]), A(cv[:16]), A(cl[:16])
    shared = dict(w_ada=A(np.asarray(w_ada)[0]), b_adaT=A(b_adaT), w_in=A(np.asarray(w_in)[0]), b_fm=A(b_fm),
                  b_rows=A(b_rows), gains=A(gains), convpar=A(convpar), w_pa=A(np.asarray(w_pa)[0]),
                  w_pb=A(np.asarray(w_pb)[0]), w_o=A(np.asarray(w_o)[0]), w_f1=A(np.asarray(w_ffn_in)[0]),
                  w_f2=A(np.asarray(w_ffn_out)[0]), cache_k=ck, cache_v=cv, cache_l=cl, cst=_consts(),
                  shi=A((np.arange(128) % 16).astype(f32)[:, None]),
                  bmask=A((np.arange(32)[:, None] // 4 == np.arange(512)[None, :] // 64).astype(f32)))
    pt = np.asarray(page_table)
    sc = np.asarray(state_conv, f32)[0]
    in_maps = []
    for c in range(8):
        b, j = c // 4, c % 4
        ob = own_blocks(j)
        xo = np.zeros((NOWN, 160, D), f32)
        hv = np.zeros((128, NOWN), f32)
        for q, blk in enumerate(ob):
            lo = blk * 128 - 32
            if lo >= 0:
                xo[q] = x_prompt[b, lo:lo + 160]
                hv[:, q] = 1.0
            else:
                xo[q, 32:] = x_prompt[b, 0:128]
        xs = np.zeros((128, D), f32)
        cvec = np.zeros((5, D), f32)
        cvec[0] = np.asarray(c_prompt)[b]
        ptr = np.zeros((128, 32), np.int32)
        stT = np.zeros((128, 4, 4, 30), f32)
        strows = np.zeros((4, 26, CW), f32)
        for s in range(4):
            gs = 4 * c + s
            xs[32 * s:32 * s + 4] = x_sample[gs]
            cvec[1 + s] = np.asarray(c_sample)[gs]
            for g in range(8):
                ptr[:, s * 8 + g] = pt[gs, 8 * g + np.arange(128) // 16]
            stT[:, :, s, :] = sc[gs].T.reshape(4, 128, 30).transpose(1, 0, 2)
            strows[s] = sc[gs, 4:30]
        cT = A(cvec.reshape(5, 8, 128).transpose(2, 1, 0))
        kbo = np.tile((128.0 * (np.arange(4) - j)).astype(f32)[None, :], (128, 1))
        m = dict(shared)
        m.update(xseq=A(x_prompt[b]), xown=xo, xs=xs, cT=cT, pt_rep=ptr, stT=stT, st_rows=strows,
                 kboff=A(kbo), hval=hv)
        in_maps.append(m)

    ncores = _CACHE.get("ncores", 8)
    res = run_bass_kernel_spmd(nc, in_maps[:ncores], core_ids=list(range(ncores)))
    R = list(res.results) + [res.results[0]] * (8 - ncores)
    _CACHE["last"] = R
    y_prompt = np.zeros((2, SEQ, D), f32)
    k_p = np.zeros((2, SEQ, 512), f32)
    v_p = np.zeros((2, SEQ, 512), f32)
    l_p = np.zeros((2, SEQ, 8), f32)
    conv_p = np.zeros((1, 2, 30, CW), f32)
    y_sample = np.zeros((32, 4, D), f32)
    k_s = np.zeros((32, 4, 512), f32)
    v_s = np.zeros((32, 4, 512), f32)
    l_s = np.zeros((32, 4, 8), f32)
    conv_s = np.zeros((1, 32, 30, CW), f32)
    for c in range(8):
        b, j = c // 4, c % 4
        r = R[c]
        for q, blk in enumerate(own_blocks(j)):
            sl = slice(blk * 128, blk * 128 + 128)
            y_prompt[b, sl] = r["y_own"][q]
            k_p[b, sl] = r["k_own"][q]
            v_p[b, sl] = r["v_own"][q]
            l_p[b, sl] = r["lf_own"][q]
        if j == 3:
            conv_p[0, b] = r["convp"][2:32]
        for s in range(4):
            gs = 4 * c + s
            y_sample[gs] = r["y_s"][32 * s:32 * s + 4]
            k_s[gs] = r["ks"][32 * s:32 * s + 4]
            v_s[gs] = r["vs"][32 * s:32 * s + 4]
            l_s[gs] = r["lfs"][32 * s:32 * s + 4]
            conv_s[0, gs] = r["convs"][s]
    return (y_prompt, y_sample,
            k_p.reshape(1, 2, NB, 128, H, DH), v_p.reshape(1, 2, NB, 128, H, DH), l_p.reshape(1, 2, NB, 128, H),
            conv_p, k_s.reshape(1, 32, 4, H, DH), v_s.reshape(1, 32, 4, H, DH), l_s.reshape(1, 32, 4, H), conv_s)
```

```python
from contextlib import ExitStack

import numpy as np
import concourse.bass as bass
import concourse.mybir as mybir
from concourse.bass_utils import run_bass_kernel_spmd

F32 = mybir.dt.float32
BF16 = mybir.dt.bfloat16
I32 = mybir.dt.int32
AF = mybir.ActivationFunctionType
ALU = mybir.AluOpType

D = 1024
SEQ = 8192
NB = SEQ // 128
H = 8
DH = 64
NIN = 4616
Q_OFF, K_OFF, V_OFF, F_OFF, GLU_OFF, GA_OFF, GB_OFF = 0, 512, 1024, 1536, 1544, 2568, 3592
CW = 512
CK = 31
FH = 2816
EPS = 1e-6
SCALE = DH ** -0.5
BIG = 30000.0
NPHYS = 2560
NOWN = 16
ENG = ("pe", "act", "dve", "pool", "sp")

STAGE = 4
DBG_OT = False
SKIP_P5 = False
SKIP_ATT = False
SDBG = 0
ATT_LAG = 2


class _Stop(Exception):
    pass


class Buf:
    __slots__ = ("w", "r", "sem", "semv", "name")

    def __init__(self, name=""):
        self.w = None
        self.r = []
        self.sem = None
        self.semv = 0
        self.name = name


class _Rec:
    def __init__(self):
        self.call = None

    def __getattr__(self, name):
        def f(*a, **k):
            self.call = (name, a, k)
            return self
        return f


def _record(fn):
    r = _Rec()
    fn(r)
    assert r.call is not None
    return r.call


class Prog:
    def __init__(self, nc, es):
        self.nc = nc
        self.es = es
        self.items = {e: [] for e in ENG}
        self.cnt = {e: 0 for e in ENG}
        self.sems = {e: es.enter_context(nc.semaphore("c_" + e)) for e in ENG}
        self.known = {e: {} for e in ENG}
        self.pending = {e: [] for e in ENG}
        self.dma_bufs = []
        self.nsem = 0

    def _deps(self, R, W):
        deps = []
        for b in R:
            if b.w is not None:
                deps.append(b.w)
        for b in W:
            if b.w is not None:
                deps.append(b.w)
            deps.extend(b.r)
        return deps

    def _waits(self, eng, deps):
        deps = list(deps) + self.pending[eng]
        self.pending[eng] = []
        best = {}
        for (sem, val, src) in deps:
            if eng == "pe" and src == "pe":
                continue
            k = id(sem)
            if self.known[eng].get(k, 0) >= val:
                continue
            if k not in best or best[k][1] < val:
                best[k] = (sem, val)
        for k, (sem, val) in best.items():
            self.known[eng][k] = val
        return list(best.values())

    def op(self, eng, fn, R=(), W=()):
        waits = self._waits(eng, self._deps(R, W))
        self.cnt[eng] += 1
        ev = (self.sems[eng], self.cnt[eng], eng)
        for b in R:
            b.r.append(ev)
        for b in W:
            b.w = ev
            b.r = []
        self.items[eng].append((waits, _record(fn), self.sems[eng], 1))

    def dma(self, q, fn, R=(), W=(), sb=None):
        waits = self._waits(q, self._deps(R, W))
        if sb is None:
            sb = W[0] if W else R[0]
        if sb.sem is None:
            sb.sem = self.es.enter_context(self.nc.semaphore("d%d" % self.nsem))
            self.nsem += 1
            self.dma_bufs.append(sb)
        sb.semv += 16
        ev = (sb.sem, sb.semv, "dma")
        for b in R:
            b.r.append(ev)
        for b in W:
            b.w = ev
            b.r = []
        self.items[q].append((waits, _record(fn), sb.sem, 16))

    def barrier(self):
        evs = [(self.sems[e], self.cnt[e], e) for e in ENG if self.cnt[e] > 0]
        evs += [(b.sem, b.semv, "dma") for b in self.dma_bufs]
        for e in ENG:
            self.pending[e] = list(evs)

    def finish(self):
        self.barrier()
        nc = self.nc
        with nc.Block() as block:
            def run(eng, items, final):
                for waits, (name, a, k), sem, inc in items:
                    for (s, v) in waits:
                        eng.wait_ge(s, v)
                    getattr(eng, name)(*a, **k).then_inc(sem, inc)
                for (s, v, _) in final:
                    eng.wait_ge(s, v)

            fin = self.pending["sp"]
            block.tensor(lambda e: run(e, self.items["pe"], []))
            block.scalar(lambda e: run(e, self.items["act"], []))
            block.vector(lambda e: run(e, self.items["dve"], []))
            block.gpsimd(lambda e: run(e, self.items["pool"], []))
            block.sync(lambda e: run(e, self.items["sp"], fin))


def own_blocks(j):
    return [16 * m + 4 * i + j for m in range(4) for i in range(4)]


FFN_GROUPS = []
_h = 0
for _n in (4, 4, 4, 4, 4, 2):
    FFN_GROUPS.append((_h, _n))
    _h += _n
QBASE = [0, 16, 48, 96]


def build_program():
    nc = bass.Bass("TRN2", target_bir_lowering=False)

    def din(name, shape, dt=F32):
        return nc.dram_tensor(name, list(shape), dt, kind="ExternalInput").ap()

    def dout(name, shape, dt=F32):
        return nc.dram_tensor(name, list(shape), dt, kind="ExternalOutput").ap()

    xseq = din("xseq", [SEQ, D])
    xown = din("xown", [NOWN, 160, D])
    xs_d = din("xs", [128, D])
    cT_d = din("cT", [128, 8, 5])
    w_ada = din("w_ada", [D, 6 * D])
    b_adaT = din("b_adaT", [128, 48])
    w_in = din("w_in", [D, NIN])
    b_fm = din("b_fm", [128, 32])
    b_rows = din("b_rows", [1, 1032])
    gains = din("gains", [128, 24])
    convp_d = din("convpar", [128, 4 * CK + 20])
    w_pa = din("w_pa", [512, D])
    w_pb = din("w_pb", [512, D])
    w_o = din("w_o", [D, D])
    w_f1 = din("w_f1", [D, 2 * FH])
    w_f2 = din("w_f2", [FH, D])
    NCR = NPHYS * 16 if STAGE >= 4 else 16
    cache_k = din("cache_k", [NCR, 8 * 512])
    cache_v = din("cache_v", [NCR, 8 * 512])
    cache_l = din("cache_l", [NCR, 64])
    pt_rep = din("pt_rep", [128, 32], I32)
    shi_d = din("shi", [128, 1])
    stT_d = din("stT", [128, 4, 4, 30])
    st_rows = din("st_rows", [4, 26, CW])
    kboff_d = din("kboff", [128, 4])
    hval_d = din("hval", [128, NOWN])
    bmask_d = din("bmask", [32, 512])
    cst_d = din("cst", [128, 5, 128])

    y_own = dout("y_own", [NOWN, 128, D])
    y_s = dout("y_s", [128, D])
    k_own = dout("k_own", [NOWN, 128, 512])
    v_own = dout("v_own", [NOWN, 128, 512])
    lf_own = dout("lf_own", [NOWN, 128, 8])
    convp_o = dout("convp", [32, CW])
    ks_o = dout("ks", [128, 512])
    vs_o = dout("vs", [128, 512])
    lfs_o = dout("lfs", [128, 8])
    convs_o = dout("convs", [4, 30, CW])
    dbg_o = dout("dbg", [64, 8, 2048 + 128], BF16) if (DBG_OT or STAGE < 3) else None
    dbg2 = dout("dbg2", [128, 2048], F32) if DBG_OT else None

    es = ExitStack()
    with es:
        P = Prog(nc, es)

        _uid = [0]

        def sb(name, shape, dt, stack=es):
            _uid[0] += 1
            return stack.enter_context(nc.sbuf_tensor("%s_s%d" % (name, _uid[0]), list(shape), dt))

        psum = es.enter_context(nc.psum_tensor("psum", [128, 8 * 512], F32))
        psum_bf = psum[:, :].bitcast(BF16)

        def bank(i, n=512, off=0):
            return psum[:, i * 512 + off: i * 512 + off + n]

        cst = sb("cst", [128, 5, 128], F32)
        identb = sb("identb", [128, 128], BF16)
        modT = sb("modT", [128, 48, 5], F32)
        a1T = sb("a1T", [128, 8, 5], F32)
        a2T = sb("a2T", [128, 8, 5], F32)
        bfm = sb("bfm", [128, 32], F32)
        gn = sb("gn", [128, 24], F32)
        cvp = sb("cvp", [128, 4 * CK + 20], F32)
        kboff = sb("kboff_t", [128, 4], F32)
        hval = sb("hval_t", [128, NOWN], F32)
        maskT = sb("maskT", [128, 4, 128], BF16)
        oT = sb("oT", [64, 8, 2048 + 128], BF16)
        KTs = sb("KTs", [128, 4, 128], BF16)
        QTs = sb("QTs", [128, 4, 128], BF16)
        Vs_bf = sb("Vs_bf", [128, 512], BF16)
        lfs_t = sb("lfs_t", [128, 8], F32)
        B_cst = Buf("cst")
        B_mod = Buf("mod")
        B_small = Buf("small")
        B_mask = Buf("mask")
        B_oT = [Buf("oT%d" % i) for i in range(5)]
        B_KTs = Buf("KTs")
        B_QTs = Buf("QTs")
        B_Vs = Buf("Vs")
        B_lfs = Buf("lfs")

        ident = cst[:, 0, :]
        ones = cst[:, 1, :]
        U_incl = cst[:, 2, :]
        U_strict = cst[:, 3, :]
        tmins = cst[:, 4, :]

        P.dma("sp", lambda e: e.dma_start(out=cst[:], in_=cst_d), W=[B_cst])
        for (t, d_) in ((bfm, b_fm), (gn, gains), (cvp, convp_d), (kboff, kboff_d), (hval, hval_d)):
            P.dma("sp", lambda e, t=t, d_=d_: e.dma_start(out=t[:], in_=d_), W=[B_small])
        P.op("dve", lambda e: e.tensor_copy(out=identb[:], in_=ident), R=[B_cst], W=[B_small])
        for r in range(4):
            P.op("dve", lambda e, r=r: e.tensor_scalar(out=maskT[:, r, :], in0=tmins, scalar1=kboff[:, r:r + 1],
                                                       scalar2=0.0, op0=ALU.subtract, op1=ALU.is_ge),
                 R=[B_cst, B_small], W=[B_mask])
        P.op("dve", lambda e: e.tensor_scalar(out=maskT[:], in0=maskT[:], scalar1=-1.0, scalar2=BIG,
                                              op0=ALU.add, op1=ALU.mult), R=[B_mask], W=[B_mask])

        with ExitStack() as s0:
            cTf = sb("cTf", [128, 8, 5], F32, s0)
            cTb = sb("cTb", [128, 8, 5], BF16, s0)
            badaT = sb("badaT", [128, 48], F32, s0)
            wada = [sb("wada%d" % i, [128, 8, 1024], BF16, s0) for i in range(2)]
            B_c = Buf()
            B_wada = [Buf(), Buf()]
            B_ps0 = Buf()
            P.dma("sp", lambda e: e.dma_start(out=cTf[:], in_=cT_d), W=[B_c])
            P.dma("sp", lambda e: e.dma_start(out=badaT[:], in_=b_adaT), W=[B_c])
            P.op("act", lambda e: e.activation(out=cTb[:], in_=cTf[:], func=AF.Silu), R=[B_c], W=[B_c])
            wada_v = w_ada.rearrange("(kc p) n -> p kc n", p=128)
            for pc in range(6):
                sl = pc % 2
                P.dma("pool", lambda e, pc=pc, sl=sl: e.dma_start(
                    out=wada[sl][:], in_=wada_v[:, :, pc * 1024:(pc + 1) * 1024]), W=[B_wada[sl]])
                for nch in range(8):
                    ch = pc * 8 + nch
                    for kc in range(8):
                        P.op("pe", lambda e, sl=sl, nch=nch, kc=kc, ch=ch: e.matmul(
                            bank(0, 5, ch * 5), lhsT=wada[sl][:, kc, nch * 128:(nch + 1) * 128], rhs=cTb[:, kc, :],
                            start=(kc == 0), stop=(kc == 7)), R=[B_wada[sl], B_c], W=[B_ps0])
            P.op("dve", lambda e: e.tensor_tensor(
                out=modT[:], in0=bank(0, 240).rearrange("p (c v) -> p c v", v=5),
                in1=badaT[:, :].unsqueeze(2).to_broadcast([128, 48, 5]), op=ALU.add), R=[B_ps0, B_c], W=[B_mod])
            P.op("dve", lambda e: e.scalar_tensor_tensor(
                out=a1T[:], in0=modT[:, 8:16, :], scalar=1.0, in1=gn[:, 0:8].unsqueeze(2).to_broadcast([128, 8, 5]),
                op0=ALU.add, op1=ALU.mult), R=[B_mod, B_small], W=[B_mod])
            P.op("dve", lambda e: e.scalar_tensor_tensor(
                out=a2T[:], in0=modT[:, 32:40, :], scalar=1.0, in1=gn[:, 8:16].unsqueeze(2).to_broadcast([128, 8, 5]),
                op0=ALU.add, op1=ALU.mult), R=[B_mod, B_small], W=[B_mod])
            P.barrier()

        def sh1(kc, v=0):
            return modT[:, 0 + kc, v:v + 1]

        def g1(kc, v=0):
            return modT[:, 16 + kc, v:v + 1]

        def sh2(kc, v=0):
            return modT[:, 24 + kc, v:v + 1]

        def g2(kc, v=0):
            return modT[:, 40 + kc, v:v + 1]

        w_in_v = w_in.rearrange("(kc p) n -> p kc n", p=128)

        class NormPipe:
            def __init__(self, stack, nx=3):
                self.nx = nx
                self.xsl = [sb("xsl%d" % i, [128, D], F32, stack) for i in range(nx)]
                self.xn = [sb("xn%d" % i, [128, D], BF16, stack) for i in range(2)]
                self.junk = sb("junk", [128, D], BF16, stack)
                self.ssq = sb("ssq", [128, 4], F32, stack)
                self.rs = sb("rs_t", [128, 4], F32, stack)
                self.B_x = [Buf() for _ in range(nx)]
                self.B_xn = [Buf() for _ in range(2)]
                self.B_junk = Buf()
                self.B_ss = [Buf() for _ in range(4)]
                self.B_rs = [Buf() for _ in range(4)]
                self.TPs = [psum_bf[:, 2048 * i:2048 * (i + 1)].rearrange("p (k t) -> p k t", t=256) for i in range(2)]
                self.B_TP = [Buf() for _ in range(2)]
                self.c = dict(x=0, xn=0, ss=0, tp=0)

            def load(self, src_ap, nrows):
                xi = self.c["x"] % self.nx
                self.c["x"] += 1
                P.dma("sp", lambda e: e.dma_start(out=self.xsl[xi][0:nrows, :], in_=src_ap), W=[self.B_x[xi]])
                return xi

            def rstd(self, xi, nrows, rstd_ap=None, B_rs=None):
                si = self.c["ss"] % 4
                self.c["ss"] += 1
                if rstd_ap is None:
                    rstd_ap, B_rs = self.rs[0:nrows, si:si + 1], self.B_rs[si]
                P.op("act", lambda e: e.activation(out=self.junk[0:nrows, :], in_=self.xsl[xi][0:nrows, :],
                                                   func=AF.Square, accum_out=self.ssq[0:nrows, si:si + 1]),
                     R=[self.B_x[xi]], W=[self.B_junk, self.B_ss[si]])
                P.op("act", lambda e: e.activation(out=self.ssq[0:nrows, si:si + 1], in_=self.ssq[0:nrows, si:si + 1],
                                                   func=AF.Sqrt, scale=1.0 / D, bias=EPS),
                     R=[self.B_ss[si]], W=[self.B_ss[si]])
                P.op("dve", lambda e: e.reciprocal(out=rstd_ap, in_=self.ssq[0:nrows, si:si + 1]),
                     R=[self.B_ss[si]], W=[B_rs])
                return rstd_ap, B_rs

            def normalize(self, xi, nrows, rstd_ap, B_rs):
                ni = self.c["xn"] % 2
                self.c["xn"] += 1
                P.op("act", lambda e: e.activation(out=self.xn[ni][0:nrows, :], in_=self.xsl[xi][0:nrows, :],
                                                   func=AF.Copy, scale=rstd_ap),
                     R=[self.B_x[xi], B_rs], W=[self.B_xn[ni]])
                return ni

            def new_tp(self):
                ti = self.c["tp"] % 2
                self.c["tp"] += 1
                return ti

            def transpose(self, ni, nrows, ti, col0):
                for kc in range(8):
                    P.op("pe", lambda e, kc=kc: e.transpose(self.TPs[ti][:, kc, col0:col0 + nrows],
                                                            self.xn[ni][0:nrows, kc * 128:(kc + 1) * 128],
                                                            identb[0:nrows, 0:nrows]),
                         R=[self.B_xn[ni], B_small], W=[self.B_TP[ti]])

            def evac_mod(self, ti, ncols, dst, B_dst, aT, shf, sample=False):
                if not sample:
                    for kc in range(8):
                        if kc % 2 == 0:
                            P.op("dve", lambda e, kc=kc: e.tensor_scalar(
                                out=dst(kc), in0=self.TPs[ti][:, kc, 0:ncols], scalar1=aT[:, kc, 0:1],
                                scalar2=shf(kc), op0=ALU.mult, op1=ALU.add), R=[self.B_TP[ti], B_mod], W=[B_dst])
                        else:
                            P.op("act", lambda e, kc=kc: e.activation(
                                out=dst(kc), in_=self.TPs[ti][:, kc, 0:ncols], func=AF.Identity,
                                scale=aT[:, kc, 0:1], bias=shf(kc)), R=[self.B_TP[ti], B_mod], W=[B_dst])
                else:
                    for kc in range(8):
                        for s in range(4):
                            c0 = 32 * s
                            P.op("dve", lambda e, kc=kc, s=s, c0=c0: e.tensor_scalar(
                                out=dst(kc)[:, c0:c0 + 32], in0=self.TPs[ti][:, kc, c0:c0 + 32],
                                scalar1=aT[:, kc, 1 + s:2 + s], scalar2=shf(kc, 1 + s), op0=ALU.mult, op1=ALU.add),
                                R=[self.B_TP[ti], B_mod], W=[B_dst])

        with ExitStack() as sA:
            KT = sb("KT", [128, 2, SEQ], BF16, sA)
            Vaug = sb("Vaug", [128, NB, 4, 65], BF16, sA)
            QT = sb("QT", [128, 2, 2048], BF16, sA)
            brow_bc = sb("brow_bc", [128, 1032], F32, sA)
            zf_all = sb("zf_all", [128, NB, 8], F32, sA)
            rstd_all = sb("rstd_all", [128, NB], F32, sA)
            rstd_o = sb("rstd_o", [128, NOWN + 1], F32, sA)
            bias_all = sb("bias_all", [128, 160, 8], F32, sA)
            zf_own = sb("zf_own", [128, NOWN + 1, 8], F32, sA)
            Wc = sb("Wc", [128, NB, 8], F32, sA)
            Tex = sb("Tex", [128, NB + 1, 8], F32, sA)
            onesrow = sb("onesrow", [128, NB], F32, sA)
            B_KT = [Buf() for _ in range(NB // 2)]
            B_V = [Buf() for _ in range(NB)]
            B_QT = [Buf() for _ in range(NOWN)]
            B_brow = Buf()
            B_zf = Buf()
            B_rstd = [Buf() for _ in range(NB)]
            B_rso = [Buf() for _ in range(NOWN + 1)]
            B_bias = Buf()
            B_zfo = Buf()
            B_lf = Buf()

            P.op("pool", lambda e: e.memset(Vaug[:, :, :, 64:65], 1.0), W=B_V)
            P.op("pool", lambda e: e.memset(onesrow[:], 1.0), W=[B_lf])
            P.op("pool", lambda e: e.memset(Tex[:, 0, :], 0.0), W=[B_lf])
            with ExitStack() as sb0:
                brow = sb("brow", [1, 1032], F32, sb0)
                B_br = Buf()
                B_pb = Buf()
                P.dma("sp", lambda e: e.dma_start(out=brow[:], in_=b_rows), W=[B_br])
                for (o, n) in ((0, 512), (512, 512), (1024, 8)):
                    P.op("pe", lambda e, o=o, n=n: e.matmul(bank(1, n), lhsT=ones[0:1, :], rhs=brow[0:1, o:o + n],
                                                            start=True, stop=True), R=[B_cst, B_br], W=[B_pb])
                    P.op("act", lambda e, o=o, n=n: e.activation(out=brow_bc[:, o:o + n], in_=bank(1, n), func=AF.Copy),
                         R=[B_pb], W=[B_brow])
                P.barrier()

            for g in range(2):
                with ExitStack() as s1:
                    NP = NormPipe(s1)
                    wqkv = sb("wqkv", [128, 8, 776], BF16, s1)
                    hT = [sb("hT%d" % i, [128, 8, 256], BF16, s1) for i in range(2)]
                    kst = [sb("kst%d" % i, [128, 256], F32, s1) for i in range(2)]
                    vst = [sb("vst%d" % i, [128, 256], F32, s1) for i in range(2)]
                    B_w = Buf()
                    B_hT = [Buf() for _ in range(2)]
                    B_pk = [Buf() for _ in range(1)]
                    B_pv = [Buf() for _ in range(3)]
                    B_kst = [Buf() for _ in range(2)]
                    B_vst = [Buf() for _ in range(2)]
                    c1 = dict(ht=0, pk=0, pv=0, kst=0, vst=0)
                    for (o, src, n) in ((0, Q_OFF + 256 * g, 256), (256, K_OFF + 256 * g, 256),
                                        (512, V_OFF + 256 * g, 256), (768, F_OFF, 8)):
                        P.dma("pool", lambda e, o=o, src=src, n=n: e.dma_start(out=wqkv[:, :, o:o + n],
                                                                               in_=w_in_v[:, :, src:src + n]), W=[B_w])
                    nv = 264 if g == 0 else 256
                    for bt in range(NB // 2):
                        ti = NP.new_tp()
                        for bb in range(2):
                            blk = 2 * bt + bb
                            xi = NP.load(xseq[blk * 128:(blk + 1) * 128, :], 128)
                            if g == 0:
                                NP.rstd(xi, 128, rstd_all[:, blk:blk + 1], B_rstd[blk])
                            ni = NP.normalize(xi, 128, rstd_all[:, blk:blk + 1], B_rstd[blk])
                            NP.transpose(ni, 128, ti, bb * 128)
                        hi = c1["ht"] % 2
                        c1["ht"] += 1
                        NP.evac_mod(ti, 256, lambda kc, hi=hi: hT[hi][:, kc, 0:256], B_hT[hi], a1T, sh1)
                        pk = 0
                        c1["pk"] += 1
                        for pl in range(2):
                            for kc in range(8):
                                P.op("pe", lambda e, pl=pl, kc=kc, pk=pk, hi=hi: e.matmul(
                                    bank(4 + pk, 256, pl * 256), lhsT=wqkv[:, kc, 256 + pl * 128:256 + (pl + 1) * 128],
                                    rhs=hT[hi][:, kc, 0:256], start=(kc == 0), stop=(kc == 7)),
                                    R=[B_w, B_hT[hi]], W=[B_pk[pk]])
                        for pl in range(2):
                            P.op("dve", lambda e, pl=pl, pk=pk, bt=bt, g=g: e.tensor_scalar(
                                out=KT[:, pl, bt * 256:(bt + 1) * 256], in0=bank(4 + pk, 256, pl * 256),
                                scalar1=bfm[:, 4 + 2 * g + pl:5 + 2 * g + pl], scalar2=None, op0=ALU.add),
                                R=[B_pk[pk], B_small], W=[B_KT[bt]])
                        for bb in range(2):
                            blk = 2 * bt + bb
                            pv = c1["pv"] % 3
                            c1["pv"] += 1
                            for kc in range(8):
                                P.op("pe", lambda e, bb=bb, kc=kc, pv=pv, hi=hi, nv=nv: e.matmul(
                                    bank(5 + pv, nv, 0), lhsT=hT[hi][:, kc, bb * 128:(bb + 1) * 128],
                                    rhs=wqkv[:, kc, 512:512 + nv], start=(kc == 0), stop=(kc == 7)),
                                    R=[B_w, B_hT[hi]], W=[B_pv[pv]])
                            P.op("dve", lambda e, pv=pv, blk=blk, g=g: e.tensor_tensor(
                                out=Vaug[:, blk, :, 0:64],
                                in0=bank(5 + pv, 256, 0).rearrange("p (h d) -> p h d", d=64),
                                in1=brow_bc[:, 512 + 256 * g:512 + 256 * (g + 1)].rearrange("p (h d) -> p h d", d=64),
                                op=ALU.add), R=[B_pv[pv], B_brow], W=[B_V[blk]])
                            if g == 0:
                                P.op("dve", lambda e, pv=pv, blk=blk: e.tensor_tensor(
                                    out=zf_all[:, blk, :], in0=bank(5 + pv, 8, 256),
                                    in1=brow_bc[:, 1024:1032], op=ALU.add), R=[B_pv[pv], B_brow], W=[B_zf])

                    for q in range(NOWN + 1):
                        sample = (q == NOWN)
                        ti = NP.new_tp()
                        xi = NP.load(xs_d if sample else xown[q, 32:160, :], 128)
                        if g == 0:
                            NP.rstd(xi, 128, rstd_o[:, q:q + 1], B_rso[q])
                        ni = NP.normalize(xi, 128, rstd_o[:, q:q + 1], B_rso[q])
                        NP.transpose(ni, 128, ti, 0)
                        hi = c1["ht"] % 2
                        c1["ht"] += 1
                        NP.evac_mod(ti, 128, lambda kc, hi=hi: hT[hi][:, kc, 0:128], B_hT[hi], a1T, sh1, sample=sample)
                        pk = 0
                        c1["pk"] += 1
                        for pl in range(2):
                            for kc in range(8):
                                P.op("pe", lambda e, pl=pl, kc=kc, pk=pk, hi=hi: e.matmul(
                                    bank(4 + pk, 128, pl * 128), lhsT=wqkv[:, kc, pl * 128:(pl + 1) * 128],
                                    rhs=hT[hi][:, kc, 0:128], start=(kc == 0), stop=(kc == 7)),
                                    R=[B_w, B_hT[hi]], W=[B_pk[pk]])
                        if sample:
                            for pl in range(2):
                                for kc in range(8):
                                    P.op("pe", lambda e, pl=pl, kc=kc, pk=pk, hi=hi: e.matmul(
                                        bank(4 + pk, 128, 256 + pl * 128),
                                        lhsT=wqkv[:, kc, 256 + pl * 128:256 + (pl + 1) * 128],
                                        rhs=hT[hi][:, kc, 0:128], start=(kc == 0), stop=(kc == 7)),
                                        R=[B_w, B_hT[hi]], W=[B_pk[pk]])
                        for pl in range(2):
                            qdst = QTs[:, 2 * g + pl, :] if sample else QT[:, pl, q * 128:(q + 1) * 128]
                            P.op("dve", lambda e, pl=pl, pk=pk, qdst=qdst, g=g: e.tensor_scalar(
                                out=qdst, in0=bank(4 + pk, 128, pl * 128),
                                scalar1=bfm[:, 2 * g + pl:2 * g + pl + 1], scalar2=None, op0=ALU.add),
                                R=[B_pk[pk], B_small], W=[B_QTs if sample else B_QT[q]])
                            if sample:
                                P.op("dve", lambda e, pl=pl, pk=pk, g=g: e.tensor_scalar(
                                    out=KTs[:, 2 * g + pl, :], in0=bank(4 + pk, 128, 256 + pl * 128),
                                    scalar1=bfm[:, 4 + 2 * g + pl:5 + 2 * g + pl], scalar2=None, op0=ALU.add),
                                    R=[B_pk[pk], B_small], W=[B_KTs])
                        pv = c1["pv"] % 3
                        c1["pv"] += 1
                        for kc in range(8):
                            P.op("pe", lambda e, kc=kc, pv=pv, hi=hi: e.matmul(
                                bank(5 + pv, 256, 0), lhsT=hT[hi][:, kc, 0:128], rhs=wqkv[:, kc, 256:512],
                                start=(kc == 0), stop=(kc == 7)), R=[B_w, B_hT[hi]], W=[B_pv[pv]])
                        ks_i = c1["kst"] % 2
                        c1["kst"] += 1
                        P.op("dve", lambda e, pv=pv, ks_i=ks_i, g=g: e.tensor_tensor(
                            out=kst[ks_i][:], in0=bank(5 + pv, 256, 0), in1=brow_bc[:, 256 * g:256 * (g + 1)],
                            op=ALU.add), R=[B_pv[pv], B_brow], W=[B_kst[ks_i]])
                        kdst = ks_o[:, 256 * g:256 * (g + 1)] if sample else k_own[q, :, 256 * g:256 * (g + 1)]
                        P.dma("act", lambda e, ks_i=ks_i, kdst=kdst: e.dma_start(out=kdst, in_=kst[ks_i][:]),
                              R=[B_kst[ks_i]])
                        pv = c1["pv"] % 3
                        c1["pv"] += 1
                        for kc in range(8):
                            P.op("pe", lambda e, kc=kc, pv=pv, hi=hi, nv=nv: e.matmul(
                                bank(5 + pv, nv, 0), lhsT=hT[hi][:, kc, 0:128], rhs=wqkv[:, kc, 512:512 + nv],
                                start=(kc == 0), stop=(kc == 7)), R=[B_w, B_hT[hi]], W=[B_pv[pv]])
                        vs_i = c1["vst"] % 2
                        c1["vst"] += 1
                        P.op("dve", lambda e, pv=pv, vs_i=vs_i, g=g: e.tensor_tensor(
                            out=vst[vs_i][:], in0=bank(5 + pv, 256, 0),
                            in1=brow_bc[:, 512 + 256 * g:512 + 256 * (g + 1)], op=ALU.add),
                            R=[B_pv[pv], B_brow], W=[B_vst[vs_i]])
                        if sample:
                            P.op("pool", lambda e, vs_i=vs_i, g=g: e.tensor_copy(
                                out=Vs_bf[:, 256 * g:256 * (g + 1)], in_=vst[vs_i][:]), R=[B_vst[vs_i]], W=[B_Vs])
                        if g == 0:
                            P.op("dve", lambda e, pv=pv, q=q: e.tensor_tensor(
                                out=zf_own[:, q, :], in0=bank(5 + pv, 8, 256), in1=brow_bc[:, 1024:1032], op=ALU.add),
                                R=[B_pv[pv], B_brow], W=[B_zfo])
                        vdst = vs_o[:, 256 * g:256 * (g + 1)] if sample else v_own[q, :, 256 * g:256 * (g + 1)]
                        P.dma("act", lambda e, vs_i=vs_i, vdst=vdst: e.dma_start(out=vdst, in_=vst[vs_i][:]),
                              R=[B_vst[vs_i]])
                    P.barrier()

                if g == 0:
                    for (zt, Bz) in ((zf_all, B_zf), (zf_own, B_zfo)):
                        P.op("act", lambda e, zt=zt: e.activation(out=zt[:], in_=zt[:], func=AF.Exp, scale=-1.0),
                             R=[Bz], W=[Bz])
                        P.op("act", lambda e, zt=zt: e.activation(out=zt[:], in_=zt[:], func=AF.Ln, bias=1.0),
                             R=[Bz], W=[Bz])
                        P.op("dve", lambda e, zt=zt: e.tensor_scalar(out=zt[:], in0=zt[:], scalar1=-1.0, scalar2=None,
                                                                     op0=ALU.mult), R=[Bz], W=[Bz])
                    P.dma("act", lambda e: e.dma_start(out=lf_own.rearrange("q p h -> p q h"),
                                                       in_=zf_own[:, 0:NOWN, :]), R=[B_zfo])
                    P.dma("act", lambda e: e.dma_start(out=lfs_o, in_=zf_own[:, NOWN, :]), R=[B_zfo])
                    P.op("dve", lambda e: e.tensor_copy(out=lfs_t[:], in_=zf_own[:, NOWN, :]), R=[B_zfo], W=[B_lfs])
                    B_pc = Buf()
                    lf_flat = zf_all[:].rearrange("p b h -> p (b h)")
                    P.op("pe", lambda e: e.matmul(bank(0), lhsT=U_incl, rhs=lf_flat, start=True, stop=True),
                         R=[B_cst, B_zf], W=[B_pc])
                    P.op("pe", lambda e: e.matmul(bank(1), lhsT=ones, rhs=lf_flat, start=True, stop=True),
                         R=[B_cst, B_zf], W=[B_pc])
                    P.op("act", lambda e: e.activation(out=Wc[:].rearrange("p b h -> p (b h)"), in_=bank(0),
                                                       func=AF.Copy), R=[B_pc], W=[B_lf])
                    for h in range(H):
                        P.op("dve", lambda e, h=h: e.tensor_tensor_scan(
                            out=Tex[:, 1:NB + 1, h], data0=onesrow[:, :],
                            data1=bank(1).rearrange("p (b h) -> p b h", h=8)[:, :, h], initial=0.0,
                            op0=ALU.mult, op1=ALU.add), R=[B_pc, B_lf], W=[B_lf])
                    for m in range(4):
                        nk = 16 * (m + 1)
                        dstb = bias_all[:, QBASE[m]:QBASE[m] + nk, :]
                        P.op("dve", lambda e, nk=nk, dstb=dstb: e.tensor_tensor(
                            out=dstb, in0=Tex[:, 0:nk, :], in1=Wc[:, 0:nk, :], op=ALU.add), R=[B_lf], W=[B_bias])
                        P.op("dve", lambda e, nk=nk, dstb=dstb, m=m: e.scalar_tensor_tensor(
                            out=dstb, in0=dstb, scalar=-1.0, in1=Tex[:, 16 * m:16 * m + 1, :].to_broadcast([128, nk, 8]),
                            op0=ALU.mult, op1=ALU.add), R=[B_lf, B_bias], W=[B_bias])
                    P.barrier()

                if STAGE < 2 or SKIP_ATT:
                    continue
                with ExitStack() as s2:
                    PT = [sb("PT%d" % i, [128, 512], BF16, s2) for i in range(4)]
                    Osb = [sb("Osb%d" % i, [128, 512], F32, s2) for i in range(2)]
                    rden = sb("rden", [128, 512], F32, s2)
                    B_PT = [Buf() for _ in range(4)]
                    B_S = [Buf() for _ in range(4)]
                    B_O = [Buf() for _ in range(2)]
                    B_Osb = [Buf() for _ in range(2)]
                    B_rden = Buf()
                    B_BC = Buf()
                    c2 = dict(s=0, o=0)
                    for m in range(4):
                        for hl in range(4):
                            pl, e2 = hl // 2, hl % 2
                            h = 4 * g + hl
                            prt = slice(64 * e2, 64 * e2 + 64)
                            kbs = [(kb, 0, None) for kb in range(16 * m)] + \
                                  [(16 * m + 4 * ip + r, 128 * ip, r) for ip in range(4) for r in range(4)]
                            oi = c2["o"] % 2
                            c2["o"] += 1
                            RQ = [B_QT[4 * m + i] for i in range(4)]

                            def emit_pv(pi, kb, c0, first, last, oi=oi, hl=hl):
                                P.op("pe", lambda e: e.matmul(
                                    bank(4 + oi)[0:65, c0:512], lhsT=Vaug[:, kb, hl, 0:65], rhs=PT[pi][:, c0:512],
                                    start=first, stop=last), R=[B_V[kb], B_PT[pi]], W=[B_O[oi]])

                            pend = []
                            for idx, (kb, c0, r) in enumerate(kbs):
                                si = c2["s"] % 4
                                c2["s"] += 1
                                P.op("pe", lambda e, si=si, kb=kb, c0=c0, r=r, pl=pl, prt=prt, m=m: e.matmul(
                                    bank(si)[:, c0:512], lhsT=KT[prt, pl, kb * 128:(kb + 1) * 128],
                                    rhs=QT[prt, pl, m * 512 + c0:(m + 1) * 512], start=True, stop=(r is None)),
                                    R=[B_KT[kb // 2]] + RQ, W=[B_S[si]])
                                if r is not None:
                                    P.op("pe", lambda e, si=si, c0=c0, r=r: e.matmul(
                                        bank(si)[:, c0:c0 + 128], lhsT=identb[:], rhs=maskT[:, r, :],
                                        start=False, stop=True), R=[B_mask, B_small], W=[B_S[si]])
                                if len(pend) >= ATT_LAG:
                                    emit_pv(*pend.pop(0))
                                P.op("act", lambda e, si=si, c0=c0, kb=kb, m=m, h=h: e.activation(
                                    out=PT[si][:, c0:512], in_=bank(si)[:, c0:512], func=AF.Exp, scale=SCALE,
                                    bias=bias_all[:, QBASE[m] + kb, h:h + 1]), R=[B_S[si], B_bias], W=[B_PT[si]])
                                pend.append((si, kb, c0, idx == 0, idx == len(kbs) - 1))
                            for pv_ in pend:
                                emit_pv(*pv_)
                            P.op("act", lambda e, oi=oi: e.activation(out=Osb[oi][0:65, :], in_=bank(4 + oi)[0:65, :],
                                                                      func=AF.Copy), R=[B_O[oi]], W=[B_Osb[oi]])
                            P.op("dve", lambda e, oi=oi: e.reciprocal(out=rden[64:65, :], in_=Osb[oi][64:65, :]),
                                 R=[B_Osb[oi]], W=[B_rden])
                            P.op("pe", lambda e: e.matmul(bank(6)[0:64, :], lhsT=ones[64:65, 0:64], rhs=rden[64:65, :],
                                                          start=True, stop=True), R=[B_cst, B_rden], W=[B_BC])
                            P.op("dve", lambda e, oi=oi, h=h, m=m: e.tensor_tensor(
                                out=oT[0:64, h, m * 512:(m + 1) * 512], in0=Osb[oi][0:64, :], in1=bank(6)[0:64, :],
                                op=ALU.mult), R=[B_Osb[oi], B_BC], W=[B_oT[m]])
                    P.barrier()


        def sample_phase():
          if True:
            with ExitStack() as sS:
                  ptr_sb = sb("ptr_sb", [128, 32], I32, sS)
                  shi_sb = sb("shi_sb", [128, 1], F32, sS)
                  idx_all = sb("idx_all", [128, 32], I32, sS)
                  bmask = sb("bmask", [32, 512], F32, sS)
                  K8 = [sb("K8_%d" % i, [128, 4096], F32, sS) for i in range(2)]
                  V8 = [sb("V8_%d" % i, [128, 4096], F32, sS) for i in range(2)]
                  V8b = [sb("V8b_%d" % i, [128, 4096], BF16, sS) for i in range(2)]
                  K8T = [sb("K8T_%d" % i, [128, 8, 4, 128], BF16, sS) for i in range(2)]
                  Lall = sb("Lall", [128, 8, 8, 8], F32, sS)
                  Ls = [sb("Ls%d" % i, [128, 8, 8, 8], F32, sS) for i in range(3)]
                  Rall = sb("Rall", [128, 8, 8, 8], F32, sS)
                  rowtot = sb("rowtot", [128, 8, 8], F32, sS)
                  gsuf = sb("gsuf", [128, 8, 8], F32, sS)
                  sctmp = [sb("sctmp%d" % i, [128, 8, 8, 4], F32, sS) for i in range(2)]
                  Pt = [sb("Pt%d" % i, [128, 8, 32], BF16, sS) for i in range(2)]
                  onesb = sb("onesb", [128, 2], BF16, sS)
                  m4 = sb("m4", [128, 4], F32, sS)
                  cnn = sb("cnn", [128, 8], F32, sS)
                  tmpN = sb("tmpN", [128, 8, 4], F32, sS)
                  PtN = sb("PtN", [128, 32], BF16, sS)
                  Oss = sb("Oss", [32, 512], F32, sS)
                  Osel = sb("Osel", [32, 64], F32, sS)
                  rdn = sb("rdn", [32, 2], F32, sS)
                  B_idx, B_bm, B_L, B_R, B_rt, B_gs, B_m4, B_cn, B_tN, B_PtN, B_Oss, B_Osel, B_rdn = (Buf() for _ in range(13))
                  B_Ls = [Buf() for _ in range(3)]
                  B_K8 = [Buf(), Buf()]
                  B_V8 = [Buf(), Buf()]
                  B_V8b = [Buf(), Buf()]
                  B_K8T = [Buf(), Buf()]
                  B_sct = [Buf(), Buf()]
                  B_Pt = [Buf(), Buf()]
                  BKs = [Buf() for _ in range(8)]
                  cS = dict(k=0, t=0, s=0, e=0)
                  P.dma("sp", lambda e: e.dma_start(out=ptr_sb[:], in_=pt_rep), W=[B_idx])
                  P.dma("sp", lambda e: e.dma_start(out=shi_sb[:], in_=shi_d), W=[B_idx])
                  P.dma("sp", lambda e: e.dma_start(out=bmask[:], in_=bmask_d), W=[B_bm])
                  P.op("dve", lambda e: e.tensor_scalar(out=idx_all[:], in0=ptr_sb[:], scalar1=16.0, scalar2=shi_sb[:, 0:1],
                                                        op0=ALU.mult, op1=ALU.add), R=[B_idx], W=[B_idx])
                  if SDBG == 1:
                      return
                  P.op("pool", lambda e: e.memset(onesb[:], 1.0), W=[B_m4])
                  P.op("pool", lambda e: e.memset(oT[:, :, 2048:2176], 0.0), W=[B_oT[4]])
                  P.op("dve", lambda e: e.tensor_scalar(out=m4[0:4, :], in0=tmins[0:4, 0:4], scalar1=0.0, scalar2=None,
                                                        op0=ALU.is_ge), R=[B_cst], W=[B_m4])
                  P.op("dve", lambda e: e.tensor_scalar(out=m4[0:4, :], in0=m4[0:4, :], scalar1=-1.0, scalar2=BIG,
                                                        op0=ALU.add, op1=ALU.mult), R=[B_m4], W=[B_m4])
                  Vs4 = sb("Vs4", [4, 4, 512], BF16, sS)
                  lfs4 = sb("lfs4", [4, 4, 8], F32, sS)
                  B_n4 = Buf()
                  for s_ in range(4):
                      P.dma("sp", lambda e, s_=s_: e.dma_start(out=Vs4[0:4, s_, :], in_=Vs_bf[32 * s_:32 * s_ + 4, :]),
                            R=[B_Vs], W=[B_n4])
                      P.dma("sp", lambda e, s_=s_: e.dma_start(out=lfs4[0:4, s_, :], in_=lfs_t[32 * s_:32 * s_ + 4, :]),
                            R=[B_lfs], W=[B_n4])
                  for s_ in range(4):
                      sl = slice(0, 4)
                      for g8 in range(8):
                          P.dma("pool", lambda e, g8=g8, s_=s_: e.indirect_dma_start(
                              out=Lall[:, g8, :, :].rearrange("p a b -> p (a b)"), out_offset=None, in_=cache_l,
                              in_offset=bass.IndirectOffsetOnAxis(ap=idx_all[:, 8 * s_ + g8:8 * s_ + g8 + 1], axis=0)),
                              R=[B_idx], W=[B_L])
                      if SDBG == 2:
                          return
                      P.op("dve", lambda e: e.tensor_reduce(out=rowtot[:], in_=Lall[:].rearrange("p g s h -> p g h s"),
                                                            axis=mybir.AxisListType.X, op=ALU.add), R=[B_L], W=[B_rt])
                      src, Bsrc = Lall, B_L
                      for step, (dst, Bd) in zip((1, 2, 4), zip(Ls, B_Ls)):
                          P.op("dve", lambda e, src=src, dst=dst: e.tensor_copy(out=dst[:], in_=src[:]), R=[Bsrc], W=[Bd])
                          P.op("dve", lambda e, src=src, dst=dst, step=step: e.tensor_tensor(
                              out=dst[:, :, 0:8 - step, :], in0=src[:, :, 0:8 - step, :], in1=src[:, :, step:8, :], op=ALU.add),
                              R=[Bsrc], W=[Bd])
                          src, Bsrc = dst, Bd
                      P.op("dve", lambda e, src=src: e.tensor_tensor(out=Rall[:], in0=src[:], in1=Lall[:], op=ALU.subtract),
                           R=[Bsrc, B_L], W=[B_R])
                      rt_flat = rowtot[:].rearrange("p g h -> p (g h)")
                      P.op("pe", lambda e: e.matmul(bank(6)[:, 0:64], lhsT=U_strict, rhs=rt_flat, start=True, stop=True),
                           R=[B_cst, B_rt], W=[BKs[6]])
                      P.op("pe", lambda e: e.matmul(bank(7)[:, 0:64], lhsT=ones, rhs=rt_flat, start=True, stop=True),
                           R=[B_cst, B_rt], W=[BKs[7]])
                      P.op("pool", lambda e: e.memset(gsuf[:], 0.0), W=[B_gs])
                      for g8 in range(6, -1, -1):
                          P.op("dve", lambda e, g8=g8: e.tensor_tensor(
                              out=gsuf[:, g8, :], in0=gsuf[:, g8 + 1, :], in1=bank(7)[:, 8 * (g8 + 1):8 * (g8 + 2)], op=ALU.add),
                              R=[BKs[7], B_gs], W=[B_gs])
                      P.op("dve", lambda e: e.tensor_tensor(out=gsuf[:], in0=gsuf[:],
                                                            in1=bank(6)[:, 0:64].rearrange("p (g h) -> p g h", h=8), op=ALU.add),
                           R=[BKs[6], B_gs], W=[B_gs])
                      P.op("dve", lambda e: e.tensor_tensor(out=Rall[:], in0=Rall[:],
                                                            in1=gsuf[:].unsqueeze(2).to_broadcast([128, 8, 8, 8]), op=ALU.add),
                           R=[B_gs, B_R], W=[B_R])
                      if SDBG == 3:
                          return
                      for g8 in range(8):
                          ki = cS["k"] % 2
                          cS["k"] += 1
                          ix = idx_all[:, 8 * s_ + g8:8 * s_ + g8 + 1]
                          P.dma("pool", lambda e, ki=ki, ix=ix: e.indirect_dma_start(
                              out=K8[ki][:, :], out_offset=None, in_=cache_k,
                              in_offset=bass.IndirectOffsetOnAxis(ap=ix, axis=0)), R=[B_idx], W=[B_K8[ki]])
                          P.dma("pool", lambda e, ki=ki, ix=ix: e.indirect_dma_start(
                              out=V8[ki][:, :], out_offset=None, in_=cache_v,
                              in_offset=bass.IndirectOffsetOnAxis(ap=ix, axis=0)), R=[B_idx], W=[B_V8[ki]])
                          for hh in range(2):
                              P.op("pool", lambda e, ki=ki, hh=hh: e.tensor_copy(out=V8b[ki][:, 2048 * hh:2048 * (hh + 1)],
                                                                                 in_=V8[ki][:, 2048 * hh:2048 * (hh + 1)]),
                                   R=[B_V8[ki]], W=[B_V8b[ki]])
                          for sl8 in range(8):
                              tb = cS["t"] % 2
                              cS["t"] += 1
                              for pr in range(4):
                                  P.op("pe", lambda e, ki=ki, sl8=sl8, pr=pr, tb=tb: e.transpose(
                                      bank(tb)[:, 128 * pr:128 * (pr + 1)],
                                      K8[ki][:, 512 * sl8 + 128 * pr:512 * sl8 + 128 * (pr + 1)], ident),
                                      R=[B_K8[ki], B_cst], W=[BKs[tb]])
                              ev = cS["e"] % 2
                              cS["e"] += 1
                              if ev == 0:
                                  P.op("act", lambda e, ki=ki, sl8=sl8, tb=tb: e.activation(
                                      out=K8T[ki][:, sl8, :, :].rearrange("p a b -> p (a b)"), in_=bank(tb), func=AF.Copy),
                                      R=[BKs[tb]], W=[B_K8T[ki]])
                              else:
                                  P.op("dve", lambda e, ki=ki, sl8=sl8, tb=tb: e.tensor_copy(
                                      out=K8T[ki][:, sl8, :, :].rearrange("p a b -> p (a b)"), in_=bank(tb)),
                                      R=[BKs[tb]], W=[B_K8T[ki]])
                          si = cS["s"] % 2
                          cS["s"] += 1
                          for sl8 in range(8):
                              for h_ in range(8):
                                  pr, e2 = h_ // 2, h_ % 2
                                  prt = slice(64 * e2, 64 * e2 + 64)
                                  P.op("pe", lambda e, ki=ki, sl8=sl8, h_=h_, pr=pr, prt=prt, si=si, s_=s_: e.matmul(
                                      bank(2 + si)[:, 32 * sl8 + 4 * h_:32 * sl8 + 4 * h_ + 4], lhsT=K8T[ki][prt, sl8, pr, :],
                                      rhs=QTs[prt, pr, 32 * s_:32 * s_ + 4], start=True, stop=True),
                                      R=[B_K8T[ki], B_QTs], W=[BKs[2 + si]])
                          P.op("dve", lambda e, si=si, g8=g8: e.scalar_tensor_tensor(
                              out=sctmp[si][:], in0=bank(2 + si)[:, 0:256].rearrange("p (a h t) -> p a h t", h=8, t=4),
                              scalar=SCALE, in1=Rall[:, g8, :, :].unsqueeze(3).to_broadcast([128, 8, 8, 4]),
                              op0=ALU.mult, op1=ALU.add), R=[BKs[2 + si], B_R], W=[B_sct[si]])
                          P.op("act", lambda e, si=si: e.activation(
                              out=Pt[si][:].rearrange("p a b -> p (a b)"), in_=sctmp[si][:].rearrange("p a h t -> p (a h t)"),
                              func=AF.Exp), R=[B_sct[si]], W=[B_Pt[si]])
                          for sl8 in range(8):
                              first = (g8 == 0 and sl8 == 0)
                              P.op("pe", lambda e, si=si, ki=ki, sl8=sl8, first=first: e.matmul(
                                  bank(4)[0:32, :], lhsT=Pt[si][:, sl8, :], rhs=V8b[ki][:, 512 * sl8:512 * (sl8 + 1)],
                                  start=first, stop=False), R=[B_Pt[si], B_V8b[ki]], W=[BKs[4]])
                              P.op("pe", lambda e, si=si, sl8=sl8, first=first: e.matmul(
                                  bank(5)[0:32, 0:2], lhsT=Pt[si][:, sl8, :], rhs=onesb[:, 0:2],
                                  start=first, stop=False), R=[B_Pt[si], B_m4], W=[BKs[5]])
                      if SDBG == 4:
                          return
                      P.op("pe", lambda e, sl=sl, s_=s_: e.matmul(bank(6)[sl, 0:8], lhsT=U_incl[sl, sl], rhs=lfs4[sl, s_, :],
                                                           start=True, stop=True), R=[B_cst, B_n4, B_gs], W=[BKs[6]])
                      P.op("dve", lambda e, sl=sl: e.tensor_scalar(out=cnn[sl, :], in0=bank(6)[sl, 0:8], scalar1=-1.0,
                                                                   scalar2=None, op0=ALU.mult), R=[BKs[6]], W=[B_cn])
                      if SDBG == 41:
                          return
                      for h_ in range(8):
                          pr, e2 = h_ // 2, h_ % 2
                          prt = slice(64 * e2, 64 * e2 + 64)
                          P.op("pe", lambda e, h_=h_, pr=pr, prt=prt, s_=s_, e2=e2: e.matmul(
                              bank(6 + e2)[0:32, 64 + 4 * h_:64 + 4 * h_ + 4], lhsT=KTs[prt, pr, 32 * s_:32 * s_ + 32],
                              rhs=QTs[prt, pr, 32 * s_:32 * s_ + 4], start=True, stop=True),
                              R=[B_KTs, B_QTs, B_gs, B_cn], W=[BKs[6 + e2]])
                      if SDBG == 42:
                          return
                      for e2 in range(2):
                          P.op("dve", lambda e, sl=sl, e2=e2: e.scalar_tensor_tensor(
                              out=tmpN[sl, :, :].rearrange("p (a b) t -> p a b t", b=2)[:, :, e2, :],
                              in0=bank(6 + e2)[sl, 64:96].rearrange("p (a b t) -> p a b t", b=2, t=4)[:, :, e2, :], scalar=SCALE,
                              in1=cnn[sl, :].rearrange("p (a b) -> p a b", b=2)[:, :, e2].unsqueeze(2).to_broadcast([4, 4, 4]),
                              op0=ALU.mult, op1=ALU.add), R=[BKs[6 + e2], B_cn], W=[B_tN])
                      P.op("dve", lambda e, sl=sl: e.tensor_tensor(
                          out=tmpN[sl, :, :], in0=tmpN[sl, :, :], in1=m4[sl, :].unsqueeze(1).to_broadcast([4, 8, 4]), op=ALU.add),
                          R=[B_tN, B_m4], W=[B_tN])
                      P.op("act", lambda e, sl=sl: e.activation(out=PtN[sl, :], in_=tmpN[sl, :, :].rearrange("p h t -> p (h t)"),
                                                                func=AF.Exp), R=[B_tN], W=[B_PtN])
                      if SDBG == 43:
                          return
                      P.op("pe", lambda e, sl=sl, s_=s_: e.matmul(bank(4)[0:32, :], lhsT=PtN[sl, :], rhs=Vs4[sl, s_, :],
                                                           start=False, stop=True), R=[B_PtN, B_n4], W=[BKs[4]])
                      P.op("pe", lambda e, sl=sl: e.matmul(bank(5)[0:32, 0:2], lhsT=PtN[sl, :], rhs=onesb[sl, 0:2],
                                                           start=False, stop=True), R=[B_PtN, B_m4], W=[BKs[5]])
                      if SDBG == 5:
                          return
                      P.op("act", lambda e: e.activation(out=Oss[:], in_=bank(4)[0:32, :], func=AF.Copy), R=[BKs[4]], W=[B_Oss])
                      P.op("dve", lambda e: e.reciprocal(out=rdn[:], in_=bank(5)[0:32, 0:2]), R=[BKs[5]], W=[B_rdn])
                      P.op("dve", lambda e: e.tensor_tensor(out=Oss[:], in0=Oss[:], in1=bmask[:], op=ALU.mult),
                           R=[B_bm, B_Oss], W=[B_Oss])
                      P.op("dve", lambda e: e.tensor_reduce(out=Osel[:], in_=Oss[:].rearrange("p (h d) -> p d h", d=64),
                                                            axis=mybir.AxisListType.X, op=ALU.add), R=[B_Oss], W=[B_Osel])
                      P.op("dve", lambda e: e.tensor_scalar(out=Osel[:], in0=Osel[:], scalar1=rdn[:, 0:1], scalar2=None,
                                                            op0=ALU.mult), R=[B_rdn, B_Osel], W=[B_Osel])
                      if DBG_OT and s_ == 0:
                          P.dma("sp", lambda e: e.dma_start(out=dbg2[:, 0:512], in_=Rall[:].rearrange("p a b c -> p (a b c)")), R=[B_R])
                          P.dma("sp", lambda e: e.dma_start(out=dbg2[:, 512:768], in_=sctmp[1][:].rearrange("p a b c -> p (a b c)")), R=[B_sct[1]])
                          P.dma("sp", lambda e: e.dma_start(out=dbg2[0:32, 768:1280], in_=Oss[:]), R=[B_Oss])
                          P.dma("sp", lambda e: e.dma_start(out=dbg2[0:32, 1280:1282], in_=rdn[:]), R=[B_rdn])
                          P.dma("sp", lambda e: e.dma_start(out=dbg2[0:32, 1290:1354], in_=Osel[:]), R=[B_Osel])
                          P.dma("sp", lambda e: e.dma_start(out=dbg2[:, 1400:1432], in_=idx_all[:].bitcast(F32)), R=[B_idx])
                          P.dma("sp", lambda e: e.dma_start(out=dbg2[0:4, 1440:1472], in_=tmpN[0:4, :, :].rearrange("p a b -> p (a b)")), R=[B_tN])
                      P.op("pe", lambda e: e.transpose(bank(7)[0:64, 64:96], Osel[:], ident[0:32, 0:32]),
                           R=[B_Osel, B_cst, B_tN], W=[BKs[7]])
                      P.op("act", lambda e, s_=s_: e.activation(
                          out=oT[0:64, :, 2048 + 32 * s_:2048 + 32 * s_ + 4],
                          in_=bank(7)[0:64, 64:96].rearrange("p (h t) -> p h t", t=4), func=AF.Copy),
                          R=[BKs[7]], W=[B_oT[4]])
                  P.barrier()
        if STAGE >= 4:
            sample_phase()
            P.barrier()

        if STAGE >= 3 and not SKIP_P5:
            def wv(w):
                return w.rearrange("(kc p) n -> p kc n", p=128)
            w_pa_v = w_pa.rearrange("(h p) n -> p h n", p=64)
            w_pb_v, w_o_v, w_f1_v, w_f2_v = wv(w_pb), wv(w_o), wv(w_f1), wv(w_f2)
            for hf in range(2):
                with ExitStack() as sH:
                    NT = 1152 if hf == 0 else 1024
                    nblk = NT // 128
                    xT = sb("xT", [128, 8, NT], F32, sH)
                    hTh = sb("hTh", [128, 8, NT], BF16, sH)
                    sT = sb("sT", [128, 4, NT], BF16, sH)
                    rsT = sb("rsT", [128, 512], F32, sH)
                    tA = [sb("tA%d" % i, [128, 512], F32, sH) for i in range(2)]
                    tB = [sb("tB%d" % i, [128, 512], F32, sH) for i in range(2)]
                    B_rsT = Buf()
                    B_tA = [Buf(), Buf()]
                    B_tB = [Buf(), Buf()]
                    BK = [Buf() for _ in range(8)]
                    tiles = [(0, 512, 2 * hf * 512), (512, 512, (2 * hf + 1) * 512)]
                    if hf == 0:
                        tiles.append((1024, 128, 2048))
                    B_xT = [Buf() for _ in tiles]
                    B_hTh = [Buf() for _ in tiles]
                    B_sT = [Buf() for _ in tiles]
                    rot = dict(a=0, b=0, c=0, d=0, t=0, u=0)

                    def nxt(k, n=2):
                        v = rot[k] % n
                        rot[k] += 1
                        return v

                    def per_sample(N, fn_full, fn_s):
                        if N == 512:
                            fn_full()
                        else:
                            for s_ in range(4):
                                fn_s(s_, 32 * s_)

                    def col_stats(ti, src_fn, nparts, scale, Rsrc=None):
                        c0, N, _ = tiles[ti]
                        for kc in range(nparts):
                            a = nxt("t")
                            P.op("act", lambda e, kc=kc, a=a: e.activation(out=tA[a][:, 0:N], in_=src_fn(kc), func=AF.Square),
                                 R=(Rsrc or [B_xT[ti]]), W=[B_tA[a]])
                            P.op("pe", lambda e, kc=kc, a=a: e.matmul(bank(6)[:, 0:N], lhsT=ones, rhs=tA[a][:, 0:N],
                                                                      start=(kc == 0), stop=(kc == nparts - 1)),
                                 R=[B_tA[a], B_cst], W=[BK[6]])
                        P.op("act", lambda e: e.activation(out=rsT[:, 0:N], in_=bank(6)[:, 0:N], func=AF.Sqrt, scale=scale,
                                                           bias=EPS), R=[BK[6]], W=[B_rsT])
                        P.op("dve", lambda e: e.reciprocal(out=rsT[:, 0:N], in_=rsT[:, 0:N]), R=[B_rsT], W=[B_rsT])

                    with ExitStack() as sa:
                        hTo = sb("hTo", [128, 8, 256], BF16, sa)
                        B_hTo = Buf()
                        with ExitStack() as s1:
                            NP = NormPipe(s1)
                            TPf = [psum[:, 2048 + 1024 * i:2048 + 1024 * (i + 1)].rearrange("p (k t) -> p k t", t=128)
                                   for i in range(2)]
                            B_TPf = [Buf(), Buf()]
                            for qq in range(nblk):
                                sample = (qq == 8)
                                q = 8 * hf + qq
                                col = qq * 128
                                ti_ = min(qq // 4, 2)
                                xi = NP.load(xs_d if sample else xown[q, 32:160, :], 128)
                                rs_ap, B_rs = NP.rstd(xi, 128)
                                ni = NP.normalize(xi, 128, rs_ap, B_rs)
                                tpi = NP.new_tp()
                                NP.transpose(ni, 128, tpi, 0)
                                NP.evac_mod(tpi, 128, lambda kc, col=col: hTh[:, kc, col:col + 128], B_hTh[ti_], a1T, sh1,
                                            sample=sample)
                                fi = qq % 2
                                for kc in range(8):
                                    P.op("pe", lambda e, kc=kc, fi=fi, xi=xi: e.transpose(
                                        TPf[fi][:, kc, :], NP.xsl[xi][:, kc * 128:(kc + 1) * 128], ident),
                                        R=[NP.B_x[xi], B_cst], W=[B_TPf[fi]])
                                P.op("act", lambda e, fi=fi, col=col: e.activation(out=xT[:, :, col:col + 128], in_=TPf[fi],
                                                                                   func=AF.Copy), R=[B_TPf[fi]], W=[B_xT[ti_]])
                                if not sample:
                                    xi = NP.load(xown[q, 0:32, :], 32)
                                    rs_ap, B_rs = NP.rstd(xi, 32)
                                    ni = NP.normalize(xi, 32, rs_ap, B_rs)
                                    tpi = NP.new_tp()
                                    NP.transpose(ni, 32, tpi, 0)
                                    NP.evac_mod(tpi, 32, lambda kc, qq=qq: hTo[:, kc, qq * 32:(qq + 1) * 32], B_hTo, a1T, sh1)
                            P.barrier()

                        with ExitStack() as s2:
                            wglu = sb("wglu", [128, 8, 1024], BF16, s2)
                            uT = sb("uT", [128, 4, 8, 160], BF16, s2)
                            uS = sb("uS", [128, 4, 4, 34], BF16, s2)
                            stTf = sb("stTf", [128, 4, 4, 30], F32, s2)
                            u32 = sb("u32", [128, 4, 128], F32, s2)
                            ucp = sb("ucp", [128, 512], F32, s2)
                            stro = sb("stro", [104, 512], F32, s2)
                            diag = [sb("diag%d" % i, [128, CK, 128], BF16, s2) for i in range(2)]
                            ycv = sb("ycv", [128, 4, 512], F32, s2)
                            mean_sb = sb("mean_sb", [128, 512], F32, s2)
                            B_wglu, B_uT, B_uS, B_u32, B_ucp, B_ycv, B_mean, B_stro = (Buf() for _ in range(8))
                            B_diag = [Buf(), Buf()]
                            for hh in range(2):
                                P.dma("pool", lambda e, hh=hh: e.dma_start(
                                    out=wglu[:, :, 512 * hh:512 * (hh + 1)],
                                    in_=w_in_v[:, :, GLU_OFF + 512 * hh:GLU_OFF + 512 * (hh + 1)]), W=[B_wglu])
                            P.op("pool", lambda e: e.memset(ycv[:], 0.0), W=[B_ycv])
                            P.op("pool", lambda e: e.memset(u32[:], 0.0), W=[B_u32])
                            if hf == 0:
                                P.dma("sp", lambda e: e.dma_start(out=stTf[:], in_=stT_d), W=[B_uS])
                                for cc in range(4):
                                    P.op("pool", lambda e, cc=cc: e.tensor_copy(out=uS[:, cc, :, 0:30], in_=stTf[:, cc, :, :]),
                                         R=[B_uS], W=[B_uS])
                                P.dma("sp", lambda e: e.dma_start(out=stro[:], in_=st_rows.rearrange("s r c -> (s r) c")),
                                      W=[B_stro])
                                for s_ in range(4):
                                    P.dma("act", lambda e, s_=s_: e.dma_start(out=convs_o[s_, 0:26, :],
                                                                              in_=stro[26 * s_:26 * (s_ + 1), :]), R=[B_stro])

                            def glu(src_fn, N, Rsrc, outs):
                                for cc in range(4):
                                    ia, ib = nxt("a"), nxt("b")
                                    for kc in range(8):
                                        P.op("pe", lambda e, kc=kc, cc=cc, ia=ia: e.matmul(
                                            bank(ia)[:, 0:N], lhsT=wglu[:, kc, cc * 128:(cc + 1) * 128], rhs=src_fn(kc),
                                            start=(kc == 0), stop=(kc == 7)), R=[B_wglu] + Rsrc, W=[BK[ia]])
                                    for kc in range(8):
                                        P.op("pe", lambda e, kc=kc, cc=cc, ib=ib: e.matmul(
                                            bank(2 + ib)[:, 0:N], lhsT=wglu[:, kc, 512 + cc * 128:512 + (cc + 1) * 128],
                                            rhs=src_fn(kc), start=(kc == 0), stop=(kc == 7)), R=[B_wglu] + Rsrc, W=[BK[2 + ib]])
                                    P.op("act", lambda e, cc=cc, ib=ib: e.activation(
                                        out=tB[ib][:, 0:N], in_=bank(2 + ib)[:, 0:N], func=AF.Sigmoid,
                                        bias=bfm[:, 12 + cc:13 + cc]), R=[BK[2 + ib], B_small], W=[B_tB[ib]])
                                    for (dst_fn, view, Wb) in outs:
                                        P.op("dve", lambda e, cc=cc, ia=ia, ib=ib, dst_fn=dst_fn, view=view: e.scalar_tensor_tensor(
                                            out=dst_fn(cc), in0=view(bank(ia)[:, 0:N]), scalar=bfm[:, 8 + cc:9 + cc],
                                            in1=view(tB[ib][:, 0:N]), op0=ALU.add, op1=ALU.mult),
                                            R=[BK[ia], B_tB[ib], B_small], W=Wb)

                            glu(lambda kc: hTo[:, kc, :], 256, [B_hTo],
                                [(lambda cc: uT[:, cc, :, 0:32], lambda a: a.rearrange("p (b t) -> p b t", t=32), [B_uT])])
                            for cc in range(4):
                                P.op("dve", lambda e, cc=cc: e.tensor_tensor(
                                    out=uT[:, cc, :, 0:32], in0=uT[:, cc, :, 0:32],
                                    in1=hval[:, 8 * hf:8 * hf + 8].unsqueeze(2).to_broadcast([128, 8, 32]), op=ALU.mult),
                                    R=[B_small, B_uT], W=[B_uT])
                            for ti, (c0, N, oc0) in enumerate(tiles):
                                if N == 512:
                                    outs = [(lambda cc, ti=ti: uT[:, cc, 4 * ti:4 * ti + 4, 32:160],
                                             lambda a: a.rearrange("p (b t) -> p b t", t=128), [B_uT])]
                                    if hf == 1 and ti == 1:
                                        outs.append((lambda cc: u32[:, cc, 0:32], lambda a: a[:, 480:512], [B_u32]))
                                else:
                                    outs = [(lambda cc: uS[:, cc, :, 30:34],
                                             lambda a: a.rearrange("p (s t) -> p s t", t=32)[:, :, 0:4], [B_uS]),
                                            (lambda cc: u32[:, cc, :], lambda a: a, [B_u32])]
                                glu(lambda kc, c0=c0, N=N: hTh[:, kc, c0:c0 + N], N, [B_hTh[ti]], outs)
                            for cc in range(4):
                                P.op("pe", lambda e, cc=cc: e.transpose(
                                    bank(5)[0:(128 if hf == 0 else 32), cc * 128:(cc + 1) * 128],
                                    u32[:, cc, 0:(128 if hf == 0 else 32)], ident), R=[B_u32, B_cst], W=[BK[5]])
                            nr = 128 if hf == 0 else 32
                            P.op("act", lambda e: e.activation(out=ucp[0:nr, :], in_=bank(5)[0:nr, :], func=AF.Copy),
                                 R=[BK[5]], W=[B_ucp])
                            if hf == 1:
                                P.dma("act", lambda e: e.dma_start(out=convp_o, in_=ucp[0:32, :]), R=[B_ucp])
                            else:
                                for s_ in range(4):
                                    P.dma("act", lambda e, s_=s_: e.dma_start(out=convs_o[s_, 26:30, :],
                                                                              in_=ucp[32 * s_:32 * s_ + 4, :]), R=[B_ucp])
                            for ti, (c0, N, oc0) in enumerate(tiles):
                                NC = N if N == 512 else 16
                                for cc in range(4):
                                    di = nxt("d")
                                    for k in range(CK):
                                        P.op("pool", lambda e, cc=cc, k=k, di=di: e.tensor_scalar(
                                            out=diag[di][:, k, :], in0=identb[:], scalar1=cvp[:, cc * CK + k:cc * CK + k + 1],
                                            scalar2=0.0, op0=ALU.mult, op1=ALU.add), R=[B_small], W=[B_diag[di]])
                                    ci = nxt("c")
                                    for k in range(CK):
                                        if N == 512:
                                            rhs = uT[:, cc, 4 * ti:4 * ti + 4, 2 + k:2 + k + 128]
                                        else:
                                            rhs = uS[:, cc, :, k:k + 4]
                                        P.op("pe", lambda e, k=k, di=di, ci=ci, rhs=rhs: e.matmul(
                                            bank(4 + ci)[:, 0:NC], lhsT=diag[di][:, k, :], rhs=rhs, start=(k == 0),
                                            stop=(k == CK - 1)), R=[B_diag[di], B_uT, B_uS], W=[BK[4 + ci]])
                                    if N == 512:
                                        ydst, ysrc = ycv[:, cc, :], bank(4 + ci)[:, 0:512]
                                    else:
                                        ydst = ycv[:, cc, 0:128].rearrange("p (s t) -> p s t", t=32)[:, :, 0:4]
                                        ysrc = bank(4 + ci)[:, 0:16].rearrange("p (s t) -> p s t", t=4)
                                    P.op("act", lambda e, cc=cc, ydst=ydst, ysrc=ysrc: e.activation(
                                        out=ydst, in_=ysrc, func=AF.Identity, bias=cvp[:, 124 + cc:125 + cc]),
                                        R=[BK[4 + ci], B_small], W=[B_ycv])
                                for cc in range(4):
                                    P.op("pe", lambda e, cc=cc: e.matmul(bank(7)[:, 0:N], lhsT=ones, rhs=ycv[:, cc, 0:N],
                                                                         start=(cc == 0), stop=(cc == 3)),
                                         R=[B_ycv, B_cst], W=[BK[7]])
                                P.op("act", lambda e: e.activation(out=mean_sb[:, 0:N], in_=bank(7)[:, 0:N], func=AF.Copy,
                                                                   scale=1.0 / CW), R=[BK[7]], W=[B_mean])
                                for cc in range(4):
                                    P.op("dve", lambda e, cc=cc: e.tensor_tensor(out=ycv[:, cc, 0:N], in0=ycv[:, cc, 0:N],
                                                                                 in1=mean_sb[:, 0:N], op=ALU.subtract),
                                         R=[B_mean, B_ycv], W=[B_ycv])
                                B_save = B_xT[ti]
                                col_stats(ti, lambda kc: ycv[:, kc, 0:N], 4, 1.0 / CW, Rsrc=[B_ycv])
                                for cc in range(4):
                                    a = nxt("t")
                                    P.op("dve", lambda e, cc=cc, a=a: e.tensor_tensor(
                                        out=tA[a][:, 0:N], in0=ycv[:, cc, 0:N], in1=rsT[:, 0:N], op=ALU.mult),
                                        R=[B_ycv, B_rsT], W=[B_tA[a]])
                                    P.op("act", lambda e, cc=cc, a=a, c0=c0: e.activation(
                                        out=sT[:, cc, c0:c0 + N], in_=tA[a][:, 0:N], func=AF.Silu,
                                        scale=cvp[:, 128 + cc:129 + cc], bias=cvp[:, 132 + cc:133 + cc]),
                                        R=[B_tA[a], B_small], W=[B_sT[ti]])
                                P.barrier()
                            P.barrier()
                        P.barrier()

                    with ExitStack() as sb_:
                        mT = sb("mT", [128, 8, NT], BF16, sb_)
                        B_mT = [Buf() for _ in tiles]
                        with ExitStack() as s3:
                            wga = [sb("wga%d" % i, [128, 8, 256], BF16, s3) for i in range(2)]
                            wgb = [sb("wgb%d" % i, [128, 8, 256], BF16, s3) for i in range(2)]
                            wpa = [sb("wpa%d" % i, [64, 8, 256], BF16, s3) for i in range(2)]
                            wpb = [sb("wpb%d" % i, [128, 4, 256], BF16, s3) for i in range(2)]
                            t1 = sb("t1", [128, 512], F32, s3)
                            t2 = sb("t2", [128, 512], F32, s3)
                            B_ws = [Buf(), Buf()]
                            B_t1, B_t2 = Buf(), Buf()
                            for rr in range(4):
                                wi = rr % 2
                                cs = slice(256 * rr, 256 * (rr + 1))
                                P.dma("pool", lambda e, wi=wi, rr=rr: e.dma_start(
                                    out=wga[wi][:], in_=w_in_v[:, :, GA_OFF + 256 * rr:GA_OFF + 256 * (rr + 1)]), W=[B_ws[wi]])
                                P.dma("pool", lambda e, wi=wi, rr=rr: e.dma_start(
                                    out=wgb[wi][:], in_=w_in_v[:, :, GB_OFF + 256 * rr:GB_OFF + 256 * (rr + 1)]), W=[B_ws[wi]])
                                P.dma("pool", lambda e, wi=wi, cs=cs: e.dma_start(out=wpa[wi][:], in_=w_pa_v[:, :, cs]),
                                      W=[B_ws[wi]])
                                P.dma("pool", lambda e, wi=wi, cs=cs: e.dma_start(out=wpb[wi][:], in_=w_pb_v[:, :, cs]),
                                      W=[B_ws[wi]])
                                for ti, (c0, N, oc0) in enumerate(tiles):
                                    B_o = B_oT[4] if N == 128 else B_oT[oc0 // 512]
                                    for nn in range(2):
                                        n = 2 * rr + nn
                                        ns = slice(128 * nn, 128 * (nn + 1))
                                        r_ = nxt("u")
                                        for kc in range(8):
                                            P.op("pe", lambda e, kc=kc, wi=wi, ns=ns, r_=r_, c0=c0, N=N: e.matmul(
                                                bank(r_)[:, 0:N], lhsT=wga[wi][:, kc, ns], rhs=hTh[:, kc, c0:c0 + N],
                                                start=(kc == 0), stop=(kc == 7)), R=[B_ws[wi], B_hTh[ti]], W=[BK[r_]])
                                        for kc in range(8):
                                            P.op("pe", lambda e, kc=kc, wi=wi, ns=ns, r_=r_, c0=c0, N=N: e.matmul(
                                                bank(2 + r_)[:, 0:N], lhsT=wgb[wi][:, kc, ns], rhs=hTh[:, kc, c0:c0 + N],
                                                start=(kc == 0), stop=(kc == 7)), R=[B_ws[wi], B_hTh[ti]], W=[BK[2 + r_]])
                                        for h_ in range(8):
                                            P.op("pe", lambda e, h_=h_, wi=wi, ns=ns, r_=r_, oc0=oc0, N=N: e.matmul(
                                                bank(4 + r_)[:, 0:N], lhsT=wpa[wi][:, h_, ns], rhs=oT[:, h_, oc0:oc0 + N],
                                                start=(h_ == 0), stop=(h_ == 7)), R=[B_ws[wi], B_o], W=[BK[4 + r_]])
                                        for cc in range(4):
                                            P.op("pe", lambda e, cc=cc, wi=wi, ns=ns, r_=r_, c0=c0, N=N: e.matmul(
                                                bank(6 + r_)[:, 0:N], lhsT=wpb[wi][:, cc, ns], rhs=sT[:, cc, c0:c0 + N],
                                                start=(cc == 0), stop=(cc == 3)), R=[B_ws[wi], B_sT[ti]], W=[BK[6 + r_]])
                                        P.op("act", lambda e, r_=r_, n=n, N=N: e.activation(
                                            out=tA[r_][:, 0:N], in_=bank(r_)[:, 0:N], func=AF.Sigmoid,
                                            bias=bfm[:, 16 + n:17 + n]), R=[BK[r_], B_small], W=[B_tA[r_]])
                                        P.op("act", lambda e, r_=r_, n=n, N=N: e.activation(
                                            out=tB[r_][:, 0:N], in_=bank(2 + r_)[:, 0:N], func=AF.Sigmoid,
                                            bias=bfm[:, 24 + n:25 + n]), R=[BK[2 + r_], B_small], W=[B_tB[r_]])
                                        P.op("dve", lambda e, r_=r_, N=N: e.tensor_tensor(
                                            out=t1[:, 0:N], in0=tA[r_][:, 0:N], in1=bank(4 + r_)[:, 0:N], op=ALU.mult),
                                            R=[B_tA[r_], BK[4 + r_]], W=[B_t1])
                                        P.op("dve", lambda e, r_=r_, n=n, N=N: e.scalar_tensor_tensor(
                                            out=t2[:, 0:N], in0=bank(6 + r_)[:, 0:N], scalar=cvp[:, 136 + n:137 + n],
                                            in1=tB[r_][:, 0:N], op0=ALU.add, op1=ALU.mult),
                                            R=[B_tB[r_], BK[6 + r_], B_small], W=[B_t2])
                                        P.op("dve", lambda e, n=n, c0=c0, N=N: e.tensor_tensor(
                                            out=mT[:, n, c0:c0 + N], in0=t1[:, 0:N], in1=t2[:, 0:N], op=ALU.add),
                                            R=[B_t1, B_t2], W=[B_mT[ti]])
                            P.barrier()
                        with ExitStack() as s4:
                            wo = sb("wo", [128, 8, 1024], BF16, s4)
                            B_wo = Buf()
                            for hh in range(2):
                                P.dma("pool", lambda e, hh=hh: e.dma_start(out=wo[:, :, 512 * hh:512 * (hh + 1)],
                                                                           in_=w_o_v[:, :, 512 * hh:512 * (hh + 1)]), W=[B_wo])
                            for ti, (c0, N, oc0) in enumerate(tiles):
                                for n in range(8):
                                    r_ = nxt("u", 4)
                                    for kc in range(8):
                                        P.op("pe", lambda e, kc=kc, n=n, r_=r_, c0=c0, N=N: e.matmul(
                                            bank(r_)[:, 0:N], lhsT=wo[:, kc, n * 128:(n + 1) * 128], rhs=mT[:, kc, c0:c0 + N],
                                            start=(kc == 0), stop=(kc == 7)), R=[B_wo, B_mT[ti]], W=[BK[r_]])
                                    per_sample(
                                        N,
                                        lambda n=n, r_=r_, c0=c0, ti=ti: P.op("dve", lambda e: e.scalar_tensor_tensor(
                                            out=xT[:, n, c0:c0 + 512], in0=bank(r_)[:, 0:512], scalar=g1(n),
                                            in1=xT[:, n, c0:c0 + 512], op0=ALU.mult, op1=ALU.add),
                                            R=[BK[r_], B_mod, B_xT[ti]], W=[B_xT[ti]]),
                                        lambda s_, o_, n=n, r_=r_, c0=c0, ti=ti: P.op("dve", lambda e: e.scalar_tensor_tensor(
                                            out=xT[:, n, c0 + o_:c0 + o_ + 32], in0=bank(r_)[:, o_:o_ + 32], scalar=g1(n, 1 + s_),
                                            in1=xT[:, n, c0 + o_:c0 + o_ + 32], op0=ALU.mult, op1=ALU.add),
                                            R=[BK[r_], B_mod, B_xT[ti]], W=[B_xT[ti]]))
                            P.barrier()

                    for ti, (c0, N, oc0) in enumerate(tiles):
                        col_stats(ti, lambda kc, c0=c0, N=N: xT[:, kc, c0:c0 + N], 8, 1.0 / D)
                        for kc in range(8):
                            a = nxt("t")
                            P.op("dve", lambda e, kc=kc, a=a, c0=c0, N=N: e.tensor_tensor(
                                out=tA[a][:, 0:N], in0=xT[:, kc, c0:c0 + N], in1=rsT[:, 0:N], op=ALU.mult),
                                R=[B_xT[ti], B_rsT], W=[B_tA[a]])
                            per_sample(
                                N,
                                lambda kc=kc, a=a, c0=c0, ti=ti: P.op("act", lambda e: e.activation(
                                    out=hTh[:, kc, c0:c0 + 512], in_=tA[a][:, 0:512], func=AF.Identity, scale=a2T[:, kc, 0:1],
                                    bias=sh2(kc)), R=[B_tA[a], B_mod], W=[B_hTh[ti]]),
                                lambda s_, o_, kc=kc, a=a, c0=c0, ti=ti: P.op("dve", lambda e: e.tensor_scalar(
                                    out=hTh[:, kc, c0 + o_:c0 + o_ + 32], in0=tA[a][:, o_:o_ + 32],
                                    scalar1=a2T[:, kc, 1 + s_:2 + s_], scalar2=sh2(kc, 1 + s_), op0=ALU.mult, op1=ALU.add),
                                    R=[B_tA[a], B_mod], W=[B_hTh[ti]]))
                    P.barrier()

                    with ExitStack() as s5:
                        wG = [sb("wG%d" % i, [128, 8, 512], BF16, s5) for i in range(2)]
                        wU = [sb("wU%d" % i, [128, 8, 512], BF16, s5) for i in range(2)]
                        wD = [sb("wD%d" % i, [128, 4, 1024], BF16, s5) for i in range(2)]
                        aT = [sb("aT%d" % i, [128, 4, 512], BF16, s5) for i in range(2)]
                        B_wf = [Buf(), Buf()]
                        B_aT = [Buf(), Buf()]
                        for gi, (h0, ng) in enumerate(FFN_GROUPS):
                            wi = gi % 2
                            P.dma("pool", lambda e, wi=wi, h0=h0, ng=ng: e.dma_start(
                                out=wG[wi][:, :, 0:128 * ng], in_=w_f1_v[:, :, 128 * h0:128 * (h0 + ng)]), W=[B_wf[wi]])
                            P.dma("pool", lambda e, wi=wi, h0=h0, ng=ng: e.dma_start(
                                out=wU[wi][:, :, 0:128 * ng], in_=w_f1_v[:, :, FH + 128 * h0:FH + 128 * (h0 + ng)]),
                                W=[B_wf[wi]])
                            P.dma("pool", lambda e, wi=wi, h0=h0, ng=ng: e.dma_start(
                                out=wD[wi][:, 0:ng, :], in_=w_f2_v[:, h0:h0 + ng, :]), W=[B_wf[wi]])
                            for ti, (c0, N, oc0) in enumerate(tiles):
                                ai = nxt("a")
                                for i in range(ng):
                                    r_ = nxt("b")
                                    for kc in range(8):
                                        P.op("pe", lambda e, kc=kc, i=i, wi=wi, r_=r_, c0=c0, N=N: e.matmul(
                                            bank(r_)[:, 0:N], lhsT=wG[wi][:, kc, 128 * i:128 * (i + 1)],
                                            rhs=hTh[:, kc, c0:c0 + N], start=(kc == 0), stop=(kc == 7)),
                                            R=[B_wf[wi], B_hTh[ti]], W=[BK[r_]])
                                    for kc in range(8):
                                        P.op("pe", lambda e, kc=kc, i=i, wi=wi, r_=r_, c0=c0, N=N: e.matmul(
                                            bank(2 + r_)[:, 0:N], lhsT=wU[wi][:, kc, 128 * i:128 * (i + 1)],
                                            rhs=hTh[:, kc, c0:c0 + N], start=(kc == 0), stop=(kc == 7)),
                                            R=[B_wf[wi], B_hTh[ti]], W=[BK[2 + r_]])
                                    P.op("act", lambda e, r_=r_, N=N: e.activation(out=tA[r_][:, 0:N], in_=bank(r_)[:, 0:N],
                                                                                   func=AF.Silu), R=[BK[r_]], W=[B_tA[r_]])
                                    P.op("dve", lambda e, r_=r_, i=i, ai=ai, N=N: e.tensor_tensor(
                                        out=aT[ai][:, i, 0:N], in0=tA[r_][:, 0:N], in1=bank(2 + r_)[:, 0:N], op=ALU.mult),
                                        R=[B_tA[r_], BK[2 + r_]], W=[B_aT[ai]])
                                for n in range(8):
                                    r4 = 4 + nxt("c", 4)
                                    for i in range(ng):
                                        P.op("pe", lambda e, i=i, n=n, wi=wi, ai=ai, r4=r4, N=N, ng=ng: e.matmul(
                                            bank(r4)[:, 0:N], lhsT=wD[wi][:, i, n * 128:(n + 1) * 128], rhs=aT[ai][:, i, 0:N],
                                            start=(i == 0), stop=(i == ng - 1)), R=[B_wf[wi], B_aT[ai]], W=[BK[r4]])
                                    per_sample(
                                        N,
                                        lambda n=n, r4=r4, c0=c0, ti=ti: P.op("dve", lambda e: e.scalar_tensor_tensor(
                                            out=xT[:, n, c0:c0 + 512], in0=bank(r4)[:, 0:512], scalar=g2(n),
                                            in1=xT[:, n, c0:c0 + 512], op0=ALU.mult, op1=ALU.add),
                                            R=[BK[r4], B_mod, B_xT[ti]], W=[B_xT[ti]]),
                                        lambda s_, o_, n=n, r4=r4, c0=c0, ti=ti: P.op("dve", lambda e: e.scalar_tensor_tensor(
                                            out=xT[:, n, c0 + o_:c0 + o_ + 32], in0=bank(r4)[:, o_:o_ + 32], scalar=g2(n, 1 + s_),
                                            in1=xT[:, n, c0 + o_:c0 + o_ + 32], op0=ALU.mult, op1=ALU.add),
                                            R=[BK[r4], B_mod, B_xT[ti]], W=[B_xT[ti]]))
                        P.barrier()

                    with ExitStack() as s6:
                        yst = [sb("yst%d" % i, [128, D], F32, s6) for i in range(2)]
                        B_yst = [Buf(), Buf()]
                        for ti, (c0, N, oc0) in enumerate(tiles):
                            col_stats(ti, lambda kc, c0=c0, N=N: xT[:, kc, c0:c0 + N], 8, 1.0 / D)
                            for kc in range(8):
                                P.op("dve", lambda e, kc=kc, c0=c0, N=N: e.scalar_tensor_tensor(
                                    out=xT[:, kc, c0:c0 + N], in0=xT[:, kc, c0:c0 + N], scalar=gn[:, 16 + kc:17 + kc],
                                    in1=rsT[:, 0:N], op0=ALU.mult, op1=ALU.mult), R=[B_xT[ti], B_rsT, B_small], W=[B_xT[ti]])
                            for bq in range(N // 128):
                                col = c0 + 128 * bq
                                yi = nxt("d")
                                for kc in range(8):
                                    P.op("pe", lambda e, kc=kc, col=col, yi=yi: e.transpose(
                                        psum[:, 1024 * yi + 128 * kc + 2048:1024 * yi + 128 * (kc + 1) + 2048],
                                        xT[:, kc, col:col + 128], ident), R=[B_xT[ti], B_cst], W=[BK[4 + 2 * yi]])
                                P.op("act", lambda e, yi=yi: e.activation(out=yst[yi][:], in_=psum[:, 2048 + 1024 * yi:3072 + 1024 * yi],
                                                                          func=AF.Copy), R=[BK[4 + 2 * yi]], W=[B_yst[yi]])
                                ydst = y_s if N == 128 else y_own[8 * hf + col // 128]
                                P.dma("act", lambda e, yi=yi, ydst=ydst: e.dma_start(out=ydst, in_=yst[yi][:]), R=[B_yst[yi]])
                        P.barrier()

        if STAGE < 3 or DBG_OT:
            P.dma("sp", lambda e: e.dma_start(out=dbg_o, in_=oT[:]), R=B_oT)
        P.finish()
    return nc


_CACHE = {}


def _consts():
    c = np.zeros((128, 5, 128), np.float32)
    c[:, 0, :] = np.eye(128, dtype=np.float32)
    c[:, 1, :] = 1.0
    s = np.arange(128)
    c[:, 2, :] = (s[:, None] <= s[None, :]).astype(np.float32)
    c[:, 3, :] = (s[:, None] > s[None, :]).astype(np.float32)
    c[:, 4, :] = (s[None, :] - s[:, None]).astype(np.float32)
    return c


def kernel(x_prompt, x_sample, c_prompt, c_sample, cache_k, cache_v, cache_logf, state_conv,
           page_table, rms1_g, rms2_g, w_ada, b_ada, w_in, b_in, dw_w, dw_b, ln_g, ln_b,
           w_pa, w_pb, b_pb, w_o, w_ffn_in, w_ffn_out, final_g):
    f32 = np.float32
    A = lambda a: np.ascontiguousarray(np.asarray(a))
    x_prompt, x_sample = A(x_prompt), A(x_sample)
    if "nc" not in _CACHE:
        _CACHE["nc"] = build_program()
    nc = _CACHE["nc"]

    def fm(v, nchunk):
        return A(np.asarray(v, f32).reshape(nchunk, 128).T)

    b_in0 = np.asarray(b_in, f32)[0]
    b_fm = np.concatenate([fm(b_in0[Q_OFF:Q_OFF + 512], 4), fm(b_in0[K_OFF:K_OFF + 512], 4),
                           fm(b_in0[GLU_OFF:GLU_OFF + 1024], 8), fm(b_in0[GA_OFF:GA_OFF + 1024], 8),
                           fm(b_in0[GB_OFF:GB_OFF + 1024], 8)], axis=1)
    b_rows = np.concatenate([b_in0[K_OFF:K_OFF + 512], b_in0[V_OFF:V_OFF + 512], b_in0[F_OFF:F_OFF + 8]])[None, :]
    gains = np.concatenate([fm(np.asarray(rms1_g)[0], 8), fm(np.asarray(rms2_g)[0], 8), fm(np.asarray(final_g), 8)], axis=1)
    dwT = np.asarray(dw_w, f32)[0].T.reshape(4, 128, CK).transpose(1, 0, 2).reshape(128, 4 * CK)
    convpar = np.concatenate([dwT, fm(np.asarray(dw_b)[0], 4), fm(np.asarray(ln_g)[0], 4), fm(np.asarray(ln_b)[0], 4),
                              fm(np.asarray(b_pb)[0], 8)], axis=1)
    b_adaT = fm(np.asarray(b_ada)[0], 48)
    ck = A(cache_k).reshape(NPHYS * 16, 8 * 512)
    cv = A(cache_v).reshape(NPHYS * 16, 8 * 512)
    cl = A(cache_logf).reshape(NPHYS * 16, 64)
    if STAGE < 4:
        ck, cv, cl = A(ck[:16]), A(cv[:16]), A(cl[:16])
    shared = dict(w_ada=A(np.asarray(w_ada)[0]), b_adaT=A(b_adaT), w_in=A(np.asarray(w_in)[0]), b_fm=A(b_fm),
                  b_rows=A(b_rows), gains=A(gains), convpar=A(convpar), w_pa=A(np.asarray(w_pa)[0]),
                  w_pb=A(np.asarray(w_pb)[0]), w_o=A(np.asarray(w_o)[0]), w_f1=A(np.asarray(w_ffn_in)[0]),
                  w_f2=A(np.asarray(w_ffn_out)[0]), cache_k=ck, cache_v=cv, cache_l=cl, cst=_consts(),
                  shi=A((np.arange(128) % 16).astype(f32)[:, None]),
                  bmask=A((np.arange(32)[:, None] // 4 == np.arange(512)[None, :] // 64).astype(f32)))
    pt = np.asarray(page_table)
    sc = np.asarray(state_conv, f32)[0]
    in_maps = []
    for c in range(8):
        b, j = c // 4, c % 4
        ob = own_blocks(j)
        xo = np.zeros((NOWN, 160, D), f32)
        hv = np.zeros((128, NOWN), f32)
        for q, blk in enumerate(ob):
            lo = blk * 128 - 32
            if lo >= 0:
                xo[q] = x_prompt[b, lo:lo + 160]
                hv[:, q] = 1.0
            else:
                xo[q, 32:] = x_prompt[b, 0:128]
        xs = np.zeros((128, D), f32)
        cvec = np.zeros((5, D), f32)
        cvec[0] = np.asarray(c_prompt)[b]
        ptr = np.zeros((128, 32), np.int32)
        stT = np.zeros((128, 4, 4, 30), f32)
        strows = np.zeros((4, 26, CW), f32)
        for s in range(4):
            gs = 4 * c + s
            xs[32 * s:32 * s + 4] = x_sample[gs]
            cvec[1 + s] = np.asarray(c_sample)[gs]
            for g in range(8):
                ptr[:, s * 8 + g] = pt[gs, 8 * g + np.arange(128) // 16]
            stT[:, :, s, :] = sc[gs].T.reshape(4, 128, 30).transpose(1, 0, 2)
            strows[s] = sc[gs, 4:30]
        cT = A(cvec.reshape(5, 8, 128).transpose(2, 1, 0))
        kbo = np.tile((128.0 * (np.arange(4) - j)).astype(f32)[None, :], (128, 1))
        m = dict(shared)
        m.update(xseq=A(x_prompt[b]), xown=xo, xs=xs, cT=cT, pt_rep=ptr, stT=stT, st_rows=strows,
                 kboff=A(kbo), hval=hv)
        in_maps.append(m)

    ncores = _CACHE.get("ncores", 8)
    res = run_bass_kernel_spmd(nc, in_maps[:ncores], core_ids=list(range(ncores)))
    R = list(res.results) + [res.results[0]] * (8 - ncores)
    _CACHE["last"] = R
    y_prompt = np.zeros((2, SEQ, D), f32)
    k_p = np.zeros((2, SEQ, 512), f32)
    v_p = np.zeros((2, SEQ, 512), f32)
    l_p = np.zeros((2, SEQ, 8), f32)
    conv_p = np.zeros((1, 2, 30, CW), f32)
    y_sample = np.zeros((32, 4, D), f32)
    k_s = np.zeros((32, 4, 512), f32)
    v_s = np.zeros((32, 4, 512), f32)
    l_s = np.zeros((32, 4, 8), f32)
    conv_s = np.zeros((1, 32, 30, CW), f32)
    for c in range(8):
        b, j = c // 4, c % 4
        r = R[c]
        for q, blk in enumerate(own_blocks(j)):
            sl = slice(blk * 128, blk * 128 + 128)
            y_prompt[b, sl] = r["y_own"][q]
            k_p[b, sl] = r["k_own"][q]
            v_p[b, sl] = r["v_own"][q]
            l_p[b, sl] = r["lf_own"][q]
        if j == 3:
            conv_p[0, b] = r["convp"][2:32]
        for s in range(4):
            gs = 4 * c + s
            y_sample[gs] = r["y_s"][32 * s:32 * s + 4]
            k_s[gs] = r["ks"][32 * s:32 * s + 4]
            v_s[gs] = r["vs"][32 * s:32 * s + 4]
            l_s[gs] = r["lfs"][32 * s:32 * s + 4]
            conv_s[0, gs] = r["convs"][s]
    return (y_prompt, y_sample,
            k_p.reshape(1, 2, NB, 128, H, DH), v_p.reshape(1, 2, NB, 128, H, DH), l_p.reshape(1, 2, NB, 128, H),
            conv_p, k_s.reshape(1, 32, 4, H, DH), v_s.reshape(1, 32, 4, H, DH), l_s.reshape(1, 32, 4, H), conv_s)
```
